# Optimizing a Trainium2 kernel written in Bass

```python
import math
import jax
import jax.numpy as jnp
from jax import lax
import numpy as np

D_MODEL = 1024
BATCH = 32
SEQ = 2048
DEPTH = 2

HY_WIDTH = 512
HY_SHORT = 3
HY_BANDS = 16
HY_EMB = 1 + 2 * HY_BANDS
HY_FILT = 64
HY_TARGET = 1e-2
HY_FAST_PCT = 0.3
HY_SLOW_PCT = 1.5
S5_WIDTH = 512
S5_GROUP = 16
S5_GROUPS = S5_WIDTH // S5_GROUP
S5_STATE = 64
SSD_WIDTH = 1024
SSD_HEADDIM = 64
SSD_HEADS = SSD_WIDTH // SSD_HEADDIM
SSD_GROUPS = 2
SSD_STATE = 128
SSD_CONV = 5
SSD_CHUNK = 128
FFN_HIDDEN = 2816
FFN_CONV = 3
N_BRANCH = 3
EPS = 1e-6

HY_IN = 3 * HY_WIDTH
SSD_BC = SSD_GROUPS * SSD_STATE
SSD_XBC = SSD_WIDTH + 2 * SSD_BC
GATE_COLS = N_BRANCH * D_MODEL
IN_COLS = HY_IN + S5_WIDTH + SSD_WIDTH + SSD_XBC + 2 * SSD_HEADS + GATE_COLS
IN_SPLITS = (HY_IN,
             HY_IN + S5_WIDTH,
             HY_IN + S5_WIDTH + SSD_WIDTH,
             HY_IN + S5_WIDTH + SSD_WIDTH + SSD_XBC,
             HY_IN + S5_WIDTH + SSD_WIDTH + SSD_XBC + 2 * SSD_HEADS)

kernel_name = 'hybrid_hyena_s5_ssd_encoder'


def rmsnorm(x, w):
    x32 = x.astype(jnp.float32)
    y = x32 * lax.rsqrt(jnp.mean(x32 * x32, axis=-1, keepdims=True) + EPS)
    return y.astype(x.dtype) * w


def dwconv_centred(x, w, b):
    y = lax.conv_general_dilated(x, w[:, None, :], window_strides=(1,), padding='SAME',
                                 dimension_numbers=('NWC', 'WIO', 'NWC'),
                                 feature_group_count=x.shape[-1])
    return y + b


def hyena_filters(L, w1, b1, freq, w2, b2, w3, decay):
    f32 = jnp.float32
    pos = jnp.arange(L, dtype=f32)
    t = (pos / max(L - 1, 1))[:, None]
    bands = jnp.linspace(1e-4, HY_BANDS - 1, HY_BANDS, dtype=f32)
    ang = (2.0 * math.pi / L) * pos[:, None] * bands[None, :]
    z = jnp.concatenate([t, jnp.cos(ang), -jnp.sin(ang)], axis=-1)
    h = jnp.sin(freq[0] * (z @ w1 + b1))
    h = jnp.sin(freq[1] * (h @ w2 + b2))
    h = (h @ w3) * jnp.exp(-t * jnp.abs(decay))
    h = h.astype(f32).reshape(L, 2, HY_WIDTH)
    k = jnp.concatenate([h[:, 0], jnp.zeros((1, HY_WIDTH), f32), h[:0:-1, 1]], axis=0)
    return k / jnp.sum(jnp.abs(k), axis=0, keepdims=True)


def hyena_mixer(u, short_w, short_b, w1, b1, freq, w2, b2, w3, decay, bias):
    L = u.shape[1]
    uc = dwconv_centred(u, short_w, short_b)
    x0, x1, v = jnp.split(uc, 3, axis=-1)
    s = v * x1
    k = hyena_filters(L, w1, b1, freq, w2, b2, w3, decay)
    S = jnp.fft.rfft(s.astype(jnp.float32), n=2 * L, axis=1)
    K = jnp.fft.rfft(k, n=2 * L, axis=0)
    y = jnp.fft.irfft(S * K[None], n=2 * L, axis=1)[:, :L].astype(u.dtype)
    return x0 * (y + bias * s)


def _cmul(ar, ai, br, bi):
    return ar * br - ai * bi, ar * bi + ai * br


def _s5_combine(e1, e2):
    a1r, a1i, b1r, b1i = e1
    a2r, a2i, b2r, b2i = e2
    ar, ai = _cmul(a2r, a2i, a1r, a1i)
    br, bi = _cmul(a2r, a2i, b1r, b1i)
    return ar, ai, br + b2r, bi + b2i


def s5_direction(ug, a_re, a_im, log_dt, b_re, b_im, c_re, c_im, reverse):
    f32 = jnp.float32
    a_re = a_re.astype(f32)
    a_im = a_im.astype(f32)
    dt = jnp.exp(log_dt.astype(f32))[:, None]
    mag = jnp.exp(a_re * dt)
    abar_r, abar_i = mag * jnp.cos(a_im * dt), mag * jnp.sin(a_im * dt)
    den = a_re * a_re + a_im * a_im
    nr = abar_r - 1.0
    f_r = (nr * a_re + abar_i * a_im) / den
    f_i = (abar_i * a_re - nr * a_im) / den
    bu_r = jnp.einsum('blgc,gnc->lbgn', ug, b_re.astype(f32))
    bu_i = jnp.einsum('blgc,gnc->lbgn', ug, b_im.astype(f32))
    b_r, b_i = _cmul(f_r, f_i, bu_r, bu_i)
    shape = (ug.shape[1], 1) + abar_r.shape
    _, _, h_r, h_i = lax.associative_scan(
        _s5_combine,
        (jnp.broadcast_to(abar_r, shape), jnp.broadcast_to(abar_i, shape), b_r, b_i),
        reverse=reverse, axis=0)
    return (jnp.einsum('lbgn,gcn->blgc', h_r, c_re.astype(f32))
            - jnp.einsum('lbgn,gcn->blgc', h_i, c_im.astype(f32)))


def s5_mixer(u, a_re, a_im, log_dt, b_re, b_im, c_re, c_im, d, w_glu, b_glu):
    Bsz, L, _ = u.shape
    ug = u.astype(jnp.float32).reshape(Bsz, L, S5_GROUPS, S5_GROUP)
    y = (s5_direction(ug, a_re[0], a_im[0], log_dt[0], b_re[0], b_im[0], c_re[0], c_im[0], False)
         + s5_direction(ug, a_re[1], a_im[1], log_dt[1], b_re[1], b_im[1], c_re[1], c_im[1], True))
    y = y.reshape(Bsz, L, S5_WIDTH).astype(u.dtype) + d * u
    g = jax.nn.gelu(y)
    return g * jax.nn.sigmoid(g @ w_glu + b_glu)


def _segsum_exp(a_cs):
    Q = a_cs.shape[-1]
    diff = a_cs[..., :, None] - a_cs[..., None, :]
    mask = jnp.tril(jnp.ones((Q, Q), dtype=bool))
    return jnp.where(mask, jnp.exp(jnp.where(mask, diff, 0.0)), 0.0)


def ssd_scan(x, dt, A, Bm, Cm):
    Bsz, L, H, P = x.shape
    G, N, Q = SSD_GROUPS, SSD_STATE, SSD_CHUNK
    Hg, nc = H // G, L // Q
    xd = (x * dt[..., None]).reshape(Bsz, nc, Q, G, Hg, P)
    a = jnp.moveaxis((dt * A).reshape(Bsz, nc, Q, G, Hg), 2, -1)
    a_cs = jnp.cumsum(a, axis=-1)
    Bc = Bm.reshape(Bsz, nc, Q, G, N)
    Cc = Cm.reshape(Bsz, nc, Q, G, N)
    scores = jnp.einsum('bclgn,bcsgn->bcgls', Cc, Bc)
    M = scores[:, :, :, None] * _segsum_exp(a_cs)
    y_diag = jnp.einsum('bcghls,bcsghp->bclghp', M, xd)
    decay_states = jnp.moveaxis(jnp.exp(a_cs[..., -1:] - a_cs), -1, 2)
    states = jnp.einsum('bcsgn,bcsghp->bcghpn', Bc, xd * decay_states[..., None])
    chunk_decay = jnp.exp(a_cs[..., -1])

    def step(carry, inp):
        st, dec = inp
        return carry * dec[..., None, None] + st, carry

    init = jnp.zeros((Bsz, G, Hg, P, N), x.dtype)
    _, prev = lax.scan(step, init, (jnp.moveaxis(states, 1, 0), jnp.moveaxis(chunk_decay, 1, 0)))
    prev = jnp.moveaxis(prev, 0, 1)
    in_decay = jnp.moveaxis(jnp.exp(a_cs), -1, 2)[..., None]
    y_off = jnp.einsum('bclgn,bcghpn->bclghp', Cc, prev) * in_decay
    return (y_diag + y_off).reshape(Bsz, L, H, P)


def ssd_mixer(z, xbc, dt_raw, conv_w, conv_b, a_log, dt_bias, d, norm_w):
    f32 = jnp.float32
    Bsz, L, _ = z.shape
    xbc = jax.nn.silu(dwconv_centred(xbc, conv_w, conv_b))
    xs, Bm, Cm = jnp.split(xbc, (SSD_WIDTH, SSD_WIDTH + SSD_BC), axis=-1)
    x32 = xs.astype(f32).reshape(Bsz, L, SSD_HEADS, SSD_HEADDIM)
    B32 = Bm.astype(f32).reshape(Bsz, L, SSD_GROUPS, SSD_STATE)
    C32 = Cm.astype(f32).reshape(Bsz, L, SSD_GROUPS, SSD_STATE)
    dt = jax.nn.softplus(dt_raw.astype(f32).reshape(Bsz, L, 2, SSD_HEADS) + dt_bias.astype(f32))
    A = -jnp.exp(a_log.astype(f32))
    flip = lambda t: jnp.flip(t, axis=1)
    y_f = ssd_scan(x32, dt[:, :, 0], A[0], B32, C32)
    y_b = flip(ssd_scan(flip(x32), flip(dt[:, :, 1]), A[1], flip(B32), flip(C32)))
    y = y_f + y_b + d.astype(f32)[:, None] * x32
    y = y.reshape(Bsz, L, SSD_WIDTH) * jax.nn.silu(z.astype(f32))
    return rmsnorm(y, norm_w).astype(z.dtype)


def conv_ffn(h, w_up, conv_w, conv_b, w_down):
    u = dwconv_centred(h @ w_up, conv_w, conv_b)
    gate, val = jnp.split(u, 2, axis=-1)
    return (jax.nn.silu(gate) * val) @ w_down


def setup_inputs(seed: int = 0) -> dict:
    key = jax.random.key(seed)
    ks = iter(jax.random.split(key, 64))
    f32 = jnp.float32

    def nrm(shape, scale):
        return scale * jax.random.normal(next(ks), shape, f32)

    def gain(shape):
        return 1.0 + nrm(shape, 0.01)

    Ld = DEPTH
    hy_lo = -math.log(HY_TARGET) / HY_SLOW_PCT
    hy_hi = -math.log(HY_TARGET) / HY_FAST_PCT
    hy_decay = jnp.linspace(hy_lo, hy_hi, 2 * HY_WIDTH, dtype=f32) * (1.0 + nrm((Ld, 2 * HY_WIDTH), 0.02))
    s5_a_re = -0.5 + nrm((Ld, 2, S5_GROUPS, S5_STATE), 0.01)
    s5_a_im = math.pi * jnp.arange(S5_STATE, dtype=f32) + nrm((Ld, 2, S5_GROUPS, S5_STATE), 0.01)
    s5_log_dt = jax.random.uniform(next(ks), (Ld, 2, S5_GROUPS), f32, math.log(1e-3), math.log(1e-1))
    ssd_dt0 = jnp.exp(jax.random.uniform(next(ks), (Ld, 2, SSD_HEADS), f32, math.log(1e-3), math.log(1e-1)))
    ssd_dt_bias = ssd_dt0 + jnp.log(-jnp.expm1(-ssd_dt0))
    ssd_a_log = jnp.log(jax.random.uniform(next(ks), (Ld, 2, SSD_HEADS), f32, 1.0, 16.0))
    return {
        'x': nrm((BATCH, SEQ, D_MODEL), 1.0),
        'norm_mix': gain((Ld, D_MODEL)),
        'w_in': nrm((Ld, D_MODEL, IN_COLS), D_MODEL ** -0.5),
        'hy_short_w': nrm((Ld, HY_SHORT, HY_IN), HY_SHORT ** -0.5),
        'hy_short_b': nrm((Ld, HY_IN), 0.01),
        'hy_w1': nrm((Ld, HY_EMB, HY_FILT), HY_EMB ** -0.5),
        'hy_b1': nrm((Ld, HY_FILT), 0.1),
        'hy_freq': gain((Ld, 2, HY_FILT)),
        'hy_w2': nrm((Ld, HY_FILT, HY_FILT), HY_FILT ** -0.5),
        'hy_b2': nrm((Ld, HY_FILT), 0.1),
        'hy_w3': nrm((Ld, HY_FILT, 2 * HY_WIDTH), HY_FILT ** -0.5),
        'hy_decay': hy_decay,
        'hy_bias': nrm((Ld, HY_WIDTH), 1.0),
        's5_a_re': s5_a_re,
        's5_a_im': s5_a_im,
        's5_log_dt': s5_log_dt,
        's5_b_re': nrm((Ld, 2, S5_GROUPS, S5_STATE, S5_GROUP), (2 * S5_GROUP) ** -0.5),
        's5_b_im': nrm((Ld, 2, S5_GROUPS, S5_STATE, S5_GROUP), (2 * S5_GROUP) ** -0.5),
        's5_c_re': nrm((Ld, 2, S5_GROUPS, S5_GROUP, S5_STATE), S5_STATE ** -0.5),
        's5_c_im': nrm((Ld, 2, S5_GROUPS, S5_GROUP, S5_STATE), S5_STATE ** -0.5),
        's5_d': gain((Ld, S5_WIDTH)),
        's5_w_glu': nrm((Ld, S5_WIDTH, S5_WIDTH), S5_WIDTH ** -0.5),
        's5_b_glu': nrm((Ld, S5_WIDTH), 0.01),
        'ssd_conv_w': nrm((Ld, SSD_CONV, SSD_XBC), SSD_CONV ** -0.5),
        'ssd_conv_b': nrm((Ld, SSD_XBC), 0.01),
        'ssd_a_log': ssd_a_log,
        'ssd_dt_bias': ssd_dt_bias,
        'ssd_d': gain((Ld, SSD_HEADS)),
        'ssd_norm': gain((Ld, SSD_WIDTH)),
        'p_a': nrm((Ld, HY_WIDTH, D_MODEL), HY_WIDTH ** -0.5),
        'p_b': nrm((Ld, S5_WIDTH, D_MODEL), S5_WIDTH ** -0.5),
        'p_c': nrm((Ld, SSD_WIDTH, D_MODEL), SSD_WIDTH ** -0.5),
        'w_out': nrm((Ld, D_MODEL, D_MODEL), D_MODEL ** -0.5),
        'norm_ffn': gain((Ld, D_MODEL)),
        'ffn_up': nrm((Ld, D_MODEL, 2 * FFN_HIDDEN), D_MODEL ** -0.5),
        'ffn_conv_w': nrm((Ld, FFN_CONV, 2 * FFN_HIDDEN), FFN_CONV ** -0.5),
        'ffn_conv_b': nrm((Ld, 2 * FFN_HIDDEN), 0.01),
        'ffn_down': nrm((Ld, FFN_HIDDEN, D_MODEL), FFN_HIDDEN ** -0.5),
        'norm_final': gain((D_MODEL,)),
    }


def reference(x, norm_mix, w_in, hy_short_w, hy_short_b, hy_w1, hy_b1, hy_freq, hy_w2, hy_b2, hy_w3,
              hy_decay, hy_bias, s5_a_re, s5_a_im, s5_log_dt, s5_b_re, s5_b_im, s5_c_re, s5_c_im, s5_d,
              s5_w_glu, s5_b_glu, ssd_conv_w, ssd_conv_b, ssd_a_log, ssd_dt_bias, ssd_d, ssd_norm,
              p_a, p_b, p_c, w_out, norm_ffn, ffn_up, ffn_conv_w, ffn_conv_b, ffn_down, norm_final):
    for l in range(DEPTH):
        h = rmsnorm(x, norm_mix[l])
        proj = h @ w_in[l]
        hy_u, s5_u, ssd_z, ssd_xbc, ssd_dt, gate_logits = jnp.split(proj, IN_SPLITS, axis=-1)
        y_a = hyena_mixer(hy_u, hy_short_w[l], hy_short_b[l], hy_w1[l], hy_b1[l], hy_freq[l],
                          hy_w2[l], hy_b2[l], hy_w3[l], hy_decay[l], hy_bias[l])
        y_b = s5_mixer(s5_u, s5_a_re[l], s5_a_im[l], s5_log_dt[l], s5_b_re[l], s5_b_im[l],
                       s5_c_re[l], s5_c_im[l], s5_d[l], s5_w_glu[l], s5_b_glu[l])
        y_c = ssd_mixer(ssd_z, ssd_xbc, ssd_dt, ssd_conv_w[l], ssd_conv_b[l], ssd_a_log[l],
                        ssd_dt_bias[l], ssd_d[l], ssd_norm[l])
        g_a, g_b, g_c = jnp.split(jax.nn.sigmoid(gate_logits), N_BRANCH, axis=-1)
        merged = g_a * (y_a @ p_a[l]) + g_b * (y_b @ p_b[l]) + g_c * (y_c @ p_c[l])
        x = x + merged @ w_out[l]
        h = rmsnorm(x, norm_ffn[l])
        x = x + conv_ffn(h, ffn_up[l], ffn_conv_w[l], ffn_conv_b[l], ffn_down[l])
    return rmsnorm(x, norm_final)
```

```python
import contextlib
import math
import os
import numpy as np
import ml_dtypes
import concourse.bass as bass
import concourse.mybir as mybir
from concourse.bass_utils import run_bass_kernel_spmd

F32 = mybir.dt.float32
BF16 = mybir.dt.bfloat16
AF = mybir.ActivationFunctionType
ALU = mybir.AluOpType
AX = mybir.AxisListType

D = 1024
T = 2048
NCORE = 8
NSEQ = 4
DEPTH = 2
IN_COLS = 7712
FFN_H = 2816
EPS = 1e-6
C_HY, C_S5, C_Z, C_XBC, C_DT, C_G = 0, 1536, 2048, 3072, 4608, 4640


class Unit:
    __slots__ = ("name", "lw", "rd")

    def __init__(self, name):
        self.name = name
        self.lw = None
        self.rd = {}


class Tile(Unit):
    __slots__ = ("t", "psum")

    def __init__(self, name, t, psum=False):
        super().__init__(name)
        self.t = t
        self.psum = psum

    def __getitem__(self, idx):
        return self.t[idx]


class Scope:
    def __init__(self, kb):
        self.kb = kb
        self.es = contextlib.ExitStack()

    def sb(self, name, shape, dtype):
        self.kb.uid += 1
        nm = "%s_%d" % (name, self.kb.uid)
        t = self.es.enter_context(self.kb.nc.sbuf_tensor(nm, list(shape), dtype))
        return Tile(nm, t)

    def __enter__(self):
        return self

    def __exit__(self, *a):
        if a[0] is None:
            self.kb.barrier()
        self.es.close()
        return False


class KB:
    ENG = ("pe", "act", "dve", "pool", "sp")

    def __init__(self, n_dma_sems=32):
        self.nc = bass.Bass("TRN2", target_bir_lowering=False)
        self.es = contextlib.ExitStack()
        nc = self.nc
        self.eng = {"pe": nc.tensor, "act": nc.scalar, "dve": nc.vector,
                    "pool": nc.gpsimd, "sp": nc.sync}
        self.sem = {}
        self.cnt = {}
        for e in self.ENG:
            self.sem[e] = self.es.enter_context(nc.semaphore("sem_" + e))
            self.cnt[e] = 0
        self.pending = {e: False for e in self.ENG}
        self.dma_sems = []
        for i in range(n_dma_sems):
            key = "d%d" % i
            self.sem[key] = self.es.enter_context(nc.semaphore("dsem%d" % i))
            self.cnt[key] = 0
            self.dma_sems.append(key)
        self.dma_rr = 0
        self.waited = {e: {} for e in self.ENG}
        self.n_inst = 0
        self.n_wait = 0
        self.uid = 0

    def sb(self, name, shape, dtype):
        t = self.es.enter_context(self.nc.sbuf_tensor(name, list(shape), dtype))
        return Tile(name, t)

    def ps(self, name, shape, dtype=F32):
        t = self.es.enter_context(self.nc.psum_tensor(name, list(shape), dtype))
        return Tile(name, t, psum=True)

    def dram(self, name, shape, dtype, kind="Internal"):
        return self.nc.dram_tensor(name, list(shape), dtype, kind=kind).ap()

    def scope(self):
        return Scope(self)

    def _wait(self, e, key, val):
        if key == e and e == "pe":
            return
        w = self.waited[e]
        if w.get(key, 0) >= val:
            return
        self.eng[e].wait_ge(self.sem[key], val)
        w[key] = val
        self.n_wait += 1

    def _deps(self, e, reads, writes):
        for u in reads:
            if u.lw is not None:
                self._wait(e, *u.lw)
        for u in writes:
            if u.lw is not None:
                self._wait(e, *u.lw)
            for kk, v in u.rd.items():
                self._wait(e, kk, v)

    def _mark(self, key, val, reads, writes):
        for u in reads:
            if u.rd.get(key, 0) < val:
                u.rd[key] = val
        for u in writes:
            u.lw = (key, val)
            u.rd = {}

    def op(self, e, fn, reads=(), writes=(), inc=True):
        if e != "pe":
            px = [u for u in reads if getattr(u, "psum", False)]
            if px:
                reads = [u for u in reads if not getattr(u, "psum", False)]
                writes = list(writes) + px
        self._deps(e, reads, writes)
        inst = fn(self.eng[e])
        self.n_inst += 1
        if inc:
            self.cnt[e] += 1
            inst.then_inc(self.sem[e], 1)
            val = self.cnt[e]
            self.pending[e] = False
        else:
            val = self.cnt[e] + 1
            self.pending[e] = True
        self._mark(e, val, reads, writes)
        return inst

    def dma(self, out, in_, reads=(), writes=(), q="sp", **kw):
        key = self.dma_sems[self.dma_rr]
        self.dma_rr = (self.dma_rr + 1) % len(self.dma_sems)
        if self.cnt[key] > 0:
            self._wait(q, key, self.cnt[key])
        self._deps(q, reads, writes)
        inst = self.eng[q].dma_start(out=out, in_=in_, **kw)
        self.cnt[key] += 16
        inst.then_inc(self.sem[key], 16)
        self.n_inst += 1
        self._mark(key, self.cnt[key], reads, writes)
        return inst

    def barrier(self):
        for e in self.ENG:
            assert not self.pending[e], "pending non-inc op on " + e
        for e in self.ENG:
            for key in list(self.ENG) + self.dma_sems:
                if self.cnt[key] > 0:
                    self._wait(e, key, self.cnt[key])

    def finish(self):
        self.barrier()


class Ctx:
    pass


BIGW = {"w_in": (D, IN_COLS), "p_a": (512, D), "p_b": (512, D), "p_c": (1024, D), "w_out": (D, D),
        "ffn_up": (D, 2 * FFN_H), "ffn_down": (FFN_H, D), "s5_w_glu": (512, 512)}


class WStream:
    def __init__(self, k, sc, C, specs, kt_max, nbuf=3, name="ws"):
        self.k, self.C, self.specs, self.nbuf = k, C, specs, nbuf
        self.bufs = [sc.sb(name, [128, kt_max, 128], BF16) for _ in range(nbuf)]
        self.issued = 0

    def _issue(self, i):
        wn, l, c0, msz, KT = self.specs[i]
        b = self.bufs[i % self.nbuf]
        self.k.dma(b[:, 0:KT, 0:msz],
                   self.C.Wb[wn][l, :, c0:c0 + msz].rearrange("(n p) m -> p n m", p=128),
                   reads=[self.C.u_Wb[(wn, l)]], writes=[b])

    def get(self, i):
        while self.issued < min(i + self.nbuf - 1, len(self.specs)):
            self._issue(self.issued)
            self.issued += 1
        return self.bufs[i % self.nbuf]


def stage_convert_weights(k, C):
    with k.scope() as sc:
        fb = [sc.sb("cvf", [128, 2048], F32) for _ in range(3)]
        bb = [sc.sb("cvb", [128, 2048], BF16) for _ in range(3)]
        n = 0
        engs = ["act", "dve", "pool"]
        only = os.environ.get('CONV_ONLY', '')
        for nm, (R, Cc) in BIGW.items():
            if only and nm not in only.split('+'):
                continue
            for l in range(DEPTH if not only else 1):
                for r0 in range(0, R, 128):
                    for c0 in range(0, Cc, 2048):
                        cw_ = min(2048, Cc - c0)
                        f, b = fb[n % 3], bb[n % 3]
                        k.dma(f[:, 0:cw_], C.W[nm][l, r0:r0 + 128, c0:c0 + cw_], writes=[f])
                        e = engs[n % 3]
                        if e == "act":
                            k.op("act", lambda e_: e_.activation(out=b[:, 0:cw_], in_=f[:, 0:cw_], func=AF.Copy), reads=[f], writes=[b])
                        else:
                            k.op(e, lambda e_: e_.tensor_copy(out=b[:, 0:cw_], in_=f[:, 0:cw_]), reads=[f], writes=[b])
                        k.dma(C.Wb[nm][l, r0:r0 + 128, c0:c0 + cw_], b[:, 0:cw_], reads=[b], writes=[C.u_Wb[(nm, l)]])
                        n += 1


def units(prefix, n):
    return [Unit("%s%d" % (prefix, i)) for i in range(n)]


def inproj_tiles():
    tl = [(c0, 128) for c0 in range(0, C_DT, 128)]
    tl.append((C_DT, 32))
    tl += [(C_G + i * 128, 128) for i in range(24)]
    return tl


def load_featvecs(k, C, dst, vecs, n):
    with k.scope() as sc:
        ps = C.PS[7]
        for i, v in enumerate(vecs):
            st = sc.sb("fvst", [n, 128], F32)
            k.dma(st[:], v.rearrange("(n p) -> n p", p=128), writes=[st])
            k.op("pe", lambda e: e.transpose(out=ps[:, i * n:(i + 1) * n], in_=st[0:n, :], identity=C.identF[0:n, 0:n]),
                 reads=[st, C.identF], writes=[ps])
        nv = len(vecs)
        k.op("dve", lambda e: e.tensor_copy(out=dst[:].rearrange("p a b -> p (a b)"), in_=ps[:, 0:nv * n]),
             reads=[ps], writes=[dst])


def build_context(k, nseq, dbg):
    C = Ctx()
    C.nseq = nseq
    C.dbg = dbg
    ein = lambda n, s, dt=F32: k.dram(n, s, dt, kind="ExternalInput")
    C.x = ein("x", [nseq, T, D])
    C.out = k.dram("out", [nseq, T, D], F32, kind="ExternalOutput")
    W = {}
    W["norm_mix"] = ein("norm_mix", [DEPTH, D])
    W["w_in"] = ein("w_in", [DEPTH, D, IN_COLS])
    W["p_a"] = ein("p_a", [DEPTH, 512, D])
    W["p_b"] = ein("p_b", [DEPTH, 512, D])
    W["p_c"] = ein("p_c", [DEPTH, 1024, D])
    W["w_out"] = ein("w_out", [DEPTH, D, D])
    W["norm_ffn"] = ein("norm_ffn", [DEPTH, D])
    W["ffn_up"] = ein("ffn_up", [DEPTH, D, 2 * FFN_H])
    W["ffn_conv_w"] = ein("ffn_conv_w", [DEPTH, 3, 2 * FFN_H])
    W["ffn_conv_b"] = ein("ffn_conv_b", [DEPTH, 2 * FFN_H])
    W["ffn_down"] = ein("ffn_down", [DEPTH, FFN_H, D])
    W["norm_final"] = ein("norm_final", [D])
    for nm, shp in (("s5_a_re", [DEPTH, 2, 32, 64]), ("s5_a_im", [DEPTH, 2, 32, 64]), ("s5_log_dt", [DEPTH, 2, 32]),
                    ("s5_b_re", [DEPTH, 2, 32, 64, 16]), ("s5_b_im", [DEPTH, 2, 32, 64, 16]),
                    ("s5_c_re", [DEPTH, 2, 32, 16, 64]), ("s5_c_im", [DEPTH, 2, 32, 16, 64]),
                    ("s5_d", [DEPTH, 512]), ("s5_w_glu", [DEPTH, 512, 512]), ("s5_b_glu", [DEPTH, 512])):
        W[nm] = ein(nm, shp)
    C.W = W
    C.Wb = {}
    C.u_Wb = {}
    for nm in BIGW:
        shp = BIGW[nm]
        C.Wb[nm] = k.dram("wb_" + nm, [DEPTH] + list(shp), BF16)
        for l in range(DEPTH):
            C.u_Wb[(nm, l)] = Unit("wb_%s%d" % (nm, l))
    C.c_ident = ein("c_ident", [128, 128])
    kind = "ExternalOutput" if dbg else "Internal"
    yk = "ExternalInput" if dbg == "ytest" else kind
    nsc = 1 if dbg else nseq
    C.xT_all = [k.dram("xT" + ("%d" % q if q else ""), [D, T], F32, kind=kind) for q in range(nsc)]
    C.PT_all = [k.dram("PT" + ("%d" % q if q else ""), [IN_COLS, T], BF16, kind=kind) for q in range(nsc)]
    C.DT_all = [k.dram("DTs" + ("%d" % q if q else ""), [32, T], F32, kind=kind) for q in range(nsc)]
    C.YT_all = [k.dram("YT" + ("%d" % q if q else ""), [2048, T], BF16, kind=yk) for q in range(nsc)]
    C.u_xT_all = [units("xT%d_" % q, 8) for q in range(nsc)]
    C.u_PT_all = [{c0: Unit("PT%d_%d" % (q, c0)) for c0, _ in inproj_tiles()} for q in range(nsc)]
    C.u_YT_all = [units("YT%d_" % q, 16) for q in range(nsc)]

    def set_seq(q):
        q = min(q, nsc - 1)
        C.xT, C.PT, C.DT, C.YT = C.xT_all[q], C.PT_all[q], C.DT_all[q], C.YT_all[q]
        C.u_xT, C.u_PT, C.u_YT = C.u_xT_all[q], C.u_PT_all[q], C.u_YT_all[q]
    C.set_seq = set_seq
    set_seq(0)
    C.YS5 = k.dram("YS5", [nsc, 2, 512, T], BF16, kind="Internal")
    C.u_YS5 = [Unit("YS5_%d" % q) for q in range(nsc)]
    C.identF = k.sb("identF", [128, 128], F32)
    C.identB = k.sb("identB", [128, 128], BF16)
    C.onesB = k.sb("onesB", [128, 128], BF16)
    k.dma(C.identF[:], C.c_ident[:, :], writes=[C.identF])
    k.op("dve", lambda e: e.tensor_copy(out=C.identB[:], in_=C.identF[:]), reads=[C.identF], writes=[C.identB])
    k.op("dve", lambda e: e.memset(C.onesB[:], 1.0), writes=[C.onesB])
    C.PS = [k.ps("ps%d" % i, [128, 512], F32) for i in range(8)]
    C.HT = k.sb("HT", [128, 8, T], BF16)
    nwt = k.sb("nw_all", [128, 5, 8], F32)
    load_featvecs(k, C, nwt, [W["norm_mix"][0], W["norm_mix"][1], W["norm_ffn"][0], W["norm_ffn"][1], W["norm_final"]], 8)
    C.nw = {("norm_mix", 0): (nwt, 0), ("norm_mix", 1): (nwt, 1), ("norm_ffn", 0): (nwt, 2), ("norm_ffn", 1): (nwt, 3),
            ("norm_final", 0): (nwt, 4)}
    return C


def evac(k, i, out_ap, in_ap, reads, writes):
    if i % 2 == 0:
        k.op("act", lambda e: e.activation(out=out_ap, in_=in_ap, func=AF.Copy), reads=reads, writes=writes)
    else:
        k.op("dve", lambda e: e.tensor_copy(out=out_ap, in_=in_ap), reads=reads, writes=writes)


def stage_load_x(k, C, b):
    with k.scope() as sc:
        xin = [sc.sb("xin", [128, 4, D], F32) for _ in range(2)]
        xo = [sc.sb("xo", [128, 512], F32) for _ in range(3)]
        n = 0
        for g in range(4):
            xt = xin[g % 2]
            k.dma(xt[:], C.x[b, g * 512:(g + 1) * 512, :].rearrange("(n p) d -> p n d", p=128), writes=[xt])
            for dt in range(8):
                ps = C.PS[n % 8]
                for j in range(4):
                    k.op("pe", lambda e: e.transpose(out=ps[:, j * 128:(j + 1) * 128],
                                                     in_=xt[:, j, dt * 128:(dt + 1) * 128],
                                                     identity=C.identF[:]),
                         reads=[xt, C.identF], writes=[ps], inc=(j == 3))
                o = xo[n % 3]
                evac(k, n, o[:], ps[:], [ps], [o])
                k.dma(C.xT[dt * 128:(dt + 1) * 128, g * 512:(g + 1) * 512], o[:], reads=[o], writes=[C.u_xT[dt]])
                n += 1


def rstd_cols(k, C, sc_tiles, ct, xt):
    sq, rs = sc_tiles
    s = sq[ct % 2]
    r = rs[ct % 2]
    k.op("act", lambda e: e.activation(out=s[:], in_=xt[:], func=AF.Square), reads=[xt], writes=[s])
    ps = C.PS[ct % 8]
    for dt in range(8):
        k.op("pe", lambda e: e.matmul(ps[:], lhsT=C.onesB[:], rhs=s[:, dt, :], start=(dt == 0), stop=(dt == 7)),
             reads=[C.onesB, s], writes=[ps], inc=(dt == 7))
    k.op("act", lambda e: e.activation(out=r[:], in_=ps[:], func=AF.Sqrt, scale=1.0 / D, bias=C.epsT[:, 0:1]),
         reads=[ps, C.epsT], writes=[r])
    k.op("dve", lambda e: e.reciprocal(out=r[:], in_=r[:]), reads=[r], writes=[r])
    return r


def stage_norm(k, C, wcol):
    HT = C.HT
    with k.scope() as sc:
        xs = [sc.sb("nx", [128, 8, 512], F32) for _ in range(2)]
        sq = [sc.sb("nsq", [128, 8, 512], BF16) for _ in range(2)]
        rs = [sc.sb("nrs", [128, 512], F32) for _ in range(2)]
        for ct in range(4):
            xt = xs[ct % 2]
            k.dma(xt[:], C.xT[:, ct * 512:(ct + 1) * 512].rearrange("(n p) t -> p n t", p=128),
                  reads=C.u_xT, writes=[xt])
            r = rstd_cols(k, C, (sq, rs), ct, xt)
            for dt in range(8):
                k.op("dve", lambda e: e.scalar_tensor_tensor(out=HT[:, dt, ct * 512:(ct + 1) * 512], in0=xt[:, dt, :],
                                                             scalar=wcol[0][:, wcol[1], dt:dt + 1], op0=ALU.mult,
                                                             in1=r[:], op1=ALU.mult),
                     reads=[xt, r, wcol[0]], writes=[HT])


def stage_inproj(k, C, l):
    HT = C.HT
    with k.scope() as sc:
        tl = inproj_tiles()
        ws = WStream(k, sc, C, [("w_in", l, c0, sz, 8) for c0, sz in tl], 8)
        ot = [sc.sb("po", [128, T], BF16) for _ in range(2)]
        otf = sc.sb("pof", [32, T], F32)
        for mi, (c0, sz) in enumerate(tl):
            w = ws.get(mi)
            banks = C.PS[(mi % 2) * 4:(mi % 2) * 4 + 4]
            for ct in range(4):
                for kt in range(8):
                    k.op("pe", lambda e: e.matmul(banks[ct][0:sz, :], lhsT=w[:, kt, 0:sz],
                                                  rhs=HT[:, kt, ct * 512:(ct + 1) * 512],
                                                  start=(kt == 0), stop=(kt == 7)),
                         reads=[w, HT], writes=[banks[ct]], inc=(kt == 7))
            o = otf if c0 == C_DT else ot[mi % 2]
            for ct in range(4):
                evac(k, ct, o[0:sz, ct * 512:(ct + 1) * 512], banks[ct][0:sz, :], [banks[ct]], [o])
            if c0 == C_DT:
                k.dma(C.DT[:, :], o[0:sz, :], reads=[o], writes=[C.u_PT[c0]])
            else:
                k.dma(C.PT[c0:c0 + sz, :], o[0:sz, :], reads=[o], writes=[C.u_PT[c0]])


def stage_merge(k, C, l):
    with k.scope() as sc:
        Y = sc.sb("Y", [128, 16, T], BF16)
        MT = sc.sb("MT", [128, 8, T], BF16)
        for i in range(16):
            k.dma(Y[:, i, :], C.YT[i * 128:(i + 1) * 128, :], reads=[C.u_YT[i]], writes=[Y])
        branches = [("p_a", 0, 4), ("p_b", 4, 4), ("p_c", 8, 8)]
        specs = [(wn, l, m * 128, 128, nk) for m in range(8) for (wn, y0, nk) in branches]
        specs += [("w_out", l, m * 128, 128, 8) for m in range(8)]
        ws = WStream(k, sc, C, specs, 8)
        gt = [sc.sb("gt", [128, T], BF16) for _ in range(2)]
        sg = [sc.sb("sg", [128, T], F32) for _ in range(2)]
        acc = sc.sb("acc", [128, T], F32)
        tmp = sc.sb("mtmp", [128, T], F32)
        branches = [("p_a", 0, 4), ("p_b", 4, 4), ("p_c", 8, 8)]
        n = 0
        for m in range(8):
            for bi, (wn, y0, nk) in enumerate(branches):
                w = ws.get(n)
                g = gt[n % 2]
                c0 = C_G + bi * 1024 + m * 128
                k.dma(g[:], C.PT[c0:c0 + 128, :], reads=[C.u_PT[c0]], writes=[g])
                s = sg[n % 2]
                k.op("act", lambda e: e.activation(out=s[:], in_=g[:], func=AF.Sigmoid), reads=[g], writes=[s])
                banks = C.PS[(n % 2) * 4:(n % 2) * 4 + 4]
                for ct in range(4):
                    for kt in range(nk):
                        k.op("pe", lambda e: e.matmul(banks[ct][:], lhsT=w[:, kt, :],
                                                      rhs=Y[:, y0 + kt, ct * 512:(ct + 1) * 512],
                                                      start=(kt == 0), stop=(kt == nk - 1)),
                             reads=[w, Y], writes=[banks[ct]], inc=(kt == nk - 1))
                for ct in range(4):
                    cs = slice(ct * 512, (ct + 1) * 512)
                    if bi == 0:
                        k.op("dve", lambda e: e.tensor_tensor(out=acc[:, cs], in0=banks[ct][:], in1=s[:, cs], op=ALU.mult),
                             reads=[banks[ct], s], writes=[acc])
                    else:
                        k.op("dve", lambda e: e.tensor_tensor(out=tmp[:, cs], in0=banks[ct][:], in1=s[:, cs], op=ALU.mult),
                             reads=[banks[ct], s], writes=[tmp])
                        if bi == 1:
                            k.op("pool", lambda e: e.tensor_tensor(out=acc[:, cs], in0=acc[:, cs], in1=tmp[:, cs], op=ALU.add),
                                 reads=[acc, tmp], writes=[acc])
                        else:
                            k.op("pool", lambda e: e.tensor_tensor(out=MT[:, m, cs], in0=acc[:, cs], in1=tmp[:, cs], op=ALU.add),
                                 reads=[acc, tmp], writes=[MT])
                n += 1
        if C.dbg:
            for m in range(8):
                k.dma(C.dbgMT[m * 128:(m + 1) * 128, :], MT[:, m, :], reads=[MT])
        xr = [sc.sb("xr", [128, T], F32) for _ in range(2)]
        for m in range(8):
            w = ws.get(n)
            x_ = xr[m % 2]
            k.dma(x_[:], C.xT[m * 128:(m + 1) * 128, :], reads=[C.u_xT[m]], writes=[x_])
            banks = C.PS[(n % 2) * 4:(n % 2) * 4 + 4]
            for ct in range(4):
                for kt in range(8):
                    k.op("pe", lambda e: e.matmul(banks[ct][:], lhsT=w[:, kt, :],
                                                  rhs=MT[:, kt, ct * 512:(ct + 1) * 512],
                                                  start=(kt == 0), stop=(kt == 7)),
                         reads=[w, MT], writes=[banks[ct]], inc=(kt == 7))
            for ct in range(4):
                cs = slice(ct * 512, (ct + 1) * 512)
                k.op("dve", lambda e: e.tensor_tensor(out=x_[:, cs], in0=banks[ct][:], in1=x_[:, cs], op=ALU.add),
                     reads=[banks[ct], x_], writes=[x_])
            k.dma(C.xT[m * 128:(m + 1) * 128, :], x_[:], reads=[x_], writes=[C.u_xT[m]])
            n += 1


def stage_ffn(k, C, l):
    HT = C.HT
    SK = set(os.environ.get('FFN_SKIP', '').split(','))
    with k.scope() as sc0:
        AT = sc0.sb("AT", [128, 22, T], BF16)
        n = 0
        with k.scope() as sc:
            NJ = int(os.environ.get('FFN_J', '22'))
            ws = WStream(k, sc, C, [("ffn_up", l, (half * 22 + j) * 128, 128, 8) for j in range(NJ) for half in range(2)], 8)
            cw = sc.sb("cw", [128, 4, 44], F32)
            load_featvecs(k, C, cw, [C.W["ffn_conv_w"][l, 0], C.W["ffn_conv_w"][l, 1], C.W["ffn_conv_w"][l, 2],
                                     C.W["ffn_conv_b"][l]], 44)
            gs = [sc.sb("gs", [128, T + 2], BF16) for _ in range(2)]
            ac = [sc.sb("ac", [128, T], F32) for _ in range(2)]
            tb = [sc.sb("tb", [128, T], F32) for _ in range(2)]
            sgt = sc.sb("sgt", [128, T], F32)
            for g in gs:
                k.op("pool", lambda e: e.memset(g[:, 0:1], 0.0), writes=[g])
                k.op("pool", lambda e: e.memset(g[:, T + 1:T + 2], 0.0), writes=[g])
            for j in range(NJ):
                for half in range(2):
                    fi = half * 22 + j
                    w = ws.get(n)
                    banks = C.PS[(n % 2) * 4:(n % 2) * 4 + 4]
                    for ct in range(4):
                        for kt in range(8):
                            k.op("pe", lambda e: e.matmul(banks[ct][:], lhsT=w[:, kt, :],
                                                          rhs=HT[:, kt, ct * 512:(ct + 1) * 512],
                                                          start=(kt == 0), stop=(kt == 7)),
                                 reads=[w, HT], writes=[banks[ct]], inc=(kt == 7))
                    g = gs[half]
                    a = ac[half]
                    t1, t2 = tb[0], tb[1]
                    for ct in range(4):
                        cs = slice(ct * 512, (ct + 1) * 512)
                        k.op("act", lambda e: e.activation(out=g[:, 1 + ct * 512:1 + (ct + 1) * 512], in_=banks[ct][:], func=AF.Copy),
                             reads=[banks[ct]], writes=[g])
                        k.op("dve", lambda e: e.tensor_scalar(out=t1[:, cs], in0=banks[ct][:], scalar1=cw[:, 1, fi:fi + 1],
                                                              scalar2=cw[:, 3, fi:fi + 1], op0=ALU.mult, op1=ALU.add),
                             reads=[banks[ct], cw], writes=[t1])
                    k.op("dve", lambda e: e.scalar_tensor_tensor(out=t2[:, :], in0=g[:, 0:T], scalar=cw[:, 0, fi:fi + 1],
                                                                 op0=ALU.mult, in1=t1[:, :], op1=ALU.add),
                         reads=[g, t1, cw], writes=[t2])
                    k.op("dve", lambda e: e.scalar_tensor_tensor(out=a[:, :], in0=g[:, 2:T + 2], scalar=cw[:, 2, fi:fi + 1],
                                                                 op0=ALU.mult, in1=t2[:, :], op1=ALU.add),
                         reads=[g, t2, cw], writes=[a])
                    n += 1
                k.op("act", lambda e: e.activation(out=sgt[:], in_=ac[0][:], func=AF.Silu), reads=[ac[0]], writes=[sgt])
                k.op("pool", lambda e: e.tensor_tensor(out=AT[:, j, :], in0=sgt[:], in1=ac[1][:], op=ALU.mult),
                     reads=[sgt, ac[1]], writes=[AT])
        with k.scope() as sc:
            NM = int(os.environ.get('FFN_M', '8'))
            wsd = WStream(k, sc, C, [("ffn_down", l, m * 128, 128, 22) for m in range(NM)], 22, nbuf=2, name="wd")
            xr = [sc.sb("xr", [128, T], F32) for _ in range(2)]
            for m in range(int(os.environ.get('FFN_M', '8'))):
                w = wsd.get(m)
                x_ = xr[m % 2]
                k.dma(x_[:], C.xT[m * 128:(m + 1) * 128, :], reads=[C.u_xT[m]], writes=[x_])
                banks = C.PS[(n % 2) * 4:(n % 2) * 4 + 4]
                for ct in range(4):
                    for kt in range(22):
                        k.op("pe", lambda e: e.matmul(banks[ct][:], lhsT=w[:, kt, :],
                                                      rhs=AT[:, kt, ct * 512:(ct + 1) * 512],
                                                      start=(kt == 0), stop=(kt == 21)),
                             reads=[w, AT], writes=[banks[ct]], inc=(kt == 21))
                for ct in range(4):
                    cs = slice(ct * 512, (ct + 1) * 512)
                    k.op("dve", lambda e: e.tensor_tensor(out=x_[:, cs], in0=banks[ct][:], in1=x_[:, cs], op=ALU.add),
                         reads=[banks[ct], x_], writes=[x_])
                k.dma(C.xT[m * 128:(m + 1) * 128, :], x_[:], reads=[x_], writes=[C.u_xT[m]])
                n += 1


def stage_final(k, C, b):
    wcol = C.nw[("norm_final", 0)]
    with k.scope() as sc:
        xs = [sc.sb("fx", [128, 8, 512], F32) for _ in range(2)]
        sq = [sc.sb("fsq", [128, 8, 512], BF16) for _ in range(2)]
        rs = [sc.sb("frs", [128, 512], F32) for _ in range(2)]
        hn = [sc.sb("fhn", [128, 8, 512], F32) for _ in range(2)]
        ot = [sc.sb("fo", [128, D], F32) for _ in range(2)]
        n = 0
        for ct in range(4):
            xt = xs[ct % 2]
            k.dma(xt[:], C.xT[:, ct * 512:(ct + 1) * 512].rearrange("(n p) t -> p n t", p=128),
                  reads=C.u_xT, writes=[xt])
            r = rstd_cols(k, C, (sq, rs), ct, xt)
            h = hn[ct % 2]
            for dt in range(8):
                k.op("dve", lambda e: e.scalar_tensor_tensor(out=h[:, dt, :], in0=xt[:, dt, :],
                                                             scalar=wcol[0][:, wcol[1], dt:dt + 1], op0=ALU.mult,
                                                             in1=r[:], op1=ALU.mult),
                     reads=[xt, r, wcol[0]], writes=[h])
            for tt in range(4):
                o = ot[n % 2]
                for half in range(2):
                    ps = C.PS[4 + (n * 2 + half) % 4]
                    for j in range(4):
                        dt = half * 4 + j
                        k.op("pe", lambda e: e.transpose(out=ps[:, j * 128:(j + 1) * 128],
                                                         in_=h[:, dt, tt * 128:(tt + 1) * 128],
                                                         identity=C.identF[:]),
                             reads=[h, C.identF], writes=[ps], inc=(j == 3))
                    evac(k, half, o[:, half * 512:(half + 1) * 512], ps[:], [ps], [o])
                t0 = ct * 512 + tt * 128
                k.dma(C.out[b, t0:t0 + 128, :], o[:], reads=[o])
                n += 1


def build_program(nseq=NSEQ, dbg=None, stages=None):
    k = KB()
    C = build_context(k, nseq, dbg)
    hyena_context(k, C)
    ssd_context(k, C)
    s5_context(k, C)
    C.epsT = k.sb("epsT", [128, 1], F32)
    k.op("dve", lambda e: e.memset(C.epsT[:], EPS), writes=[C.epsT])
    if dbg:
        C.dbgMT = k.dram("dbgMT", [D, T], BF16, kind="ExternalOutput")
        C.dbgHT = k.dram("dbgHT", [D, T], BF16, kind="ExternalOutput")
    st = stages
    on = lambda n: (st is None) or (n in st)
    if dbg is None and stages is None:
        stage_convert_weights(k, C)
        for l in range(DEPTH):
            stage_hyena_filters(k, C, l)
            stage_s5_tables(k, C, l)
        for l in range(DEPTH):
            for b in range(nseq):
                C.set_seq(b)
                if l == 0:
                    stage_load_x(k, C, b)
                stage_norm(k, C, C.nw[("norm_mix", l)])
                stage_inproj(k, C, l)
                stage_hyena(k, C, l)
                stage_ssd(k, C, l)
            stage_s5_batched(k, C, l, nseq)
            for b in range(nseq):
                C.set_seq(b)
                stage_merge(k, C, l)
                stage_norm(k, C, C.nw[("norm_ffn", l)])
                stage_ffn(k, C, l)
                if l == DEPTH - 1:
                    stage_final(k, C, b)
        k.finish()
        return k, C
    st = stages
    on = lambda n: (st is None) or (n in st)
    for b in range(nseq):
        if b == 0 and on("conv"):
            stage_convert_weights(k, C)
        if b == 0 and on("hyfilt"):
            for l in range(int(os.environ.get('NLAYERS', DEPTH))):
                stage_hyena_filters(k, C, l)
        if b == 0 and on("s5tab"):
            for l in range(int(os.environ.get('NLAYERS', DEPTH))):
                stage_s5_tables(k, C, l)
        if on("load"):
            stage_load_x(k, C, b)
        for l in range(int(os.environ.get('NLAYERS', DEPTH))):
            if on("norm1"):
                stage_norm(k, C, C.nw[("norm_mix", l)])
                if dbg and b == 0 and l == 0:
                    for dt in range(8):
                        k.dma(C.dbgHT[dt * 128:(dt + 1) * 128, :], C.HT[:, dt, :], reads=[C.HT])
            if on("inproj"):
                stage_inproj(k, C, l)
            if on("hyena"):
                stage_hyena(k, C, l)
            if on("s5"):
                stage_s5(k, C, l)
            if on("ssd"):
                stage_ssd(k, C, l)
            if on("merge"):
                stage_merge(k, C, l)
            if on("norm2"):
                stage_norm(k, C, C.nw[("norm_ffn", l)])
            if on("ffn"):
                stage_ffn(k, C, l)
            if dbg == "ytest":
                break
        if on("final"):
            stage_final(k, C, b)
    k.finish()
    return k, C


NFT = 16
TWO_PI = 2.0 * math.pi
MAGIC = 12582912.0


def hyena_consts():
    L, N = T, 2 * T
    f = (np.arange(2048, dtype=np.float64) + 0.5)
    t = np.arange(L, dtype=np.float64)
    th = TWO_PI * np.outer(f, t) / N
    cs = np.stack([np.cos(th), np.sin(th)], 0)
    inv = cs.reshape(32, 128, L).astype(ml_dtypes.bfloat16)
    fw = cs.reshape(2, 16, 128, 16, 128)
    fw = np.ascontiguousarray(fw.transpose(0, 1, 4, 3, 2)).reshape(32, 128, 16, 128).astype(ml_dtypes.bfloat16)
    pos = np.arange(L, dtype=np.float32)
    tn = (pos / np.float32(L - 1)).astype(np.float32)
    bands = np.linspace(1e-4, 15.0, 16, dtype=np.float32)
    ang = (np.float32(2.0 * math.pi / L) * pos[:, None] * bands[None, :]).astype(np.float32)
    z = np.concatenate([tn[:, None], np.cos(ang), -np.sin(ang)], axis=-1).astype(np.float32)
    zT = np.ascontiguousarray(z.T)
    tpos = np.ascontiguousarray(np.broadcast_to(tn[None, :], (128, L))).astype(np.float32)
    return {"c_dftf": fw, "c_dfti": inv, "c_zT": zT, "c_tpos": tpos}


def hyena_context(k, C):
    ein = lambda n, s, dt=F32: k.dram(n, s, dt, kind="ExternalInput")
    C.c_dftf = ein("c_dftf", [32, 128, 16, 128], BF16)
    C.c_dfti = ein("c_dfti", [32, 128, T], BF16)
    C.c_zT = ein("c_zT", [33, T])
    C.c_tpos = ein("c_tpos", [128, T])
    for nm, shp in (("hy_short_w", [DEPTH, 3, 1536]), ("hy_short_b", [DEPTH, 1536]), ("hy_w1", [DEPTH, 33, 64]),
                    ("hy_b1", [DEPTH, 64]), ("hy_freq", [DEPTH, 2, 64]), ("hy_w2", [DEPTH, 64, 64]),
                    ("hy_b2", [DEPTH, 64]), ("hy_w3", [DEPTH, 64, 1024]), ("hy_decay", [DEPTH, 1024]),
                    ("hy_bias", [DEPTH, 512])):
        C.W[nm] = ein(nm, shp)
    kind = "ExternalOutput" if C.dbg else "Internal"
    C.KTAB = k.dram("KTAB", [DEPTH, 32, 128, 512], F32, kind=kind)
    C.u_KTAB = [Unit("KTAB%d" % l) for l in range(DEPTH)]


def sin_reduced(k, sc, h, bias_col, freq_col, bu, np_, ncols):
    y = sc.sb("sry", [np_, ncols], F32)
    t = sc.sb("srt", [np_, ncols], F32)
    k.op("dve", lambda e: e.tensor_scalar(out=y[:], in0=h[:], scalar1=bias_col, scalar2=freq_col,
                                          op0=ALU.add, op1=ALU.mult), reads=[h, bu], writes=[y])
    k.op("dve", lambda e: e.tensor_scalar(out=t[:], in0=y[:], scalar1=1.0 / TWO_PI, scalar2=MAGIC,
                                          op0=ALU.mult, op1=ALU.add), reads=[y], writes=[t])
    k.op("dve", lambda e: e.tensor_scalar(out=t[:], in0=t[:], scalar1=-MAGIC, scalar2=None, op0=ALU.add),
         reads=[t], writes=[t])
    k.op("dve", lambda e: e.scalar_tensor_tensor(out=y[:], in0=t[:], scalar=-TWO_PI, op0=ALU.mult, in1=y[:],
                                                 op1=ALU.add), reads=[t, y], writes=[y])
    k.op("act", lambda e: e.activation(out=h[:], in_=y[:], func=AF.Sin), reads=[y], writes=[h])


def dft_forward(k, C, sc, rhs_for_ft, consume):
    fw = [sc.sb("dftf", [128, 16, 128], BF16) for _ in range(4)]
    issued = [0]

    def issue_upto(n):
        while issued[0] < min(n, 32):
            i = issued[0]
            ft = (i // 2) + 16 * (i % 2)
            k.dma(fw[i % 4][:], C.c_dftf[ft], writes=[fw[i % 4]])
            issued[0] += 1
    for j in range(NFT):
        issue_upto(2 * j + 4)
        banks = (C.PS[(j % 2) * 2], C.PS[(j % 2) * 2 + 1])
        for part in range(2):
            w = fw[(2 * j + part) % 4]
            rhs, ru = rhs_for_ft(part)
            for kt in range(16):
                k.op("pe", lambda e: e.matmul(banks[part][:], lhsT=w[:, kt, :], rhs=rhs[:, kt, :],
                                              start=(kt == 0), stop=(kt == 15)),
                     reads=[w, ru], writes=[banks[part]], inc=(kt == 15))
        consume(j, banks[0], banks[1])


def transpose_to_tokmajor(k, C, dstT, src_tiles_fn, nct, src_unit):
    n = 0
    for kt in range(16):
        ps = C.PS[4 + (kt % 4)]
        psb = ps[:].bitcast(BF16)
        for ci in range(nct):
            src = src_tiles_fn(ci)
            k.op("pe", lambda e: e.transpose(out=psb[:, ci * 128:(ci + 1) * 128], in_=src[:, kt * 128:(kt + 1) * 128],
                                             identity=C.identB[:]),
                 reads=[src_unit, C.identB], writes=[ps], inc=(ci == nct - 1))
        evac(k, kt, dstT[:, kt, 0:nct * 128], psb[:, 0:nct * 128], [ps], [dstT])


def stage_hyena_filters(k, C, l):
    W = C.W
    with k.scope() as sc0:
        kpm = [sc0.sb("kpm", [128, 4, T], BF16) for _ in range(2)]
        with k.scope() as sc:
            zT = sc.sb("zT", [33, T], F32)
            k.dma(zT[:], C.c_zT[:, :], writes=[zT])
            w1 = sc.sb("hw1", [33, 64], F32)
            w2 = sc.sb("hw2", [64, 64], F32)
            w3 = sc.sb("hw3", [64, 1024], F32)
            k.dma(w1[:], W["hy_w1"][l], writes=[w1])
            k.dma(w2[:], W["hy_w2"][l], writes=[w2])
            k.dma(w3[:], W["hy_w3"][l], writes=[w3])
            sv = sc.sb("hsv", [64, 4], F32)
            for i, v in enumerate((W["hy_b1"][l], W["hy_freq"][l, 0], W["hy_b2"][l], W["hy_freq"][l, 1])):
                k.dma(sv[:, i:i + 1], v.rearrange("(p o) -> p o", o=1), writes=[sv])
            dec = sc.sb("hdec", [128, 1, 8], F32)
            load_featvecs(k, C, dec, [W["hy_decay"][l]], 8)
            decn = sc.sb("hdecn", [128, 1, 8], F32)
            k.op("dve", lambda e: e.tensor_scalar(out=decn[:], in0=dec[:], scalar1=-1.0, scalar2=None, op0=ALU.mult),
                 reads=[dec], writes=[decn])
            k.op("dve", lambda e: e.tensor_tensor(out=dec[:], in0=dec[:], in1=decn[:], op=ALU.min),
                 reads=[dec, decn], writes=[dec])
            tpos = sc.sb("tpos", [128, T], F32)
            k.dma(tpos[:], C.c_tpos[:, :], writes=[tpos])
            h1 = sc.sb("h1", [64, T], F32)
            h2 = sc.sb("h2", [64, T], F32)
            for ct in range(4):
                cs = slice(ct * 512, (ct + 1) * 512)
                ps = C.PS[ct]
                k.op("pe", lambda e: e.matmul(ps[0:64, :], lhsT=w1[:, :], rhs=zT[:, cs], start=True, stop=True),
                     reads=[w1, zT], writes=[ps])
                k.op("dve", lambda e: e.tensor_copy(out=h1[:, cs], in_=ps[0:64, :]), reads=[ps], writes=[h1])
            with k.scope() as s2:
                sin_reduced(k, s2, h1, sv[:, 0:1], sv[:, 1:2], sv, 64, T)
            for ct in range(4):
                cs = slice(ct * 512, (ct + 1) * 512)
                ps = C.PS[ct]
                k.op("pe", lambda e: e.matmul(ps[0:64, :], lhsT=w2[:, :], rhs=h1[:, cs], start=True, stop=True),
                     reads=[w2, h1], writes=[ps])
                k.op("dve", lambda e: e.tensor_copy(out=h2[:, cs], in_=ps[0:64, :]), reads=[ps], writes=[h2])
            with k.scope() as s2:
                sin_reduced(k, s2, h2, sv[:, 2:3], sv[:, 3:4], sv, 64, T)
            hfb = [sc.sb("hfb", [128, T], F32) for _ in range(2)]
            win = sc.sb("hwin", [128, T], F32)
            nrm = sc.sb("hnrm", [128, 2], F32)
            tot = sc.sb("htot", [128, 1], F32)
            tmp = sc.sb("htmp", [128, T], F32)
            for ci in range(4):
                for half in range(2):
                    ft = half * 4 + ci
                    dstt = hfb[half]
                    k.op("act", lambda e: e.activation(out=win[:], in_=tpos[:], func=AF.Exp, scale=dec[:, 0, ft:ft + 1]),
                         reads=[tpos, dec], writes=[win])
                    for ct in range(4):
                        cs = slice(ct * 512, (ct + 1) * 512)
                        ps = C.PS[(ft * 4 + ct) % 8]
                        k.op("pe", lambda e: e.matmul(ps[:], lhsT=w3[:, ft * 128:(ft + 1) * 128], rhs=h2[:, cs],
                                                      start=True, stop=True), reads=[w3, h2], writes=[ps])
                        k.op("dve", lambda e: e.tensor_tensor(out=dstt[:, cs], in0=ps[:], in1=win[:, cs], op=ALU.mult),
                             reads=[ps, win], writes=[dstt])
                    if half == 1:
                        k.op("dve", lambda e: e.memset(dstt[:, 0:1], 0.0), writes=[dstt])
                    k.op("dve", lambda e: e.tensor_reduce(out=nrm[:, half:half + 1], in_=dstt[:, :], op=ALU.add,
                                                          axis=AX.X, apply_absolute_value=True),
                         reads=[dstt], writes=[nrm])
                k.op("dve", lambda e: e.tensor_tensor(out=tot[:], in0=nrm[:, 0:1], in1=nrm[:, 1:2], op=ALU.add),
                     reads=[nrm], writes=[tot])
                k.op("dve", lambda e: e.reciprocal(out=tot[:], in_=tot[:]), reads=[tot], writes=[tot])
                k.op("dve", lambda e: e.tensor_scalar(out=tot[:], in0=tot[:], scalar1=2.0 / (2 * T), scalar2=None, op0=ALU.mult),
                     reads=[tot], writes=[tot])
                k.op("dve", lambda e: e.tensor_tensor(out=tmp[:], in0=hfb[0][:], in1=hfb[1][:], op=ALU.add),
                     reads=hfb, writes=[tmp])
                k.op("dve", lambda e: e.tensor_scalar(out=kpm[0][:, ci, :], in0=tmp[:], scalar1=tot[:, 0:1],
                                                      scalar2=None, op0=ALU.mult), reads=[tmp, tot], writes=[kpm[0]])
                k.op("dve", lambda e: e.tensor_tensor(out=tmp[:], in0=hfb[0][:], in1=hfb[1][:], op=ALU.subtract),
                     reads=hfb, writes=[tmp])
                k.op("dve", lambda e: e.tensor_scalar(out=kpm[1][:, ci, :], in0=tmp[:], scalar1=tot[:, 0:1],
                                                      scalar2=None, op0=ALU.mult), reads=[tmp, tot], writes=[kpm[1]])
        with k.scope() as sc:
            kT = [sc.sb("kT", [128, 16, 512], BF16) for _ in range(2)]
            for i in range(2):
                transpose_to_tokmajor(k, C, kT[i], lambda ci, i=i: kpm[i][:, ci, :], 4, kpm[i])
            ko = [sc.sb("ko", [128, 512], F32) for _ in range(4)]
            cnt = [0]

            def consume(j, bA, bB):
                for part, bk in enumerate((bA, bB)):
                    o = ko[cnt[0] % 4]
                    evac(k, cnt[0], o[:], bk[:], [bk], [o])
                    k.dma(C.KTAB[l, part * 16 + j], o[:], reads=[o], writes=[C.u_KTAB[l]])
                    cnt[0] += 1
            dft_forward(k, C, sc, lambda part: (kT[part], kT[part]), consume)


def stage_hyena(k, C, l):
    W = C.W
    with k.scope() as sc0:
        X0 = sc0.sb("hyX0", [128, 4, T], BF16)
        S = sc0.sb("hyS", [128, 4, T], BF16)
        Y = sc0.sb("hyY", [128, 32, 512], BF16)
        bias = sc0.sb("hybias", [128, 1, 4], F32)
        load_featvecs(k, C, bias, [W["hy_bias"][l]], 4)
        with k.scope() as sc:
            cw = sc.sb("hycw", [128, 4, 12], F32)
            load_featvecs(k, C, cw, [W["hy_short_w"][l, 0], W["hy_short_w"][l, 1], W["hy_short_w"][l, 2],
                                     W["hy_short_b"][l]], 12)
            X1 = sc.sb("hyX1", [128, 4, T], BF16)
            gin = [sc.sb("hyg", [128, T + 2], BF16) for _ in range(2)]
            t1 = sc.sb("hyt1", [128, T], F32)
            t2 = sc.sb("hyt2", [128, T], F32)
            for g in gin:
                k.op("pool", lambda e: e.memset(g[:, 0:1], 0.0), writes=[g])
                k.op("pool", lambda e: e.memset(g[:, T + 1:T + 2], 0.0), writes=[g])
            for fi in range(12):
                g = gin[fi % 2]
                k.dma(g[:, 1:T + 1], C.PT[fi * 128:(fi + 1) * 128, :], reads=[C.u_PT[fi * 128]], writes=[g])
                k.op("dve", lambda e: e.tensor_scalar(out=t1[:], in0=g[:, 1:T + 1], scalar1=cw[:, 1, fi:fi + 1],
                                                      scalar2=cw[:, 3, fi:fi + 1], op0=ALU.mult, op1=ALU.add),
                     reads=[g, cw], writes=[t1])
                k.op("dve", lambda e: e.scalar_tensor_tensor(out=t2[:], in0=g[:, 0:T], scalar=cw[:, 0, fi:fi + 1],
                                                             op0=ALU.mult, in1=t1[:], op1=ALU.add),
                     reads=[g, t1, cw], writes=[t2])
                if fi < 8:
                    dst = X0[:, fi, :] if fi < 4 else X1[:, fi - 4, :]
                    du = X0 if fi < 4 else X1
                    k.op("dve", lambda e: e.scalar_tensor_tensor(out=dst, in0=g[:, 2:T + 2], scalar=cw[:, 2, fi:fi + 1],
                                                                 op0=ALU.mult, in1=t2[:], op1=ALU.add),
                         reads=[g, t2, cw], writes=[du])
                else:
                    k.op("dve", lambda e: e.scalar_tensor_tensor(out=t1[:], in0=g[:, 2:T + 2], scalar=cw[:, 2, fi:fi + 1],
                                                                 op0=ALU.mult, in1=t2[:], op1=ALU.add),
                         reads=[g, t2, cw], writes=[t1])
                    k.op("pool", lambda e: e.tensor_tensor(out=S[:, fi - 8, :], in0=t1[:], in1=X1[:, fi - 8, :], op=ALU.mult),
                         reads=[t1, X1], writes=[S])
        with k.scope() as sc:
            sT = sc.sb("hysT", [128, 16, 512], BF16)
            transpose_to_tokmajor(k, C, sT, lambda ci: S[:, ci, :], 4, S)
            kt_ = [sc.sb("hykt", [128, 2, 512], F32) for _ in range(2)]
            pw = [sc.sb("hypw", [128, 512], F32) for _ in range(4)]

            def consume(j, bA, bB):
                kc = kt_[j % 2]
                k.dma(kc[:, 0, :], C.KTAB[l, j], reads=[C.u_KTAB[l]], writes=[kc])
                k.dma(kc[:, 1, :], C.KTAB[l, 16 + j], reads=[C.u_KTAB[l]], writes=[kc])
                k.op("dve", lambda e: e.tensor_tensor(out=pw[0][:], in0=bA[:], in1=kc[:, 0, :], op=ALU.mult), reads=[bA, kc], writes=[pw[0]])
                k.op("dve", lambda e: e.tensor_tensor(out=pw[1][:], in0=bB[:], in1=kc[:, 1, :], op=ALU.mult), reads=[bB, kc], writes=[pw[1]])
                k.op("dve", lambda e: e.tensor_tensor(out=pw[2][:], in0=bA[:], in1=kc[:, 1, :], op=ALU.mult), reads=[bA, kc], writes=[pw[2]])
                k.op("dve", lambda e: e.tensor_tensor(out=pw[3][:], in0=bB[:], in1=kc[:, 0, :], op=ALU.mult), reads=[bB, kc], writes=[pw[3]])
                k.op("pool", lambda e: e.tensor_tensor(out=Y[:, j, :], in0=pw[0][:], in1=pw[1][:], op=ALU.subtract), reads=[pw[0], pw[1]], writes=[Y])
                k.op("pool", lambda e: e.tensor_tensor(out=Y[:, 16 + j, :], in0=pw[2][:], in1=pw[3][:], op=ALU.add), reads=[pw[2], pw[3]], writes=[Y])
            dft_forward(k, C, sc, lambda part: (sT, sT), consume)
        with k.scope() as sc:
            di = [sc.sb("hydi", [128, 8, 512], BF16) for _ in range(3)]
            u_ = [sc.sb("hyu", [128, 512], F32) for _ in range(2)]
            yo = [sc.sb("hyyo", [128, 512], BF16) for _ in range(2)]
            nld = 0
            ne = 0
            for tt in range(4):
                ts_ = slice(tt * 512, (tt + 1) * 512)
                banks = C.PS[(tt % 2) * 4:(tt % 2) * 4 + 4]
                for fg in range(4):
                    d = di[nld % 3]
                    nld += 1
                    k.dma(d[:], C.c_dfti[fg * 8:(fg + 1) * 8, :, ts_].rearrange("f p t -> p f t"), writes=[d])
                    for fo in range(8):
                        ft = fg * 8 + fo
                        for ci in range(4):
                            k.op("pe", lambda e: e.matmul(banks[ci][:], lhsT=Y[:, ft, ci * 128:(ci + 1) * 128], rhs=d[:, fo, :],
                                                          start=(ft == 0), stop=(ft == 31)),
                                 reads=[Y, d], writes=[banks[ci]], inc=(ft == 31 or (fo == 7 and ci == 3)))
                for ci in range(4):
                    u = u_[ne % 2]
                    o = yo[ne % 2]
                    ne += 1
                    k.op("dve", lambda e: e.scalar_tensor_tensor(out=u[:], in0=S[:, ci, ts_], scalar=bias[:, 0, ci:ci + 1],
                                                                 op0=ALU.mult, in1=banks[ci][:], op1=ALU.add),
                         reads=[S, bias, banks[ci]], writes=[u])
                    k.op("pool", lambda e: e.tensor_tensor(out=o[:], in0=u[:], in1=X0[:, ci, ts_], op=ALU.mult),
                         reads=[u, X0], writes=[o])
                    k.dma(C.YT[ci * 128:(ci + 1) * 128, ts_], o[:], reads=[o], writes=[C.u_YT[ci]])


NCH = 16
NH = 16


def ssd_consts():
    tri = np.triu(np.ones((128, 128), np.float32))
    return {"c_triF": tri, "c_triB": np.ascontiguousarray(tri.T)}


def ssd_context(k, C):
    ein = lambda n, s, dt=F32: k.dram(n, s, dt, kind="ExternalInput")
    C.c_triF = ein("c_triF", [128, 128])
    C.c_triB = ein("c_triB", [128, 128])
    for nm, shp in (("ssd_conv_w", [DEPTH, 5, 1536]), ("ssd_conv_b", [DEPTH, 1536]), ("ssd_a_log", [DEPTH, 2, 16]),
                    ("ssd_dt_bias", [DEPTH, 2, 16]), ("ssd_d", [DEPTH, 16]), ("ssd_norm", [DEPTH, 1024])):
        C.W[nm] = ein(nm, shp)
    kind = "ExternalOutput" if C.dbg else "Internal"
    C.YR = k.dram("YR", [1024, T], F32, kind=kind)
    C.u_YR = units("YR", 8)
    C.triF = k.sb("triF", [128, 128], F32)
    C.triB = k.sb("triB", [128, 128], F32)
    C.onesF = k.sb("onesF", [128, 128], F32)
    k.dma(C.triF[:], C.c_triF[:, :], writes=[C.triF])
    k.dma(C.triB[:], C.c_triB[:, :], writes=[C.triB])
    k.op("dve", lambda e: e.memset(C.onesF[:], 1.0), writes=[C.onesF])


def stage_ssd(k, C, l):
    W = C.W
    with k.scope() as sc0:
        xT = sc0.sb("sxT", [128, NCH, 1024], BF16)
        BT = sc0.sb("sBT", [128, NCH, 256], BF16)
        BCf = sc0.sb("sBCf", [128, 4, T], BF16)
        dtT = sc0.sb("sdtT", [128, NCH, 32], F32)
        aT = sc0.sb("saT", [128, NCH, 32], F32)
        acs = sc0.sb("sacs", [128, NCH, 32], F32)
        tot = sc0.sb("stot", [128, NCH, 32], F32)
        etot = sc0.sb("setot", [128, NCH, 32], F32)
        eacs = sc0.sb("seacs", [128, NCH, 32], F32)
        Wst = sc0.sb("sWst", [128, NCH, 32], F32)
        dvec = sc0.sb("sdvec", [128, 16], F32)
        k.dma(dvec[:], W["ssd_d"][l].partition_broadcast(128), writes=[dvec])
        with k.scope() as sc:
            cw = sc.sb("scw", [128, 6, 12], F32)
            load_featvecs(k, C, cw, [W["ssd_conv_w"][l, i] for i in range(5)] + [W["ssd_conv_b"][l]], 12)
            gin = [sc.sb("sg", [128, T + 4], BF16) for _ in range(2)]
            t1 = sc.sb("st1", [128, T], F32)
            t2 = sc.sb("st2", [128, T], F32)
            xc = [sc.sb("sxc", [128, T], BF16) for _ in range(2)]
            for g in gin:
                k.op("pool", lambda e: e.memset(g[:, 0:2], 0.0), writes=[g])
                k.op("pool", lambda e: e.memset(g[:, T + 2:T + 4], 0.0), writes=[g])
            for fi in range(12):
                g = gin[fi % 2]
                c0 = C_XBC + fi * 128
                k.dma(g[:, 2:T + 2], C.PT[c0:c0 + 128, :], reads=[C.u_PT[c0]], writes=[g])
                k.op("dve", lambda e: e.tensor_scalar(out=t1[:], in0=g[:, 2:T + 2], scalar1=cw[:, 2, fi:fi + 1],
                                                      scalar2=cw[:, 5, fi:fi + 1], op0=ALU.mult, op1=ALU.add),
                     reads=[g, cw], writes=[t1])
                src, dst = t1, t2
                for tap in (0, 1, 3, 4):
                    k.op("dve", lambda e: e.scalar_tensor_tensor(out=dst[:], in0=g[:, tap:tap + T], scalar=cw[:, tap, fi:fi + 1],
                                                                 op0=ALU.mult, in1=src[:], op1=ALU.add),
                         reads=[g, src, cw], writes=[dst])
                    src, dst = dst, src
                if fi < 8:
                    o = xc[fi % 2]
                    k.op("act", lambda e: e.activation(out=o[:], in_=src[:], func=AF.Silu), reads=[src], writes=[o])
                    for c4 in range(4):
                        ps = C.PS[4 + (c4 % 4)]
                        psb = ps[:].bitcast(BF16)
                        for j in range(4):
                            c = c4 * 4 + j
                            k.op("pe", lambda e: e.transpose(out=psb[:, j * 128:(j + 1) * 128], in_=o[:, c * 128:(c + 1) * 128],
                                                             identity=C.identB[:]), reads=[o, C.identB], writes=[ps], inc=(j == 3))
                        evac(k, c4, xT[:, c4 * 4:(c4 + 1) * 4, fi * 128:(fi + 1) * 128],
                             psb[:, 0:512].rearrange("p (c f) -> p c f", c=4), [ps], [xT])
                else:
                    bi = fi - 8
                    k.op("act", lambda e: e.activation(out=BCf[:, bi, :], in_=src[:], func=AF.Silu), reads=[src], writes=[BCf])
                    if bi < 2:
                        for c4 in range(4):
                            ps = C.PS[4 + (c4 % 4)]
                            psb = ps[:].bitcast(BF16)
                            for j in range(4):
                                c = c4 * 4 + j
                                k.op("pe", lambda e: e.transpose(out=psb[:, j * 128:(j + 1) * 128],
                                                                 in_=BCf[:, bi, c * 128:(c + 1) * 128],
                                                                 identity=C.identB[:]), reads=[BCf, C.identB], writes=[ps], inc=(j == 3))
                            evac(k, c4, BT[:, c4 * 4:(c4 + 1) * 4, bi * 128:(bi + 1) * 128],
                                 psb[:, 0:512].rearrange("p (c f) -> p c f", c=4), [ps], [BT])
            dtf = sc.sb("sdtf", [32, T], F32)
            af = sc.sb("saf", [32, T], F32)
            sv = sc.sb("ssv", [32, 3], F32)
            k.dma(sv[:, 0:1], W["ssd_dt_bias"][l].rearrange("d (h o) -> (d h) o", o=1), writes=[sv])
            k.dma(sv[:, 1:2], W["ssd_a_log"][l].rearrange("d (h o) -> (d h) o", o=1), writes=[sv])
            k.op("dve", lambda e: e.memset(sv[:, 2:3], 1.0), writes=[sv])
            k.op("act", lambda e: e.activation(out=sv[:, 1:2], in_=sv[:, 1:2], func=AF.Exp), reads=[sv], writes=[sv])
            k.op("dve", lambda e: e.tensor_scalar(out=sv[:, 1:2], in0=sv[:, 1:2], scalar1=-1.0, scalar2=None, op0=ALU.mult),
                 reads=[sv], writes=[sv])
            k.dma(dtf[:], C.DT[:, :], reads=[C.u_PT[C_DT]], writes=[dtf])
            k.op("act", lambda e: e.activation(out=dtf[:], in_=dtf[:], func=AF.Exp, bias=sv[:, 0:1]), reads=[dtf, sv], writes=[dtf])
            k.op("act", lambda e: e.activation(out=dtf[:], in_=dtf[:], func=AF.Ln, bias=sv[:, 2:3]), reads=[dtf, sv], writes=[dtf])
            k.op("dve", lambda e: e.tensor_scalar(out=af[:], in0=dtf[:], scalar1=sv[:, 1:2], scalar2=None, op0=ALU.mult),
                 reads=[dtf, sv], writes=[af])
            for srcf, dstT in ((dtf, dtT), (af, aT)):
                ps = C.PS[0] if srcf is dtf else C.PS[1]
                for c in range(NCH):
                    k.op("pe", lambda e: e.transpose(out=ps[:, c * 32:(c + 1) * 32], in_=srcf[0:32, c * 128:(c + 1) * 128],
                                                     identity=C.identF[0:32, 0:32]), reads=[srcf, C.identF], writes=[ps],
                         inc=(c == NCH - 1))
                k.op("dve", lambda e: e.tensor_copy(out=dstT[:].rearrange("p c d -> p (c d)"), in_=ps[:]), reads=[ps], writes=[dstT])
            ps = C.PS[2]
            ps2 = C.PS[3]
            for c in range(NCH):
                k.op("pe", lambda e: e.matmul(ps[:, c * 32:c * 32 + 16], lhsT=C.triF[:], rhs=aT[:, c, 0:16], start=True, stop=True),
                     reads=[C.triF, aT], writes=[ps], inc=False)
                k.op("pe", lambda e: e.matmul(ps[:, c * 32 + 16:c * 32 + 32], lhsT=C.triB[:], rhs=aT[:, c, 16:32], start=True, stop=True),
                     reads=[C.triB, aT], writes=[ps], inc=False)
                k.op("pe", lambda e: e.matmul(ps2[:, c * 32:(c + 1) * 32], lhsT=C.onesF[:], rhs=aT[:, c, :], start=True, stop=True),
                     reads=[C.onesF, aT], writes=[ps2], inc=(c == NCH - 1))
            fl = lambda t_: t_[:].rearrange("p c d -> p (c d)")
            k.op("dve", lambda e: e.tensor_copy(out=fl(acs), in_=ps[:]), reads=[ps], writes=[acs])
            k.op("dve", lambda e: e.tensor_copy(out=fl(tot), in_=ps2[:]), reads=[ps2], writes=[tot])
            k.op("act", lambda e: e.activation(out=fl(etot), in_=fl(tot), func=AF.Exp), reads=[tot], writes=[etot])
            k.op("act", lambda e: e.activation(out=fl(eacs), in_=fl(acs), func=AF.Exp), reads=[acs], writes=[eacs])
            k.op("dve", lambda e: e.tensor_tensor(out=fl(Wst), in0=fl(tot), in1=fl(acs), op=ALU.subtract), reads=[tot, acs], writes=[Wst])
            k.op("act", lambda e: e.activation(out=fl(Wst), in_=fl(Wst), func=AF.Exp), reads=[Wst], writes=[Wst])
            k.op("dve", lambda e: e.tensor_tensor(out=fl(Wst), in0=fl(Wst), in1=fl(dtT), op=ALU.mult), reads=[Wst, dtT], writes=[Wst])
        with k.scope() as sc:
            nextS = sc.sb("snext", [128, NCH, 2, 512], BF16)
            st = [sc.sb("sst", [128, 2, 512], F32) for _ in range(2)]
            stb = sc.sb("sstb", [128, 2, 512], BF16)
            xw = [sc.sb("sxw", [128, 1024], BF16) for _ in range(2)]
            for d_ in range(2):
                k.op("pool", lambda e: e.memset(st[d_][:], 0.0), writes=[st[d_]])
            k.op("pool", lambda e: e.memset(nextS[:, NCH - 1, :, :], 0.0), writes=[nextS])

            def state_update(c, dirn, n):
                xw_ = xw[n % 2]
                for h in range(NH):
                    dh = dirn * 16 + h
                    if h % 2 == 0:
                        k.op("dve", lambda e: e.tensor_scalar(out=xw_[:, h * 64:(h + 1) * 64], in0=xT[:, c, h * 64:(h + 1) * 64],
                                                              scalar1=Wst[:, c, dh:dh + 1], scalar2=None, op0=ALU.mult),
                             reads=[xT, Wst], writes=[xw_])
                    else:
                        k.op("act", lambda e: e.activation(out=xw_[:, h * 64:(h + 1) * 64], in_=xT[:, c, h * 64:(h + 1) * 64],
                                                           func=AF.Copy, scale=Wst[:, c, dh:dh + 1]),
                             reads=[xT, Wst], writes=[xw_])
                for g in range(2):
                    ps = C.PS[6 + g]
                    k.op("pe", lambda e: e.matmul(ps[:], lhsT=BT[:, c, g * 128:(g + 1) * 128], rhs=xw_[:, g * 512:(g + 1) * 512],
                                                  start=True, stop=True), reads=[BT, xw_], writes=[ps])
                    for hh in range(8):
                        dh = dirn * 16 + g * 8 + hh
                        hs = slice(hh * 64, (hh + 1) * 64)
                        k.op("dve", lambda e: e.scalar_tensor_tensor(out=st[dirn][:, g, hs], in0=st[dirn][:, g, hs],
                                                                     scalar=etot[:, c, dh:dh + 1], op0=ALU.mult,
                                                                     in1=ps[:, hs], op1=ALU.add),
                             reads=[st[dirn], etot, ps], writes=[st[dirn]])
            n = 0
            for c in range(NCH - 1, 0, -1):
                state_update(c, 1, n)
                n += 1
                k.op("act", lambda e: e.activation(out=nextS[:, c - 1, :, :], in_=st[1][:], func=AF.Copy), reads=[st[1]], writes=[nextS])
            SC = [sc.sb("sSC", [128, 2, 128], F32) for _ in range(2)]
            NR = 6
            Rr = [sc.sb("sRr", [128, 128], F32) for _ in range(NR)]
            Dd = [sc.sb("sDd", [128, 128], F32) for _ in range(NR)]
            Mx = [sc.sb("sMx", [128, 128], F32) for _ in range(NR)]
            Mb = [sc.sb("sMb", [128, 128], BF16) for _ in range(4)]
            ytok = sc.sb("sytok", [128, 1024], F32)
            yTo = [sc.sb("syTo", [128, 512], F32) for _ in range(2)]
            nu = 0
            for c in range(NCH):
                cs = slice(c * 128, (c + 1) * 128)
                for g in range(2):
                    ps = C.PS[g]
                    k.op("pe", lambda e: e.matmul(ps[:, 0:128], lhsT=BCf[:, g, cs], rhs=BCf[:, 2 + g, cs], start=True, stop=True),
                         reads=[BCf], writes=[ps])
                    k.op("dve", lambda e: e.tensor_tensor(out=SC[0][:, g, :], in0=ps[:, 0:128], in1=C.triF[:], op=ALU.mult),
                         reads=[ps, C.triF], writes=[SC[0]])
                    k.op("dve", lambda e: e.tensor_tensor(out=SC[1][:, g, :], in0=ps[:, 0:128], in1=C.triB[:], op=ALU.mult),
                         reads=[ps, C.triB], writes=[SC[1]])
                ydg = [C.PS[6], C.PS[7]]

                def stage_a(h):
                    for dirn in range(2):
                        dh = dirn * 16 + h
                        tri = C.triF if dirn == 0 else C.triB
                        ui = (h * 2 + dirn) % NR
                        R = Rr[ui]
                        pq = C.PS[ui]
                        k.op("act", lambda e: e.activation(out=R[:], in_=tri[:], func=AF.Copy, scale=aT[:, c, dh:dh + 1]),
                             reads=[tri, aT], writes=[R])
                        k.op("pe", lambda e: e.matmul(pq[:, 0:128], lhsT=C.onesF[:], rhs=R[:], start=True, stop=True),
                             reads=[C.onesF, R], writes=[pq])

                def stage_b(h):
                    g = h // 8
                    mts = []
                    for dirn in range(2):
                        dh = dirn * 16 + h
                        ui = (h * 2 + dirn) % NR
                        Dm = Dd[ui]
                        M = Mx[ui]
                        pq = C.PS[ui]
                        k.op("dve", lambda e: e.tensor_scalar(out=Dm[:], in0=pq[:, 0:128], scalar1=acs[:, c, dh:dh + 1], scalar2=0.0,
                                                              op0=ALU.subtract, op1=ALU.min), reads=[pq, acs], writes=[Dm])
                        k.op("act", lambda e: e.activation(out=Dm[:], in_=Dm[:], func=AF.Exp), reads=[Dm], writes=[Dm])
                        k.op("dve", lambda e: e.scalar_tensor_tensor(out=M[:], in0=Dm[:], scalar=dtT[:, c, dh:dh + 1], op0=ALU.mult,
                                                                     in1=SC[dirn][:, g, :], op1=ALU.mult),
                             reads=[Dm, dtT, SC[dirn]], writes=[M])
                        mts.append(M)
                    mb = Mb[h % 4]
                    k.op("pool", lambda e: e.tensor_tensor(out=mb[:], in0=mts[0][:], in1=mts[1][:], op=ALU.add),
                         reads=mts, writes=[mb])
                    k.op("pe", lambda e: e.matmul(ydg[g][:, (h % 8) * 64:(h % 8 + 1) * 64], lhsT=mb[:], rhs=xT[:, c, h * 64:(h + 1) * 64],
                                                  start=True, stop=True), reads=[mb, xT], writes=[ydg[g]])
                LOOK = 2
                for h in range(min(LOOK, NH)):
                    stage_a(h)
                for h in range(NH):
                    if h + LOOK < NH:
                        stage_a(h + LOOK)
                    stage_b(h)
                k.op("act", lambda e: e.activation(out=stb[:], in_=st[0][:], func=AF.Copy), reads=[st[0]], writes=[stb])
                yoff = [[C.PS[2], C.PS[3]], [C.PS[4], C.PS[5]]]
                for g in range(2):
                    k.op("pe", lambda e: e.matmul(yoff[0][g][:], lhsT=BCf[:, 2 + g, cs], rhs=stb[:, g, :], start=True, stop=True),
                         reads=[BCf, stb], writes=[yoff[0][g]])
                    k.op("pe", lambda e: e.matmul(yoff[1][g][:], lhsT=BCf[:, 2 + g, cs], rhs=nextS[:, c, g, :], start=True, stop=True),
                         reads=[BCf, nextS], writes=[yoff[1][g]])
                for g in range(2):
                    k.op("act", lambda e: e.activation(out=ytok[:, g * 512:(g + 1) * 512], in_=ydg[g][:], func=AF.Copy),
                         reads=[ydg[g]], writes=[ytok])
                for h in range(NH):
                    g = h // 8
                    hs = slice((h % 8) * 64, (h % 8 + 1) * 64)
                    ys = ytok[:, h * 64:(h + 1) * 64]
                    k.op("dve", lambda e: e.scalar_tensor_tensor(out=ys, in0=yoff[0][g][:, hs], scalar=eacs[:, c, h:h + 1],
                                                                 op0=ALU.mult, in1=ys, op1=ALU.add),
                         reads=[yoff[0][g], eacs, ytok], writes=[ytok])
                    k.op("dve", lambda e: e.scalar_tensor_tensor(out=ys, in0=yoff[1][g][:, hs], scalar=eacs[:, c, 16 + h:17 + h],
                                                                 op0=ALU.mult, in1=ys, op1=ALU.add),
                         reads=[yoff[1][g], eacs, ytok], writes=[ytok])
                    k.op("dve", lambda e: e.scalar_tensor_tensor(out=ys, in0=xT[:, c, h * 64:(h + 1) * 64], scalar=dvec[:, h:h + 1],
                                                                 op0=ALU.mult, in1=ys, op1=ALU.add),
                         reads=[xT, dvec, ytok], writes=[ytok])
                for half in range(2):
                    ps = C.PS[half]
                    for j in range(4):
                        ft = half * 4 + j
                        k.op("pe", lambda e: e.transpose(out=ps[:, j * 128:(j + 1) * 128], in_=ytok[:, ft * 128:(ft + 1) * 128],
                                                         identity=C.identF[:]), reads=[ytok, C.identF], writes=[ps], inc=(j == 3))
                    o = yTo[half]
                    evac(k, half, o[:], ps[:], [ps], [o])
                    for j in range(4):
                        ft = half * 4 + j
                        k.dma(C.YR[ft * 128:(ft + 1) * 128, cs], o[:, j * 128:(j + 1) * 128], reads=[o], writes=[C.u_YR[ft]])
                if c < NCH - 1:
                    state_update(c, 0, n)
                    n += 1
    with k.scope() as sc:
        nw = sc.sb("snw", [128, 1, 8], F32)
        load_featvecs(k, C, nw, [W["ssd_norm"][l]], 8)
        ys_ = [sc.sb("sy", [128, 8, 512], F32) for _ in range(2)]
        zs_ = [sc.sb("sz", [128, 8, 512], BF16) for _ in range(2)]
        zg = [sc.sb("szg", [128, 8, 512], F32) for _ in range(2)]
        sq = [sc.sb("ssq", [128, 8, 512], BF16) for _ in range(2)]
        rs = [sc.sb("srs", [128, 512], F32) for _ in range(2)]
        yo = [sc.sb("syo", [128, 512], BF16) for _ in range(2)]
        ne = 0
        for ct in range(4):
            cs = slice(ct * 512, (ct + 1) * 512)
            y = ys_[ct % 2]
            z = zs_[ct % 2]
            zz = zg[ct % 2]
            k.dma(y[:], C.YR[:, cs].rearrange("(n p) t -> p n t", p=128), reads=C.u_YR, writes=[y])
            for j in range(8):
                c0 = C_Z + j * 128
                k.dma(z[:, j, :], C.PT[c0:c0 + 128, cs], reads=[C.u_PT[c0]], writes=[z])
            k.op("act", lambda e: e.activation(out=zz[:], in_=z[:], func=AF.Silu), reads=[z], writes=[zz])
            k.op("pool", lambda e: e.tensor_tensor(out=y[:], in0=y[:], in1=zz[:], op=ALU.mult), reads=[y, zz], writes=[y])
            r = rstd_cols(k, C, (sq, rs), ct, y)
            for j in range(8):
                o = yo[ne % 2]
                ne += 1
                k.op("dve", lambda e: e.scalar_tensor_tensor(out=o[:], in0=y[:, j, :], scalar=nw[:, 0, j:j + 1], op0=ALU.mult,
                                                             in1=r[:], op1=ALU.mult), reads=[y, nw, r], writes=[o])
                k.dma(C.YT[1024 + j * 128:1024 + (j + 1) * 128, cs], o[:], reads=[o], writes=[C.u_YT[8 + j]])


S5_BLK = 64
S5_NB = T // S5_BLK
HALF_PI = 0.5 * math.pi


def s5_context(k, C):
    kind = "ExternalOutput" if C.dbg else "Internal"
    C.S5BS = k.dram("S5BS", [DEPTH, 128, 64, 128], BF16, kind=kind)
    C.S5CF = k.dram("S5CF", [DEPTH, 128, 64, 128], BF16, kind=kind)
    C.S5AB = k.dram("S5AB", [DEPTH, 128, 2, 64], F32, kind=kind)
    C.u_S5 = [Unit("S5tab%d" % l) for l in range(DEPTH)]


def reduce_angle(k, sc, dst, src, shift, shape):
    t = sc.sb("rat", shape, F32)
    y = sc.sb("ray", shape, F32)
    k.op("dve", lambda e: e.tensor_scalar(out=y[:], in0=src[:], scalar1=shift, scalar2=None, op0=ALU.add), reads=[src], writes=[y])
    k.op("dve", lambda e: e.tensor_scalar(out=t[:], in0=y[:], scalar1=1.0 / TWO_PI, scalar2=MAGIC, op0=ALU.mult, op1=ALU.add),
         reads=[y], writes=[t])
    k.op("dve", lambda e: e.tensor_scalar(out=t[:], in0=t[:], scalar1=-MAGIC, scalar2=None, op0=ALU.add), reads=[t], writes=[t])
    k.op("dve", lambda e: e.scalar_tensor_tensor(out=dst[:], in0=t[:], scalar=-TWO_PI, op0=ALU.mult, in1=y[:], op1=ALU.add),
         reads=[t, y], writes=[dst])


def stage_s5_tables(k, C, l):
    W = C.W
    tt = lambda eng, out, a, b, op, rd, wr: k.op(eng, lambda e: e.tensor_tensor(out=out, in0=a, in1=b, op=op), reads=rd, writes=wr)
    with k.scope() as sc:
        A = sc.sb("s5A", [128, 2, 32], F32)
        load_featvecs(k, C, A, [W["s5_a_re"][l].rearrange("d g n -> (d g n)"),
                                W["s5_a_im"][l].rearrange("d g n -> (d g n)")], 32)
        ld = sc.sb("s5ld", [1, 64], F32)
        k.dma(ld[:], W["s5_log_dt"][l].rearrange("d (g o) -> o (d g)", o=1), writes=[ld])
        ps = C.PS[0]
        k.op("pe", lambda e: e.matmul(ps[:, 0:64], lhsT=C.onesF[0:1, :], rhs=ld[0:1, :], start=True, stop=True),
             reads=[C.onesF, ld], writes=[ps])
        dt = sc.sb("s5dt", [128, 32], F32)
        psv = ps[:, 0:64].rearrange("p (d P g) -> p d P g", d=2, g=2)
        for g2 in range(2):
            k.op("dve", lambda e: e.tensor_copy(out=dt[g2 * 64:(g2 + 1) * 64, :].rearrange("p (d P) -> p d P", d=2),
                                                in_=psv[g2 * 64:(g2 + 1) * 64, :, :, g2]), reads=[ps], writes=[dt])
        k.op("act", lambda e: e.activation(out=dt[:], in_=dt[:], func=AF.Exp), reads=[dt], writes=[dt])
        mag = sc.sb("s5mag", [128, 32], F32)
        ang = sc.sb("s5ang", [128, 32], F32)
        tt("dve", mag[:], A[:, 0, :], dt[:], ALU.mult, [A, dt], [mag])
        k.op("act", lambda e: e.activation(out=mag[:], in_=mag[:], func=AF.Exp), reads=[mag], writes=[mag])
        tt("dve", ang[:], A[:, 1, :], dt[:], ALU.mult, [A, dt], [ang])
        sn = sc.sb("s5sn", [128, 32], F32)
        cs_ = sc.sb("s5cs", [128, 32], F32)
        reduce_angle(k, sc, sn, ang, 0.0, [128, 32])
        reduce_angle(k, sc, cs_, ang, HALF_PI, [128, 32])
        k.op("act", lambda e: e.activation(out=sn[:], in_=sn[:], func=AF.Sin), reads=[sn], writes=[sn])
        k.op("act", lambda e: e.activation(out=cs_[:], in_=cs_[:], func=AF.Sin), reads=[cs_], writes=[cs_])
        abr = sc.sb("s5abr", [128, 32], F32)
        abi = sc.sb("s5abi", [128, 32], F32)
        tt("dve", abr[:], mag[:], cs_[:], ALU.mult, [mag, cs_], [abr])
        tt("dve", abi[:], mag[:], sn[:], ALU.mult, [mag, sn], [abi])
        AB = sc.sb("s5AB", [128, 2, 64], F32)
        k.op("dve", lambda e: e.tensor_copy(out=AB[:, 0, 0:32], in_=abr[:]), reads=[abr], writes=[AB])
        k.op("dve", lambda e: e.tensor_copy(out=AB[:, 0, 32:64], in_=abr[:]), reads=[abr], writes=[AB])
        k.op("dve", lambda e: e.tensor_scalar(out=AB[:, 1, 0:32], in0=abi[:], scalar1=-1.0, scalar2=None, op0=ALU.mult),
             reads=[abi], writes=[AB])
        k.op("dve", lambda e: e.tensor_copy(out=AB[:, 1, 32:64], in_=abi[:]), reads=[abi], writes=[AB])
        k.dma(C.S5AB[l], AB[:], reads=[AB], writes=[C.u_S5[l]])
        den = sc.sb("s5den", [128, 32], F32)
        t1 = sc.sb("s5t1", [128, 32], F32)
        t2 = sc.sb("s5t2", [128, 32], F32)
        nr = sc.sb("s5nr", [128, 32], F32)
        fr = sc.sb("s5fr", [128, 32, 1], F32)
        fi = sc.sb("s5fi", [128, 32, 1], F32)
        tt("dve", den[:], A[:, 0, :], A[:, 0, :], ALU.mult, [A], [den])
        tt("dve", t1[:], A[:, 1, :], A[:, 1, :], ALU.mult, [A], [t1])
        tt("dve", den[:], den[:], t1[:], ALU.add, [den, t1], [den])
        k.op("dve", lambda e: e.reciprocal(out=den[:], in_=den[:]), reads=[den], writes=[den])
        k.op("dve", lambda e: e.tensor_scalar(out=nr[:], in0=abr[:], scalar1=-1.0, scalar2=None, op0=ALU.add), reads=[abr], writes=[nr])
        tt("dve", t1[:], nr[:], A[:, 0, :], ALU.mult, [nr, A], [t1])
        tt("dve", t2[:], abi[:], A[:, 1, :], ALU.mult, [abi, A], [t2])
        tt("dve", t1[:], t1[:], t2[:], ALU.add, [t1, t2], [t1])
        tt("dve", fr[:, :, 0], t1[:], den[:], ALU.mult, [t1, den], [fr])
        tt("dve", t1[:], abi[:], A[:, 0, :], ALU.mult, [abi, A], [t1])
        tt("dve", t2[:], nr[:], A[:, 1, :], ALU.mult, [nr, A], [t2])
        tt("dve", t1[:], t1[:], t2[:], ALU.subtract, [t1, t2], [t1])
        tt("dve", fi[:, :, 0], t1[:], den[:], ALU.mult, [t1, den], [fi])
        Bre = sc.sb("s5Bre", [128, 32, 16], F32)
        Bim = sc.sb("s5Bim", [128, 32, 16], F32)
        k.dma(Bre[:], W["s5_b_re"][l].rearrange("d (P g2) n c -> (g2 n) (d P) c", g2=2), writes=[Bre])
        k.dma(Bim[:], W["s5_b_im"][l].rearrange("d (P g2) n c -> (g2 n) (d P) c", g2=2), writes=[Bim])
        frb = fr[:].to_broadcast([128, 32, 16])
        fib = fi[:].to_broadcast([128, 32, 16])
        u1 = sc.sb("s5u1", [128, 32, 16], F32)
        u2 = sc.sb("s5u2", [128, 32, 16], F32)
        E = [sc.sb("s5E", [128, 32, 32], BF16) for _ in range(2)]
        for e_ in E:
            k.op("pool", lambda e: e.memset(e_[:], 0.0), writes=[e_])
        for reim in range(2):
            if reim == 0:
                tt("dve", u1[:], Bre[:], frb, ALU.mult, [Bre, fr], [u1])
                tt("dve", u2[:], Bim[:], fib, ALU.mult, [Bim, fi], [u2])
                tt("dve", u1[:], u1[:], u2[:], ALU.subtract, [u1, u2], [u1])
            else:
                tt("dve", u1[:], Bim[:], frb, ALU.mult, [Bim, fr], [u1])
                tt("dve", u2[:], Bre[:], fib, ALU.mult, [Bre, fi], [u2])
                tt("dve", u1[:], u1[:], u2[:], ALU.add, [u1, u2], [u1])
            for g2 in range(2):
                pr = slice(g2 * 64, (g2 + 1) * 64)
                k.op("dve", lambda e: e.tensor_copy(out=E[reim][pr, :, g2 * 16:(g2 + 1) * 16], in_=u1[pr, :, :]),
                     reads=[u1], writes=[E[reim]])
        BsT = sc.sb("s5BsT", [128, 64, 128], BF16)
        Ez = [sc.sb("s5Ez", [128, 128], BF16) for _ in range(4)]
        for ez in Ez:
            k.op("pool", lambda e: e.memset(ez[:], 0.0), writes=[ez])
        n = 0
        for F in range(4):
            for dirn in range(2):
                for reim in range(2):
                    idx = (F * 2 + dirn) * 2 + reim
                    j0 = dirn * 16 + 4 * F
                    for a in range(4):
                        ez = Ez[a]
                        k.op("dve", lambda e: e.tensor_copy(out=ez[:, a * 32:(a + 1) * 32], in_=E[reim][:, j0 + a, :]),
                             reads=[E[reim]], writes=[ez])
                        ps = C.PS[1 + n % 4]
                        psb = ps[:].bitcast(BF16)
                        k.op("pe", lambda e: e.transpose(out=psb[:, 0:128], in_=ez[:], identity=C.identB[:]),
                             reads=[ez, C.identB], writes=[ps])
                        evac(k, n, BsT[:, idx * 4 + a, :], psb[:, 0:128], [ps], [BsT])
                        n += 1
        k.dma(C.S5BS[l], BsT[:], reads=[BsT], writes=[C.u_S5[l]])
        CfT = sc.sb("s5CfT", [128, 64, 128], BF16)
        k.op("pool", lambda e: e.memset(CfT[:], 0.0), writes=[CfT])
        cin = [sc.sb("s5cin", [128, 2, 64], F32) for _ in range(2)]
        CT = sc.sb("s5CT", [128, 128], F32)
        for reim, wn in enumerate(("s5_c_re", "s5_c_im")):
            for dirn in range(2):
                for h8 in range(2):
                    ci_ = cin[(dirn * 2 + h8) % 2]
                    for pp in range(8):
                        P = h8 * 8 + pp
                        k.dma(ci_[pp * 16:(pp + 1) * 16, :, :],
                              W[wn][l, dirn, 2 * P:2 * P + 2].rearrange("g c n -> c g n"), writes=[ci_])
                    ps = C.PS[5 + (dirn * 2 + h8) % 2]
                    k.op("pe", lambda e: e.transpose(out=ps[:, 0:128], in_=ci_[:].rearrange("p g n -> p (g n)"), identity=C.identF[:]),
                         reads=[ci_, C.identF], writes=[ps])
                    k.op("dve", lambda e: e.tensor_copy(out=CT[:], in_=ps[:, 0:128]), reads=[ps], writes=[CT])
                    for pp in range(8):
                        P = h8 * 8 + pp
                        F, a = P // 4, P % 4
                        idx = ((F * 2 + dirn) * 2 + reim) * 4 + a
                        for g2 in range(2):
                            pr = slice(g2 * 64, (g2 + 1) * 64)
                            k.op("dve", lambda e: e.tensor_scalar(out=CfT[pr, idx, a * 32 + g2 * 16:a * 32 + (g2 + 1) * 16],
                                                                  in0=CT[pr, pp * 16:(pp + 1) * 16],
                                                                  scalar1=(1.0 if reim == 0 else -1.0), scalar2=None, op0=ALU.mult),
                                 reads=[CT], writes=[CfT])
        k.dma(C.S5CF[l], CfT[:], reads=[CfT], writes=[C.u_S5[l]])


def stage_s5(k, C, l):
    W = C.W
    BLK, NB = S5_BLK, S5_NB
    with k.scope() as sc0:
        u = sc0.sb("s5u", [128, 4, T], BF16)
        yfb = [sc0.sb("s5y", [128, 4, T], BF16) for _ in range(2)]
        for F in range(4):
            c0 = C_S5 + F * 128
            k.dma(u[:, F, :], C.PT[c0:c0 + 128, :], reads=[C.u_PT[c0]], writes=[u])
        with k.scope() as sc:
            BsT = sc.sb("s5BsT", [128, 64, 128], BF16)
            CfT = sc.sb("s5CfT", [128, 64, 128], BF16)
            AB = sc.sb("s5AB", [128, 2, 64], F32)
            k.dma(BsT[:], C.S5BS[l], reads=[C.u_S5[l]], writes=[BsT])
            k.dma(CfT[:], C.S5CF[l], reads=[C.u_S5[l]], writes=[CfT])
            k.dma(AB[:], C.S5AB[l], reads=[C.u_S5[l]], writes=[AB])
            S = [sc.sb("s5S", [128, BLK, 64], F32) for _ in range(2)]
            Hf = sc.sb("s5Hf", [128, BLK, 64], F32)
            Hb = [sc.sb("s5Hb", [128, BLK, 64], BF16) for _ in range(2)]
            Hc = sc.sb("s5Hc", [128, 64], F32)
            m1 = sc.sb("s5m1", [128, 64], F32)
            m2 = sc.sb("s5m2", [128, 64], F32)
            k.op("dve", lambda e: e.memset(Hc[:], 0.0), writes=[Hc])
            nps = 0
            for b in range(NB):
                tf0 = b * BLK
                tb0 = T - (b + 1) * BLK
                Sb = S[b % 2]
                for F in range(4):
                    for dirn in range(2):
                        t0 = tf0 if dirn == 0 else tb0
                        ps = C.PS[nps % 4]
                        nps += 1
                        for reim in range(2):
                            idx = (F * 2 + dirn) * 2 + reim
                            for a in range(4):
                                reg = (reim * 4 + a) * BLK
                                k.op("pe", lambda e: e.matmul(ps[:, reg:reg + BLK], lhsT=BsT[:, idx * 4 + a, :],
                                                              rhs=u[:, F, t0:t0 + BLK], start=True, stop=True),
                                     reads=[BsT, u], writes=[ps], inc=(reim == 1 and a == 3))
                        for reim in range(2):
                            col0 = (reim * 2 + dirn) * 16 + 4 * F
                            src = ps[:, reim * 4 * BLK:(reim * 4 + 4) * BLK].rearrange("p (a t) -> p a t", a=4)
                            if dirn == 1:
                                src = src[:, :, ::-1]
                            k.op("act", lambda e: e.activation(out=Sb[:, :, col0:col0 + 4].rearrange("p t a -> p a t"),
                                                               in_=src, func=AF.Copy), reads=[ps], writes=[Sb])
                for i in range(BLK):
                    prev = Hc[:, :] if i == 0 else Hf[:, i - 1, :]
                    pu = Hc if i == 0 else Hf
                    prev_sw = prev.rearrange("p (r c) -> p r c", r=2)[:, ::-1, :]
                    k.op("dve", lambda e: e.tensor_tensor(out=m1[:], in0=prev, in1=AB[:, 0, :], op=ALU.mult), reads=[pu, AB], writes=[m1])
                    k.op("dve", lambda e: e.tensor_tensor(out=m2[:].rearrange("p (r c) -> p r c", r=2), in0=prev_sw,
                                                          in1=AB[:, 1, :].rearrange("p (r c) -> p r c", r=2), op=ALU.mult),
                         reads=[pu, AB], writes=[m2])
                    k.op("dve", lambda e: e.tensor_tensor(out=m1[:], in0=m1[:], in1=m2[:], op=ALU.add), reads=[m1, m2], writes=[m1])
                    k.op("dve", lambda e: e.tensor_tensor(out=Hf[:, i, :], in0=m1[:], in1=Sb[:, i, :], op=ALU.add),
                         reads=[m1, Sb], writes=[Hf])
                k.op("dve", lambda e: e.tensor_copy(out=Hc[:], in_=Hf[:, BLK - 1, :]), reads=[Hf], writes=[Hc])
                hb = Hb[b % 2]
                k.op("act", lambda e: e.activation(out=hb[:], in_=Hf[:], func=AF.Copy), reads=[Hf], writes=[hb])
                for dirn in range(2):
                    t0 = tf0 if dirn == 0 else tb0
                    for F in range(4):
                        ps = C.PS[4 + nps % 4]
                        nps += 1
                        nmm = 0
                        for reim in range(2):
                            for a in range(4):
                                idx = ((F * 2 + dirn) * 2 + reim) * 4 + a
                                col = (reim * 2 + dirn) * 16 + 4 * F + a
                                k.op("pe", lambda e: e.matmul(ps[:, 0:BLK], lhsT=CfT[:, idx, :], rhs=hb[:, :, col],
                                                              start=(nmm == 0), stop=(nmm == 7)),
                                     reads=[CfT, hb], writes=[ps], inc=(nmm == 7))
                                nmm += 1
                        src = ps[:, 0:BLK]
                        if dirn == 1:
                            src = src[:, ::-1]
                        k.op("act", lambda e: e.activation(out=yfb[dirn][:, F, t0:t0 + BLK], in_=src, func=AF.Copy),
                             reads=[ps], writes=[yfb[dirn]])
        with k.scope() as sc:
            dv = sc.sb("s5dv", [128, 2, 4], F32)
            load_featvecs(k, C, dv, [W["s5_d"][l], W["s5_b_glu"][l]], 4)
            gB = sc.sb("s5gB", [128, 4, T], BF16)
            ysum = [sc.sb("s5ys", [128, T], F32) for _ in range(2)]
            for F in range(4):
                ys = ysum[F % 2]
                k.op("pool", lambda e: e.tensor_tensor(out=ys[:], in0=yfb[0][:, F, :], in1=yfb[1][:, F, :], op=ALU.add),
                     reads=yfb, writes=[ys])
                k.op("dve", lambda e: e.scalar_tensor_tensor(out=ys[:], in0=u[:, F, :], scalar=dv[:, 0, F:F + 1], op0=ALU.mult,
                                                             in1=ys[:], op1=ALU.add), reads=[u, dv, ys], writes=[ys])
                k.op("act", lambda e: e.activation(out=gB[:, F, :], in_=ys[:], func=AF.Gelu), reads=[ys], writes=[gB])
            ws = WStream(k, sc, C, [("s5_w_glu", l, m * 128, 128, 4) for m in range(4)], 4, nbuf=2, name="wglu")
            sg = [sc.sb("s5sg", [128, T], F32) for _ in range(2)]
            yo = [sc.sb("s5yo", [128, T], BF16) for _ in range(2)]
            for m in range(4):
                w = ws.get(m)
                banks = C.PS[(m % 2) * 4:(m % 2) * 4 + 4]
                for ct in range(4):
                    for kt in range(4):
                        k.op("pe", lambda e: e.matmul(banks[ct][:], lhsT=w[:, kt, :], rhs=gB[:, kt, ct * 512:(ct + 1) * 512],
                                                      start=(kt == 0), stop=(kt == 3)), reads=[w, gB], writes=[banks[ct]], inc=(kt == 3))
                s_ = sg[m % 2]
                o = yo[m % 2]
                for ct in range(4):
                    k.op("act", lambda e: e.activation(out=s_[:, ct * 512:(ct + 1) * 512], in_=banks[ct][:], func=AF.Sigmoid,
                                                       bias=dv[:, 1, m:m + 1]), reads=[banks[ct], dv], writes=[s_])
                k.op("pool", lambda e: e.tensor_tensor(out=o[:], in0=s_[:], in1=gB[:, m, :], op=ALU.mult), reads=[s_, gB], writes=[o])
                k.dma(C.YT[512 + m * 128:512 + (m + 1) * 128, :], o[:], reads=[o], writes=[C.u_YT[4 + m]])


def all_consts():
    c = {"c_ident": np.eye(128, dtype=np.float32)}
    c.update(hyena_consts())
    c.update(ssd_consts())
    return c


_PROG = {}


def kernel(**inputs):
    nseq = NSEQ
    if "prog" not in _PROG:
        _PROG["prog"] = build_program(nseq=nseq)
        _PROG["consts"] = all_consts()
    k, C = _PROG["prog"]
    consts = _PROG["consts"]
    x = np.ascontiguousarray(np.asarray(inputs["x"], dtype=np.float32))
    in_maps = []
    for c in range(NCORE):
        m = {"x": x[c * nseq:(c + 1) * nseq]}
        for n in C.W:
            m[n] = np.ascontiguousarray(np.asarray(inputs[n], dtype=np.float32))
        m.update(consts)
        in_maps.append(m)
    res = run_bass_kernel_spmd(k.nc, in_maps, core_ids=list(range(NCORE)))
    out = np.concatenate([np.asarray(r["out"], dtype=np.float32) for r in res.results], axis=0)
    return out


def stage_s5_batched(k, C, l, nseq):
    W = C.W
    BLK, GRP = 16, 16
    NB = T // BLK
    NG = NB // GRP
    GT = BLK * GRP
    NSQ = nseq
    CW = NSQ * 64
    with k.scope() as sc:
        BsT = sc.sb("s5BsT", [128, 64, 128], BF16)
        CfT = sc.sb("s5CfT", [128, 64, 128], BF16)
        AB = sc.sb("s5AB", [128, 2, 64], F32)
        k.dma(BsT[:], C.S5BS[l], reads=[C.u_S5[l]], writes=[BsT])
        k.dma(CfT[:], C.S5CF[l], reads=[C.u_S5[l]], writes=[CfT])
        k.dma(AB[:], C.S5AB[l], reads=[C.u_S5[l]], writes=[AB])
        if NSQ >= 4:
            chain_seqs = [list(range(NSQ - 1)), [NSQ - 1]]
        elif NSQ >= 2:
            chain_seqs = [list(range(NSQ - 1)), [NSQ - 1]]
        else:
            chain_seqs = [[0]]
        ceng = ["dve", "pool"]
        NCHN = len(chain_seqs)
        q2c = {}
        for h_, lst in enumerate(chain_seqs):
            for j_, q in enumerate(lst):
                q2c[q] = (h_, j_)
        QCs = [len(lst) for lst in chain_seqs]
        ABw = [sc.sb("s5ABw", [128, 2, QCs[h_] * 64], F32) for h_ in range(NCHN)]
        for h_ in range(NCHN):
            for ab in range(2):
                for q in range(QCs[h_]):
                    k.op("dve", lambda e: e.tensor_copy(out=ABw[h_][:, ab, q * 64:(q + 1) * 64], in_=AB[:, ab, :]),
                         reads=[AB], writes=[ABw[h_]])
        ug = sc.sb("s5ug", [128, NSQ, 2, 4, GT], BF16)
        yst = sc.sb("s5yst", [128, NSQ, 2, 4, GT], BF16)
        S = [[sc.sb("s5S", [128, BLK, QCs[h_] * 64], F32) for h_ in range(NCHN)] for _ in range(2)]
        Hf = [sc.sb("s5Hf", [128, BLK, QCs[h_] * 64], F32) for h_ in range(NCHN)]
        Hb = [sc.sb("s5Hb", [128, BLK, QCs[h_] * 64], BF16) for h_ in range(NCHN)]
        Hc = [sc.sb("s5Hc", [128, QCs[h_] * 64], F32) for h_ in range(NCHN)]
        m1 = [sc.sb("s5m1", [128, QCs[h_] * 64], F32) for h_ in range(NCHN)]
        m2 = [sc.sb("s5m2", [128, QCs[h_] * 64], F32) for h_ in range(NCHN)]
        for h_ in range(NCHN):
            k.op("dve", lambda e: e.memset(Hc[h_][:], 0.0), writes=[Hc[h_]])
        cnt = {"s": 0, "o": 0}

        def trange(G, dirn):
            return (G * GT) if dirn == 0 else (T - (G + 1) * GT)

        def load_group(G):
            for q in range(NSQ):
                for dirn in range(2):
                    t0 = trange(G, dirn)
                    k.dma(ug[:, q, dirn, :, :],
                          C.PT_all[q][C_S5:C_S5 + 512, t0:t0 + GT].rearrange("(f p) t -> p f t", p=128),
                          reads=[C.u_PT_all[q][C_S5 + F * 128] for F in range(4)], writes=[ug])

        def store_group(G):
            for q in range(NSQ):
                for dirn in range(2):
                    t0 = trange(G, dirn)
                    k.dma(C.YS5[q, dirn, :, t0:t0 + GT].rearrange("(f p) t -> p f t", p=128), yst[:, q, dirn, :, :],
                          reads=[yst], writes=[C.u_YS5[q]])

        def local_cols(j, dirn):
            return (j * BLK) if dirn == 0 else (GT - (j + 1) * BLK)

        def summ(b):
            j = b % GRP
            Sb = S[b % 2]
            for q in range(NSQ):
                for F in range(4):
                    for dirn in range(2):
                        c0 = local_cols(j, dirn)
                        ps = C.PS[cnt["s"] % 4]
                        cnt["s"] += 1
                        for reim in range(2):
                            idx = (F * 2 + dirn) * 2 + reim
                            for a in range(4):
                                reg = (reim * 4 + a) * BLK
                                k.op("pe", lambda e: e.matmul(ps[:, reg:reg + BLK], lhsT=BsT[:, idx * 4 + a, :],
                                                              rhs=ug[:, q, dirn, F, c0:c0 + BLK], start=True, stop=True),
                                     reads=[BsT, ug], writes=[ps], inc=(reim == 1 and a == 3))
                        for reim in range(2):
                            col0 = q2c[q][1] * 64 + (reim * 2 + dirn) * 16 + 4 * F
                            Sq = Sb[q2c[q][0]]
                            src = ps[:, reim * 4 * BLK:(reim * 4 + 4) * BLK].rearrange("p (a t) -> p a t", a=4)
                            if dirn == 1:
                                src = src[:, :, ::-1]
                            k.op("act", lambda e: e.activation(out=Sq[:, :, col0:col0 + 4].rearrange("p t a -> p a t"),
                                                               in_=src, func=AF.Copy), reads=[ps], writes=[Sq])

        def scan(b):
            Sb = S[b % 2]
            v4 = lambda ap: ap.rearrange("p (q r c) -> p q r c", r=2, c=32)
            hs = list(range(NCHN))
            for i in range(BLK):
                prevs = [(Hc[h_][:, :], Hc[h_]) if i == 0 else (Hf[h_][:, i - 1, :], Hf[h_]) for h_ in hs]
                for h_ in hs:
                    prev, pu = prevs[h_]
                    k.op(ceng[h_], lambda e: e.tensor_tensor(out=m1[h_][:], in0=prev, in1=ABw[h_][:, 0, :], op=ALU.mult),
                         reads=[pu, ABw[h_]], writes=[m1[h_]])
                for h_ in hs:
                    prev, pu = prevs[h_]
                    k.op(ceng[h_], lambda e: e.tensor_tensor(out=v4(m2[h_][:]), in0=v4(prev)[:, :, ::-1, :], in1=v4(ABw[h_][:, 1, :]),
                                                          op=ALU.mult), reads=[pu, ABw[h_]], writes=[m2[h_]])
                for h_ in hs:
                    k.op(ceng[h_], lambda e: e.tensor_tensor(out=m1[h_][:], in0=m1[h_][:], in1=m2[h_][:], op=ALU.add),
                         reads=[m1[h_], m2[h_]], writes=[m1[h_]])
                for h_ in hs:
                    k.op(ceng[h_], lambda e: e.tensor_tensor(out=Hf[h_][:, i, :], in0=m1[h_][:], in1=Sb[h_][:, i, :], op=ALU.add),
                         reads=[m1[h_], Sb[h_]], writes=[Hf[h_]])
            for h_ in hs:
                k.op(ceng[h_], lambda e: e.tensor_copy(out=Hc[h_][:], in_=Hf[h_][:, BLK - 1, :]), reads=[Hf[h_]], writes=[Hc[h_]])
                k.op("act", lambda e: e.activation(out=Hb[h_][:], in_=Hf[h_][:], func=AF.Copy), reads=[Hf[h_]], writes=[Hb[h_]])

        def outs(b):
            j = b % GRP
            for q in range(NSQ):
                for dirn in range(2):
                    c0 = local_cols(j, dirn)
                    for F in range(4):
                        ps = C.PS[4 + cnt["o"] % 4]
                        cnt["o"] += 1
                        nmm = 0
                        for reim in range(2):
                            for a in range(4):
                                idx = ((F * 2 + dirn) * 2 + reim) * 4 + a
                                col = q2c[q][1] * 64 + (reim * 2 + dirn) * 16 + 4 * F + a
                                hbq = Hb[q2c[q][0]]
                                k.op("pe", lambda e: e.matmul(ps[:, 0:BLK], lhsT=CfT[:, idx, :], rhs=hbq[:, :, col],
                                                              start=(nmm == 0), stop=(nmm == 7)),
                                     reads=[CfT, hbq], writes=[ps], inc=(nmm == 7))
                                nmm += 1
                        src = ps[:, 0:BLK]
                        if dirn == 1:
                            src = src[:, ::-1]
                        k.op("act", lambda e: e.activation(out=yst[:, q, dirn, F, c0:c0 + BLK], in_=src, func=AF.Copy),
                             reads=[ps], writes=[yst])

        for G in range(NG):
            load_group(G)
            for j in range(GRP):
                b = G * GRP + j
                if j == 0:
                    summ(b)
                if j + 1 < GRP:
                    summ(b + 1)
                scan(b)
                outs(b)
            store_group(G)
    for q in range(NSQ):
        C.set_seq(q)
        s5_epilogue(k, C, l, q)


def s5_epilogue(k, C, l, q):
    W = C.W
    with k.scope() as sc:
        dv = sc.sb("s5dv", [128, 2, 4], F32)
        load_featvecs(k, C, dv, [W["s5_d"][l], W["s5_b_glu"][l]], 4)
        u = sc.sb("s5u", [128, 4, T], BF16)
        yfb = [sc.sb("s5y", [128, 4, T], BF16) for _ in range(2)]
        for F in range(4):
            c0 = C_S5 + F * 128
            k.dma(u[:, F, :], C.PT[c0:c0 + 128, :], reads=[C.u_PT[c0]], writes=[u])
            for d_ in range(2):
                k.dma(yfb[d_][:, F, :], C.YS5[q, d_, F * 128:(F + 1) * 128, :], reads=[C.u_YS5[q]], writes=[yfb[d_]])
        gB = sc.sb("s5gB", [128, 4, T], BF16)
        ysum = [sc.sb("s5ys", [128, T], F32) for _ in range(2)]
        for F in range(4):
            ys = ysum[F % 2]
            k.op("pool", lambda e: e.tensor_tensor(out=ys[:], in0=yfb[0][:, F, :], in1=yfb[1][:, F, :], op=ALU.add),
                 reads=yfb, writes=[ys])
            k.op("dve", lambda e: e.scalar_tensor_tensor(out=ys[:], in0=u[:, F, :], scalar=dv[:, 0, F:F + 1], op0=ALU.mult,
                                                         in1=ys[:], op1=ALU.add), reads=[u, dv, ys], writes=[ys])
            k.op("act", lambda e: e.activation(out=gB[:, F, :], in_=ys[:], func=AF.Gelu), reads=[ys], writes=[gB])
        ws = WStream(k, sc, C, [("s5_w_glu", l, m * 128, 128, 4) for m in range(4)], 4, nbuf=2, name="wglu")
        sg = [sc.sb("s5sg", [128, T], F32) for _ in range(2)]
        yo = [sc.sb("s5yo", [128, T], BF16) for _ in range(2)]
        for m in range(4):
            w = ws.get(m)
            banks = C.PS[(m % 2) * 4:(m % 2) * 4 + 4]
            for ct in range(4):
                for kt in range(4):
                    k.op("pe", lambda e: e.matmul(banks[ct][:], lhsT=w[:, kt, :], rhs=gB[:, kt, ct * 512:(ct + 1) * 512],
                                                  start=(kt == 0), stop=(kt == 3)), reads=[w, gB], writes=[banks[ct]], inc=(kt == 3))
            s_ = sg[m % 2]
            o = yo[m % 2]
            for ct in range(4):
                k.op("act", lambda e: e.activation(out=s_[:, ct * 512:(ct + 1) * 512], in_=banks[ct][:], func=AF.Sigmoid,
                                                   bias=dv[:, 1, m:m + 1]), reads=[banks[ct], dv], writes=[s_])
            k.op("pool", lambda e: e.tensor_tensor(out=o[:], in0=s_[:], in1=gB[:, m, :], op=ALU.mult), reads=[s_, gB], writes=[o])
            k.dma(C.YT[512 + m * 128:512 + (m + 1) * 128, :], o[:], reads=[o], writes=[C.u_YT[4 + m]])
```

```python
import contextlib
import math
import os
import numpy as np
import ml_dtypes
import concourse.bass as bass
import concourse.mybir as mybir
from concourse.bass_utils import run_bass_kernel_spmd

F32 = mybir.dt.float32
BF16 = mybir.dt.bfloat16
AF = mybir.ActivationFunctionType
ALU = mybir.AluOpType
AX = mybir.AxisListType

D = 1024
T = 2048
NCORE = 8
NSEQ = 4
DEPTH = 2
IN_COLS = 7712
FFN_H = 2816
EPS = 1e-6
C_HY, C_S5, C_Z, C_XBC, C_DT, C_G = 0, 1536, 2048, 3072, 4608, 4640


class Unit:
    __slots__ = ("name", "lw", "rd")

    def __init__(self, name):
        self.name = name
        self.lw = None
        self.rd = {}


class Tile(Unit):
    __slots__ = ("t", "psum")

    def __init__(self, name, t, psum=False):
        super().__init__(name)
        self.t = t
        self.psum = psum

    def __getitem__(self, idx):
        return self.t[idx]


class Scope:
    def __init__(self, kb):
        self.kb = kb
        self.es = contextlib.ExitStack()

    def sb(self, name, shape, dtype):
        self.kb.uid += 1
        nm = "%s_%d" % (name, self.kb.uid)
        t = self.es.enter_context(self.kb.nc.sbuf_tensor(nm, list(shape), dtype))
        return Tile(nm, t)

    def __enter__(self):
        return self

    def __exit__(self, *a):
        if a[0] is None:
            self.kb.barrier()
        self.es.close()
        return False


class KB:
    ENG = ("pe", "act", "dve", "pool", "sp")

    def __init__(self, n_dma_sems=32):
        self.nc = bass.Bass("TRN2", target_bir_lowering=False)
        self.es = contextlib.ExitStack()
        nc = self.nc
        self.eng = {"pe": nc.tensor, "act": nc.scalar, "dve": nc.vector,
                    "pool": nc.gpsimd, "sp": nc.sync}
        self.sem = {}
        self.cnt = {}
        for e in self.ENG:
            self.sem[e] = self.es.enter_context(nc.semaphore("sem_" + e))
            self.cnt[e] = 0
        self.pending = {e: False for e in self.ENG}
        self.dma_sems = []
        for i in range(n_dma_sems):
            key = "d%d" % i
            self.sem[key] = self.es.enter_context(nc.semaphore("dsem%d" % i))
            self.cnt[key] = 0
            self.dma_sems.append(key)
        self.dma_rr = 0
        self.waited = {e: {} for e in self.ENG}
        self.n_inst = 0
        self.n_wait = 0
        self.uid = 0

    def sb(self, name, shape, dtype):
        t = self.es.enter_context(self.nc.sbuf_tensor(name, list(shape), dtype))
        return Tile(name, t)

    def ps(self, name, shape, dtype=F32):
        t = self.es.enter_context(self.nc.psum_tensor(name, list(shape), dtype))
        return Tile(name, t, psum=True)

    def dram(self, name, shape, dtype, kind="Internal"):
        return self.nc.dram_tensor(name, list(shape), dtype, kind=kind).ap()

    def scope(self):
        return Scope(self)

    def _wait(self, e, key, val):
        if key == e and e == "pe":
            return
        w = self.waited[e]
        if w.get(key, 0) >= val:
            return
        self.eng[e].wait_ge(self.sem[key], val)
        w[key] = val
        self.n_wait += 1

    def _deps(self, e, reads, writes):
        for u in reads:
            if u.lw is not None:
                self._wait(e, *u.lw)
        for u in writes:
            if u.lw is not None:
                self._wait(e, *u.lw)
            for kk, v in u.rd.items():
                self._wait(e, kk, v)

    def _mark(self, key, val, reads, writes):
        for u in reads:
            if u.rd.get(key, 0) < val:
                u.rd[key] = val
        for u in writes:
            u.lw = (key, val)
            u.rd = {}

    def op(self, e, fn, reads=(), writes=(), inc=True):
        if e != "pe":
            px = [u for u in reads if getattr(u, "psum", False)]
            if px:
                reads = [u for u in reads if not getattr(u, "psum", False)]
                writes = list(writes) + px
        self._deps(e, reads, writes)
        inst = fn(self.eng[e])
        self.n_inst += 1
        if inc:
            self.cnt[e] += 1
            inst.then_inc(self.sem[e], 1)
            val = self.cnt[e]
            self.pending[e] = False
        else:
            val = self.cnt[e] + 1
            self.pending[e] = True
        self._mark(e, val, reads, writes)
        return inst

    def dma(self, out, in_, reads=(), writes=(), q="sp", **kw):
        key = self.dma_sems[self.dma_rr]
        self.dma_rr = (self.dma_rr + 1) % len(self.dma_sems)
        if self.cnt[key] > 0:
            self._wait(q, key, self.cnt[key])
        self._deps(q, reads, writes)
        inst = self.eng[q].dma_start(out=out, in_=in_, **kw)
        self.cnt[key] += 16
        inst.then_inc(self.sem[key], 16)
        self.n_inst += 1
        self._mark(key, self.cnt[key], reads, writes)
        return inst

    def barrier(self):
        for e in self.ENG:
            assert not self.pending[e], "pending non-inc op on " + e
        for e in self.ENG:
            for key in list(self.ENG) + self.dma_sems:
                if self.cnt[key] > 0:
                    self._wait(e, key, self.cnt[key])

    def finish(self):
        self.barrier()


class Ctx:
    pass


BIGW = {"w_in": (D, IN_COLS), "p_a": (512, D), "p_b": (512, D), "p_c": (1024, D), "w_out": (D, D),
        "ffn_up": (D, 2 * FFN_H), "ffn_down": (FFN_H, D), "s5_w_glu": (512, 512)}


class WStream:
    def __init__(self, k, sc, C, specs, kt_max, nbuf=3, name="ws"):
        self.k, self.C, self.specs, self.nbuf = k, C, specs, nbuf
        self.bufs = [sc.sb(name, [128, kt_max, 128], BF16) for _ in range(nbuf)]
        self.issued = 0

    def _issue(self, i):
        wn, l, c0, msz, KT = self.specs[i]
        b = self.bufs[i % self.nbuf]
        self.k.dma(b[:, 0:KT, 0:msz],
                   self.C.Wb[wn][l, :, c0:c0 + msz].rearrange("(n p) m -> p n m", p=128),
                   reads=[self.C.u_Wb[(wn, l)]], writes=[b])

    def get(self, i):
        while self.issued < min(i + self.nbuf - 1, len(self.specs)):
            self._issue(self.issued)
            self.issued += 1
        return self.bufs[i % self.nbuf]


def stage_convert_weights(k, C):
    with k.scope() as sc:
        fb = [sc.sb("cvf", [128, 2048], F32) for _ in range(3)]
        bb = [sc.sb("cvb", [128, 2048], BF16) for _ in range(3)]
        n = 0
        engs = ["act", "dve", "pool"]
        only = os.environ.get('CONV_ONLY', '')
        for nm, (R, Cc) in BIGW.items():
            if only and nm not in only.split('+'):
                continue
            for l in range(DEPTH if not only else 1):
                for r0 in range(0, R, 128):
                    for c0 in range(0, Cc, 2048):
                        cw_ = min(2048, Cc - c0)
                        f, b = fb[n % 3], bb[n % 3]
                        k.dma(f[:, 0:cw_], C.W[nm][l, r0:r0 + 128, c0:c0 + cw_], writes=[f])
                        e = engs[n % 3]
                        if e == "act":
                            k.op("act", lambda e_: e_.activation(out=b[:, 0:cw_], in_=f[:, 0:cw_], func=AF.Copy), reads=[f], writes=[b])
                        else:
                            k.op(e, lambda e_: e_.tensor_copy(out=b[:, 0:cw_], in_=f[:, 0:cw_]), reads=[f], writes=[b])
                        k.dma(C.Wb[nm][l, r0:r0 + 128, c0:c0 + cw_], b[:, 0:cw_], reads=[b], writes=[C.u_Wb[(nm, l)]])
                        n += 1


def units(prefix, n):
    return [Unit("%s%d" % (prefix, i)) for i in range(n)]


def inproj_tiles():
    tl = [(c0, 128) for c0 in range(0, C_DT, 128)]
    tl.append((C_DT, 32))
    tl += [(C_G + i * 128, 128) for i in range(24)]
    return tl


def load_featvecs(k, C, dst, vecs, n):
    with k.scope() as sc:
        ps = C.PS[7]
        for i, v in enumerate(vecs):
            st = sc.sb("fvst", [n, 128], F32)
            k.dma(st[:], v.rearrange("(n p) -> n p", p=128), writes=[st])
            k.op("pe", lambda e: e.transpose(out=ps[:, i * n:(i + 1) * n], in_=st[0:n, :], identity=C.identF[0:n, 0:n]),
                 reads=[st, C.identF], writes=[ps])
        nv = len(vecs)
        k.op("dve", lambda e: e.tensor_copy(out=dst[:].rearrange("p a b -> p (a b)"), in_=ps[:, 0:nv * n]),
             reads=[ps], writes=[dst])


def build_context(k, nseq, dbg):
    C = Ctx()
    C.nseq = nseq
    C.dbg = dbg
    ein = lambda n, s, dt=F32: k.dram(n, s, dt, kind="ExternalInput")
    C.x = ein("x", [nseq, T, D])
    C.out = k.dram("out", [nseq, T, D], F32, kind="ExternalOutput")
    W = {}
    W["norm_mix"] = ein("norm_mix", [DEPTH, D])
    W["w_in"] = ein("w_in", [DEPTH, D, IN_COLS])
    W["p_a"] = ein("p_a", [DEPTH, 512, D])
    W["p_b"] = ein("p_b", [DEPTH, 512, D])
    W["p_c"] = ein("p_c", [DEPTH, 1024, D])
    W["w_out"] = ein("w_out", [DEPTH, D, D])
    W["norm_ffn"] = ein("norm_ffn", [DEPTH, D])
    W["ffn_up"] = ein("ffn_up", [DEPTH, D, 2 * FFN_H])
    W["ffn_conv_w"] = ein("ffn_conv_w", [DEPTH, 3, 2 * FFN_H])
    W["ffn_conv_b"] = ein("ffn_conv_b", [DEPTH, 2 * FFN_H])
    W["ffn_down"] = ein("ffn_down", [DEPTH, FFN_H, D])
    W["norm_final"] = ein("norm_final", [D])
    for nm, shp in (("s5_a_re", [DEPTH, 2, 32, 64]), ("s5_a_im", [DEPTH, 2, 32, 64]), ("s5_log_dt", [DEPTH, 2, 32]),
                    ("s5_b_re", [DEPTH, 2, 32, 64, 16]), ("s5_b_im", [DEPTH, 2, 32, 64, 16]),
                    ("s5_c_re", [DEPTH, 2, 32, 16, 64]), ("s5_c_im", [DEPTH, 2, 32, 16, 64]),
                    ("s5_d", [DEPTH, 512]), ("s5_w_glu", [DEPTH, 512, 512]), ("s5_b_glu", [DEPTH, 512])):
        W[nm] = ein(nm, shp)
    C.W = W
    C.Wb = {}
    C.u_Wb = {}
    for nm in BIGW:
        shp = BIGW[nm]
        C.Wb[nm] = k.dram("wb_" + nm, [DEPTH] + list(shp), BF16)
        for l in range(DEPTH):
            C.u_Wb[(nm, l)] = Unit("wb_%s%d" % (nm, l))
    C.c_ident = ein("c_ident", [128, 128])
    kind = "ExternalOutput" if dbg else "Internal"
    yk = "ExternalInput" if dbg == "ytest" else kind
    nsc = 1 if dbg else nseq
    C.xT_all = [k.dram("xT" + ("%d" % q if q else ""), [D, T], F32, kind=kind) for q in range(nsc)]
    C.PT_all = [k.dram("PT" + ("%d" % q if q else ""), [IN_COLS, T], BF16, kind=kind) for q in range(nsc)]
    C.DT_all = [k.dram("DTs" + ("%d" % q if q else ""), [32, T], F32, kind=kind) for q in range(nsc)]
    C.YT_all = [k.dram("YT" + ("%d" % q if q else ""), [2048, T], BF16, kind=yk) for q in range(nsc)]
    C.u_xT_all = [units("xT%d_" % q, 8) for q in range(nsc)]
    C.u_PT_all = [{c0: Unit("PT%d_%d" % (q, c0)) for c0, _ in inproj_tiles()} for q in range(nsc)]
    C.u_YT_all = [units("YT%d_" % q, 16) for q in range(nsc)]

    def set_seq(q):
        q = min(q, nsc - 1)
        C.xT, C.PT, C.DT, C.YT = C.xT_all[q], C.PT_all[q], C.DT_all[q], C.YT_all[q]
        C.u_xT, C.u_PT, C.u_YT = C.u_xT_all[q], C.u_PT_all[q], C.u_YT_all[q]
    C.set_seq = set_seq
    set_seq(0)
    C.YS5 = k.dram("YS5", [nsc, 2, 512, T], BF16, kind="Internal")
    C.u_YS5 = [Unit("YS5_%d" % q) for q in range(nsc)]
    C.identF = k.sb("identF", [128, 128], F32)
    C.identB = k.sb("identB", [128, 128], BF16)
    C.onesB = k.sb("onesB", [128, 128], BF16)
    k.dma(C.identF[:], C.c_ident[:, :], writes=[C.identF])
    k.op("dve", lambda e: e.tensor_copy(out=C.identB[:], in_=C.identF[:]), reads=[C.identF], writes=[C.identB])
    k.op("dve", lambda e: e.memset(C.onesB[:], 1.0), writes=[C.onesB])
    C.PS = [k.ps("ps%d" % i, [128, 512], F32) for i in range(8)]
    C.HT = k.sb("HT", [128, 8, T], BF16)
    nwt = k.sb("nw_all", [128, 5, 8], F32)
    load_featvecs(k, C, nwt, [W["norm_mix"][0], W["norm_mix"][1], W["norm_ffn"][0], W["norm_ffn"][1], W["norm_final"]], 8)
    C.nw = {("norm_mix", 0): (nwt, 0), ("norm_mix", 1): (nwt, 1), ("norm_ffn", 0): (nwt, 2), ("norm_ffn", 1): (nwt, 3),
            ("norm_final", 0): (nwt, 4)}
    return C


def evac(k, i, out_ap, in_ap, reads, writes):
    if i % 2 == 0:
        k.op("act", lambda e: e.activation(out=out_ap, in_=in_ap, func=AF.Copy), reads=reads, writes=writes)
    else:
        k.op("dve", lambda e: e.tensor_copy(out=out_ap, in_=in_ap), reads=reads, writes=writes)


def stage_load_x(k, C, b):
    with k.scope() as sc:
        xin = [sc.sb("xin", [128, 4, D], F32) for _ in range(2)]
        xo = [sc.sb("xo", [128, 512], F32) for _ in range(3)]
        n = 0
        for g in range(4):
            xt = xin[g % 2]
            k.dma(xt[:], C.x[b, g * 512:(g + 1) * 512, :].rearrange("(n p) d -> p n d", p=128), writes=[xt])
            for dt in range(8):
                ps = C.PS[n % 8]
                for j in range(4):
                    k.op("pe", lambda e: e.transpose(out=ps[:, j * 128:(j + 1) * 128],
                                                     in_=xt[:, j, dt * 128:(dt + 1) * 128],
                                                     identity=C.identF[:]),
                         reads=[xt, C.identF], writes=[ps], inc=(j == 3))
                o = xo[n % 3]
                evac(k, n, o[:], ps[:], [ps], [o])
                k.dma(C.xT[dt * 128:(dt + 1) * 128, g * 512:(g + 1) * 512], o[:], reads=[o], writes=[C.u_xT[dt]])
                n += 1


def rstd_cols(k, C, sc_tiles, ct, xt):
    sq, rs = sc_tiles
    s = sq[ct % 2]
    r = rs[ct % 2]
    k.op("act", lambda e: e.activation(out=s[:], in_=xt[:], func=AF.Square), reads=[xt], writes=[s])
    ps = C.PS[ct % 8]
    for dt in range(8):
        k.op("pe", lambda e: e.matmul(ps[:], lhsT=C.onesB[:], rhs=s[:, dt, :], start=(dt == 0), stop=(dt == 7)),
             reads=[C.onesB, s], writes=[ps], inc=(dt == 7))
    k.op("act", lambda e: e.activation(out=r[:], in_=ps[:], func=AF.Sqrt, scale=1.0 / D, bias=C.epsT[:, 0:1]),
         reads=[ps, C.epsT], writes=[r])
    k.op("dve", lambda e: e.reciprocal(out=r[:], in_=r[:]), reads=[r], writes=[r])
    return r


def stage_norm(k, C, wcol):
    HT = C.HT
    with k.scope() as sc:
        xs = [sc.sb("nx", [128, 8, 512], F32) for _ in range(2)]
        sq = [sc.sb("nsq", [128, 8, 512], BF16) for _ in range(2)]
        rs = [sc.sb("nrs", [128, 512], F32) for _ in range(2)]
        for ct in range(4):
            xt = xs[ct % 2]
            k.dma(xt[:], C.xT[:, ct * 512:(ct + 1) * 512].rearrange("(n p) t -> p n t", p=128),
                  reads=C.u_xT, writes=[xt])
            r = rstd_cols(k, C, (sq, rs), ct, xt)
            for dt in range(8):
                k.op("dve", lambda e: e.scalar_tensor_tensor(out=HT[:, dt, ct * 512:(ct + 1) * 512], in0=xt[:, dt, :],
                                                             scalar=wcol[0][:, wcol[1], dt:dt + 1], op0=ALU.mult,
                                                             in1=r[:], op1=ALU.mult),
                     reads=[xt, r, wcol[0]], writes=[HT])


def stage_inproj(k, C, l):
    HT = C.HT
    with k.scope() as sc:
        tl = inproj_tiles()
        ws = WStream(k, sc, C, [("w_in", l, c0, sz, 8) for c0, sz in tl], 8)
        ot = [sc.sb("po", [128, T], BF16) for _ in range(2)]
        otf = sc.sb("pof", [32, T], F32)
        for mi, (c0, sz) in enumerate(tl):
            w = ws.get(mi)
            banks = C.PS[(mi % 2) * 4:(mi % 2) * 4 + 4]
            for ct in range(4):
                for kt in range(8):
                    k.op("pe", lambda e: e.matmul(banks[ct][0:sz, :], lhsT=w[:, kt, 0:sz],
                                                  rhs=HT[:, kt, ct * 512:(ct + 1) * 512],
                                                  start=(kt == 0), stop=(kt == 7)),
                         reads=[w, HT], writes=[banks[ct]], inc=(kt == 7))
            o = otf if c0 == C_DT else ot[mi % 2]
            for ct in range(4):
                evac(k, ct, o[0:sz, ct * 512:(ct + 1) * 512], banks[ct][0:sz, :], [banks[ct]], [o])
            if c0 == C_DT:
                k.dma(C.DT[:, :], o[0:sz, :], reads=[o], writes=[C.u_PT[c0]])
            else:
                k.dma(C.PT[c0:c0 + sz, :], o[0:sz, :], reads=[o], writes=[C.u_PT[c0]])


def stage_merge(k, C, l):
    with k.scope() as sc:
        Y = sc.sb("Y", [128, 16, T], BF16)
        MT = sc.sb("MT", [128, 8, T], BF16)
        for i in range(16):
            k.dma(Y[:, i, :], C.YT[i * 128:(i + 1) * 128, :], reads=[C.u_YT[i]], writes=[Y])
        branches = [("p_a", 0, 4), ("p_b", 4, 4), ("p_c", 8, 8)]
        specs = [(wn, l, m * 128, 128, nk) for m in range(8) for (wn, y0, nk) in branches]
        specs += [("w_out", l, m * 128, 128, 8) for m in range(8)]
        ws = WStream(k, sc, C, specs, 8)
        gt = [sc.sb("gt", [128, T], BF16) for _ in range(2)]
        sg = [sc.sb("sg", [128, T], F32) for _ in range(2)]
        acc = sc.sb("acc", [128, T], F32)
        tmp = sc.sb("mtmp", [128, T], F32)
        branches = [("p_a", 0, 4), ("p_b", 4, 4), ("p_c", 8, 8)]
        n = 0
        for m in range(8):
            for bi, (wn, y0, nk) in enumerate(branches):
                w = ws.get(n)
                g = gt[n % 2]
                c0 = C_G + bi * 1024 + m * 128
                k.dma(g[:], C.PT[c0:c0 + 128, :], reads=[C.u_PT[c0]], writes=[g])
                s = sg[n % 2]
                k.op("act", lambda e: e.activation(out=s[:], in_=g[:], func=AF.Sigmoid), reads=[g], writes=[s])
                banks = C.PS[(n % 2) * 4:(n % 2) * 4 + 4]
                for ct in range(4):
                    for kt in range(nk):
                        k.op("pe", lambda e: e.matmul(banks[ct][:], lhsT=w[:, kt, :],
                                                      rhs=Y[:, y0 + kt, ct * 512:(ct + 1) * 512],
                                                      start=(kt == 0), stop=(kt == nk - 1)),
                             reads=[w, Y], writes=[banks[ct]], inc=(kt == nk - 1))
                for ct in range(4):
                    cs = slice(ct * 512, (ct + 1) * 512)
                    if bi == 0:
                        k.op("dve", lambda e: e.tensor_tensor(out=acc[:, cs], in0=banks[ct][:], in1=s[:, cs], op=ALU.mult),
                             reads=[banks[ct], s], writes=[acc])
                    else:
                        k.op("dve", lambda e: e.tensor_tensor(out=tmp[:, cs], in0=banks[ct][:], in1=s[:, cs], op=ALU.mult),
                             reads=[banks[ct], s], writes=[tmp])
                        if bi == 1:
                            k.op("pool", lambda e: e.tensor_tensor(out=acc[:, cs], in0=acc[:, cs], in1=tmp[:, cs], op=ALU.add),
                                 reads=[acc, tmp], writes=[acc])
                        else:
                            k.op("pool", lambda e: e.tensor_tensor(out=MT[:, m, cs], in0=acc[:, cs], in1=tmp[:, cs], op=ALU.add),
                                 reads=[acc, tmp], writes=[MT])
                n += 1
        if C.dbg:
            for m in range(8):
                k.dma(C.dbgMT[m * 128:(m + 1) * 128, :], MT[:, m, :], reads=[MT])
        xr = [sc.sb("xr", [128, T], F32) for _ in range(2)]
        for m in range(8):
            w = ws.get(n)
            x_ = xr[m % 2]
            k.dma(x_[:], C.xT[m * 128:(m + 1) * 128, :], reads=[C.u_xT[m]], writes=[x_])
            banks = C.PS[(n % 2) * 4:(n % 2) * 4 + 4]
            for ct in range(4):
                for kt in range(8):
                    k.op("pe", lambda e: e.matmul(banks[ct][:], lhsT=w[:, kt, :],
                                                  rhs=MT[:, kt, ct * 512:(ct + 1) * 512],
                                                  start=(kt == 0), stop=(kt == 7)),
                         reads=[w, MT], writes=[banks[ct]], inc=(kt == 7))
            for ct in range(4):
                cs = slice(ct * 512, (ct + 1) * 512)
                k.op("dve", lambda e: e.tensor_tensor(out=x_[:, cs], in0=banks[ct][:], in1=x_[:, cs], op=ALU.add),
                     reads=[banks[ct], x_], writes=[x_])
            k.dma(C.xT[m * 128:(m + 1) * 128, :], x_[:], reads=[x_], writes=[C.u_xT[m]])
            n += 1


def stage_ffn(k, C, l):
    HT = C.HT
    SK = set(os.environ.get('FFN_SKIP', '').split(','))
    with k.scope() as sc0:
        AT = sc0.sb("AT", [128, 22, T], BF16)
        n = 0
        with k.scope() as sc:
            NJ = int(os.environ.get('FFN_J', '22'))
            ws = WStream(k, sc, C, [("ffn_up", l, (half * 22 + j) * 128, 128, 8) for j in range(NJ) for half in range(2)], 8)
            cw = sc.sb("cw", [128, 4, 44], F32)
            load_featvecs(k, C, cw, [C.W["ffn_conv_w"][l, 0], C.W["ffn_conv_w"][l, 1], C.W["ffn_conv_w"][l, 2],
                                     C.W["ffn_conv_b"][l]], 44)
            gs = [sc.sb("gs", [128, T + 2], BF16) for _ in range(2)]
            ac = [sc.sb("ac", [128, T], F32) for _ in range(2)]
            tb = [sc.sb("tb", [128, T], F32) for _ in range(2)]
            sgt = sc.sb("sgt", [128, T], F32)
            for g in gs:
                k.op("pool", lambda e: e.memset(g[:, 0:1], 0.0), writes=[g])
                k.op("pool", lambda e: e.memset(g[:, T + 1:T + 2], 0.0), writes=[g])
            for j in range(NJ):
                for half in range(2):
                    fi = half * 22 + j
                    w = ws.get(n)
                    banks = C.PS[(n % 2) * 4:(n % 2) * 4 + 4]
                    for ct in range(4):
                        for kt in range(8):
                            k.op("pe", lambda e: e.matmul(banks[ct][:], lhsT=w[:, kt, :],
                                                          rhs=HT[:, kt, ct * 512:(ct + 1) * 512],
                                                          start=(kt == 0), stop=(kt == 7)),
                                 reads=[w, HT], writes=[banks[ct]], inc=(kt == 7))
                    g = gs[half]
                    a = ac[half]
                    t1, t2 = tb[0], tb[1]
                    for ct in range(4):
                        cs = slice(ct * 512, (ct + 1) * 512)
                        k.op("act", lambda e: e.activation(out=g[:, 1 + ct * 512:1 + (ct + 1) * 512], in_=banks[ct][:], func=AF.Copy),
                             reads=[banks[ct]], writes=[g])
                        k.op("dve", lambda e: e.tensor_scalar(out=t1[:, cs], in0=banks[ct][:], scalar1=cw[:, 1, fi:fi + 1],
                                                              scalar2=cw[:, 3, fi:fi + 1], op0=ALU.mult, op1=ALU.add),
                             reads=[banks[ct], cw], writes=[t1])
                    k.op("dve", lambda e: e.scalar_tensor_tensor(out=t2[:, :], in0=g[:, 0:T], scalar=cw[:, 0, fi:fi + 1],
                                                                 op0=ALU.mult, in1=t1[:, :], op1=ALU.add),
                         reads=[g, t1, cw], writes=[t2])
                    k.op("dve", lambda e: e.scalar_tensor_tensor(out=a[:, :], in0=g[:, 2:T + 2], scalar=cw[:, 2, fi:fi + 1],
                                                                 op0=ALU.mult, in1=t2[:, :], op1=ALU.add),
                         reads=[g, t2, cw], writes=[a])
                    n += 1
                k.op("act", lambda e: e.activation(out=sgt[:], in_=ac[0][:], func=AF.Silu), reads=[ac[0]], writes=[sgt])
                k.op("pool", lambda e: e.tensor_tensor(out=AT[:, j, :], in0=sgt[:], in1=ac[1][:], op=ALU.mult),
                     reads=[sgt, ac[1]], writes=[AT])
        with k.scope() as sc:
            NM = int(os.environ.get('FFN_M', '8'))
            wsd = WStream(k, sc, C, [("ffn_down", l, m * 128, 128, 22) for m in range(NM)], 22, nbuf=2, name="wd")
            xr = [sc.sb("xr", [128, T], F32) for _ in range(2)]
            for m in range(int(os.environ.get('FFN_M', '8'))):
                w = wsd.get(m)
                x_ = xr[m % 2]
                k.dma(x_[:], C.xT[m * 128:(m + 1) * 128, :], reads=[C.u_xT[m]], writes=[x_])
                banks = C.PS[(n % 2) * 4:(n % 2) * 4 + 4]
                for ct in range(4):
                    for kt in range(22):
                        k.op("pe", lambda e: e.matmul(banks[ct][:], lhsT=w[:, kt, :],
                                                      rhs=AT[:, kt, ct * 512:(ct + 1) * 512],
                                                      start=(kt == 0), stop=(kt == 21)),
                             reads=[w, AT], writes=[banks[ct]], inc=(kt == 21))
                for ct in range(4):
                    cs = slice(ct * 512, (ct + 1) * 512)
                    k.op("dve", lambda e: e.tensor_tensor(out=x_[:, cs], in0=banks[ct][:], in1=x_[:, cs], op=ALU.add),
                         reads=[banks[ct], x_], writes=[x_])
                k.dma(C.xT[m * 128:(m + 1) * 128, :], x_[:], reads=[x_], writes=[C.u_xT[m]])
                n += 1


def stage_final(k, C, b):
    wcol = C.nw[("norm_final", 0)]
    with k.scope() as sc:
        xs = [sc.sb("fx", [128, 8, 512], F32) for _ in range(2)]
        sq = [sc.sb("fsq", [128, 8, 512], BF16) for _ in range(2)]
        rs = [sc.sb("frs", [128, 512], F32) for _ in range(2)]
        hn = [sc.sb("fhn", [128, 8, 512], F32) for _ in range(2)]
        ot = [sc.sb("fo", [128, D], F32) for _ in range(2)]
        n = 0
        for ct in range(4):
            xt = xs[ct % 2]
            k.dma(xt[:], C.xT[:, ct * 512:(ct + 1) * 512].rearrange("(n p) t -> p n t", p=128),
                  reads=C.u_xT, writes=[xt])
            r = rstd_cols(k, C, (sq, rs), ct, xt)
            h = hn[ct % 2]
            for dt in range(8):
                k.op("dve", lambda e: e.scalar_tensor_tensor(out=h[:, dt, :], in0=xt[:, dt, :],
                                                             scalar=wcol[0][:, wcol[1], dt:dt + 1], op0=ALU.mult,
                                                             in1=r[:], op1=ALU.mult),
                     reads=[xt, r, wcol[0]], writes=[h])
            for tt in range(4):
                o = ot[n % 2]
                for half in range(2):
                    ps = C.PS[4 + (n * 2 + half) % 4]
                    for j in range(4):
                        dt = half * 4 + j
                        k.op("pe", lambda e: e.transpose(out=ps[:, j * 128:(j + 1) * 128],
                                                         in_=h[:, dt, tt * 128:(tt + 1) * 128],
                                                         identity=C.identF[:]),
                             reads=[h, C.identF], writes=[ps], inc=(j == 3))
                    evac(k, half, o[:, half * 512:(half + 1) * 512], ps[:], [ps], [o])
                t0 = ct * 512 + tt * 128
                k.dma(C.out[b, t0:t0 + 128, :], o[:], reads=[o])
                n += 1


def build_program(nseq=NSEQ, dbg=None, stages=None):
    k = KB()
    C = build_context(k, nseq, dbg)
    hyena_context(k, C)
    ssd_context(k, C)
    s5_context(k, C)
    C.epsT = k.sb("epsT", [128, 1], F32)
    k.op("dve", lambda e: e.memset(C.epsT[:], EPS), writes=[C.epsT])
    if dbg:
        C.dbgMT = k.dram("dbgMT", [D, T], BF16, kind="ExternalOutput")
        C.dbgHT = k.dram("dbgHT", [D, T], BF16, kind="ExternalOutput")
    st = stages
    on = lambda n: (st is None) or (n in st)
    if dbg is None and stages is None:
        stage_convert_weights(k, C)
        for l in range(DEPTH):
            stage_hyena_filters(k, C, l)
            stage_s5_tables(k, C, l)
        for l in range(DEPTH):
            for b in range(nseq):
                C.set_seq(b)
                if l == 0:
                    stage_load_x(k, C, b)
                stage_norm(k, C, C.nw[("norm_mix", l)])
                stage_inproj(k, C, l)
                stage_hyena(k, C, l)
                stage_ssd(k, C, l)
            stage_s5_batched(k, C, l, nseq)
            for b in range(nseq):
                C.set_seq(b)
                stage_merge(k, C, l)
                stage_norm(k, C, C.nw[("norm_ffn", l)])
                stage_ffn(k, C, l)
                if l == DEPTH - 1:
                    stage_final(k, C, b)
        k.finish()
        return k, C
    st = stages
    on = lambda n: (st is None) or (n in st)
    for b in range(nseq):
        if b == 0 and on("conv"):
            stage_convert_weights(k, C)
        if b == 0 and on("hyfilt"):
            for l in range(int(os.environ.get('NLAYERS', DEPTH))):
                stage_hyena_filters(k, C, l)
        if b == 0 and on("s5tab"):
            for l in range(int(os.environ.get('NLAYERS', DEPTH))):
                stage_s5_tables(k, C, l)
        if on("load"):
            stage_load_x(k, C, b)
        for l in range(int(os.environ.get('NLAYERS', DEPTH))):
            if on("norm1"):
                stage_norm(k, C, C.nw[("norm_mix", l)])
                if dbg and b == 0 and l == 0:
                    for dt in range(8):
                        k.dma(C.dbgHT[dt * 128:(dt + 1) * 128, :], C.HT[:, dt, :], reads=[C.HT])
            if on("inproj"):
                stage_inproj(k, C, l)
            if on("hyena"):
                stage_hyena(k, C, l)
            if on("s5"):
                stage_s5(k, C, l)
            if on("ssd"):
                stage_ssd(k, C, l)
            if on("merge"):
                stage_merge(k, C, l)
            if on("norm2"):
                stage_norm(k, C, C.nw[("norm_ffn", l)])
            if on("ffn"):
                stage_ffn(k, C, l)
            if dbg == "ytest":
                break
        if on("final"):
            stage_final(k, C, b)
    k.finish()
    return k, C


NFT = 16
TWO_PI = 2.0 * math.pi
MAGIC = 12582912.0


def hyena_consts():
    L, N = T, 2 * T
    f = (np.arange(2048, dtype=np.float64) + 0.5)
    t = np.arange(L, dtype=np.float64)
    th = TWO_PI * np.outer(f, t) / N
    cs = np.stack([np.cos(th), np.sin(th)], 0)
    inv = cs.reshape(32, 128, L).astype(ml_dtypes.bfloat16)
    fw = cs.reshape(2, 16, 128, 16, 128)
    fw = np.ascontiguousarray(fw.transpose(0, 1, 4, 3, 2)).reshape(32, 128, 16, 128).astype(ml_dtypes.bfloat16)
    pos = np.arange(L, dtype=np.float32)
    tn = (pos / np.float32(L - 1)).astype(np.float32)
    bands = np.linspace(1e-4, 15.0, 16, dtype=np.float32)
    ang = (np.float32(2.0 * math.pi / L) * pos[:, None] * bands[None, :]).astype(np.float32)
    z = np.concatenate([tn[:, None], np.cos(ang), -np.sin(ang)], axis=-1).astype(np.float32)
    zT = np.ascontiguousarray(z.T)
    tpos = np.ascontiguousarray(np.broadcast_to(tn[None, :], (128, L))).astype(np.float32)
    return {"c_dftf": fw, "c_dfti": inv, "c_zT": zT, "c_tpos": tpos}


def hyena_context(k, C):
    ein = lambda n, s, dt=F32: k.dram(n, s, dt, kind="ExternalInput")
    C.c_dftf = ein("c_dftf", [32, 128, 16, 128], BF16)
    C.c_dfti = ein("c_dfti", [32, 128, T], BF16)
    C.c_zT = ein("c_zT", [33, T])
    C.c_tpos = ein("c_tpos", [128, T])
    for nm, shp in (("hy_short_w", [DEPTH, 3, 1536]), ("hy_short_b", [DEPTH, 1536]), ("hy_w1", [DEPTH, 33, 64]),
                    ("hy_b1", [DEPTH, 64]), ("hy_freq", [DEPTH, 2, 64]), ("hy_w2", [DEPTH, 64, 64]),
                    ("hy_b2", [DEPTH, 64]), ("hy_w3", [DEPTH, 64, 1024]), ("hy_decay", [DEPTH, 1024]),
                    ("hy_bias", [DEPTH, 512])):
        C.W[nm] = ein(nm, shp)
    kind = "ExternalOutput" if C.dbg else "Internal"
    C.KTAB = k.dram("KTAB", [DEPTH, 32, 128, 512], F32, kind=kind)
    C.u_KTAB = [Unit("KTAB%d" % l) for l in range(DEPTH)]


def sin_reduced(k, sc, h, bias_col, freq_col, bu, np_, ncols):
    y = sc.sb("sry", [np_, ncols], F32)
    t = sc.sb("srt", [np_, ncols], F32)
    k.op("dve", lambda e: e.tensor_scalar(out=y[:], in0=h[:], scalar1=bias_col, scalar2=freq_col,
                                          op0=ALU.add, op1=ALU.mult), reads=[h, bu], writes=[y])
    k.op("dve", lambda e: e.tensor_scalar(out=t[:], in0=y[:], scalar1=1.0 / TWO_PI, scalar2=MAGIC,
                                          op0=ALU.mult, op1=ALU.add), reads=[y], writes=[t])
    k.op("dve", lambda e: e.tensor_scalar(out=t[:], in0=t[:], scalar1=-MAGIC, scalar2=None, op0=ALU.add),
         reads=[t], writes=[t])
    k.op("dve", lambda e: e.scalar_tensor_tensor(out=y[:], in0=t[:], scalar=-TWO_PI, op0=ALU.mult, in1=y[:],
                                                 op1=ALU.add), reads=[t, y], writes=[y])
    k.op("act", lambda e: e.activation(out=h[:], in_=y[:], func=AF.Sin), reads=[y], writes=[h])


def dft_forward(k, C, sc, rhs_for_ft, consume):
    fw = [sc.sb("dftf", [128, 16, 128], BF16) for _ in range(4)]
    issued = [0]

    def issue_upto(n):
        while issued[0] < min(n, 32):
            i = issued[0]
            ft = (i // 2) + 16 * (i % 2)
            k.dma(fw[i % 4][:], C.c_dftf[ft], writes=[fw[i % 4]])
            issued[0] += 1
    for j in range(NFT):
        issue_upto(2 * j + 4)
        banks = (C.PS[(j % 2) * 2], C.PS[(j % 2) * 2 + 1])
        for part in range(2):
            w = fw[(2 * j + part) % 4]
            rhs, ru = rhs_for_ft(part)
            for kt in range(16):
                k.op("pe", lambda e: e.matmul(banks[part][:], lhsT=w[:, kt, :], rhs=rhs[:, kt, :],
                                              start=(kt == 0), stop=(kt == 15)),
                     reads=[w, ru], writes=[banks[part]], inc=(kt == 15))
        consume(j, banks[0], banks[1])


def transpose_to_tokmajor(k, C, dstT, src_tiles_fn, nct, src_unit):
    n = 0
    for kt in range(16):
        ps = C.PS[4 + (kt % 4)]
        psb = ps[:].bitcast(BF16)
        for ci in range(nct):
            src = src_tiles_fn(ci)
            k.op("pe", lambda e: e.transpose(out=psb[:, ci * 128:(ci + 1) * 128], in_=src[:, kt * 128:(kt + 1) * 128],
                                             identity=C.identB[:]),
                 reads=[src_unit, C.identB], writes=[ps], inc=(ci == nct - 1))
        evac(k, kt, dstT[:, kt, 0:nct * 128], psb[:, 0:nct * 128], [ps], [dstT])


def stage_hyena_filters(k, C, l):
    W = C.W
    with k.scope() as sc0:
        kpm = [sc0.sb("kpm", [128, 4, T], BF16) for _ in range(2)]
        with k.scope() as sc:
            zT = sc.sb("zT", [33, T], F32)
            k.dma(zT[:], C.c_zT[:, :], writes=[zT])
            w1 = sc.sb("hw1", [33, 64], F32)
            w2 = sc.sb("hw2", [64, 64], F32)
            w3 = sc.sb("hw3", [64, 1024], F32)
            k.dma(w1[:], W["hy_w1"][l], writes=[w1])
            k.dma(w2[:], W["hy_w2"][l], writes=[w2])
            k.dma(w3[:], W["hy_w3"][l], writes=[w3])
            sv = sc.sb("hsv", [64, 4], F32)
            for i, v in enumerate((W["hy_b1"][l], W["hy_freq"][l, 0], W["hy_b2"][l], W["hy_freq"][l, 1])):
                k.dma(sv[:, i:i + 1], v.rearrange("(p o) -> p o", o=1), writes=[sv])
            dec = sc.sb("hdec", [128, 1, 8], F32)
            load_featvecs(k, C, dec, [W["hy_decay"][l]], 8)
            decn = sc.sb("hdecn", [128, 1, 8], F32)
            k.op("dve", lambda e: e.tensor_scalar(out=decn[:], in0=dec[:], scalar1=-1.0, scalar2=None, op0=ALU.mult),
                 reads=[dec], writes=[decn])
            k.op("dve", lambda e: e.tensor_tensor(out=dec[:], in0=dec[:], in1=decn[:], op=ALU.min),
                 reads=[dec, decn], writes=[dec])
            tpos = sc.sb("tpos", [128, T], F32)
            k.dma(tpos[:], C.c_tpos[:, :], writes=[tpos])
            h1 = sc.sb("h1", [64, T], F32)
            h2 = sc.sb("h2", [64, T], F32)
            for ct in range(4):
                cs = slice(ct * 512, (ct + 1) * 512)
                ps = C.PS[ct]
                k.op("pe", lambda e: e.matmul(ps[0:64, :], lhsT=w1[:, :], rhs=zT[:, cs], start=True, stop=True),
                     reads=[w1, zT], writes=[ps])
                k.op("dve", lambda e: e.tensor_copy(out=h1[:, cs], in_=ps[0:64, :]), reads=[ps], writes=[h1])
            with k.scope() as s2:
                sin_reduced(k, s2, h1, sv[:, 0:1], sv[:, 1:2], sv, 64, T)
            for ct in range(4):
                cs = slice(ct * 512, (ct + 1) * 512)
                ps = C.PS[ct]
                k.op("pe", lambda e: e.matmul(ps[0:64, :], lhsT=w2[:, :], rhs=h1[:, cs], start=True, stop=True),
                     reads=[w2, h1], writes=[ps])
                k.op("dve", lambda e: e.tensor_copy(out=h2[:, cs], in_=ps[0:64, :]), reads=[ps], writes=[h2])
            with k.scope() as s2:
                sin_reduced(k, s2, h2, sv[:, 2:3], sv[:, 3:4], sv, 64, T)
            hfb = [sc.sb("hfb", [128, T], F32) for _ in range(2)]
            win = sc.sb("hwin", [128, T], F32)
            nrm = sc.sb("hnrm", [128, 2], F32)
            tot = sc.sb("htot", [128, 1], F32)
            tmp = sc.sb("htmp", [128, T], F32)
            for ci in range(4):
                for half in range(2):
                    ft = half * 4 + ci
                    dstt = hfb[half]
                    k.op("act", lambda e: e.activation(out=win[:], in_=tpos[:], func=AF.Exp, scale=dec[:, 0, ft:ft + 1]),
                         reads=[tpos, dec], writes=[win])
                    for ct in range(4):
                        cs = slice(ct * 512, (ct + 1) * 512)
                        ps = C.PS[(ft * 4 + ct) % 8]
                        k.op("pe", lambda e: e.matmul(ps[:], lhsT=w3[:, ft * 128:(ft + 1) * 128], rhs=h2[:, cs],
                                                      start=True, stop=True), reads=[w3, h2], writes=[ps])
                        k.op("dve", lambda e: e.tensor_tensor(out=dstt[:, cs], in0=ps[:], in1=win[:, cs], op=ALU.mult),
                             reads=[ps, win], writes=[dstt])
                    if half == 1:
                        k.op("dve", lambda e: e.memset(dstt[:, 0:1], 0.0), writes=[dstt])
                    k.op("dve", lambda e: e.tensor_reduce(out=nrm[:, half:half + 1], in_=dstt[:, :], op=ALU.add,
                                                          axis=AX.X, apply_absolute_value=True),
                         reads=[dstt], writes=[nrm])
                k.op("dve", lambda e: e.tensor_tensor(out=tot[:], in0=nrm[:, 0:1], in1=nrm[:, 1:2], op=ALU.add),
                     reads=[nrm], writes=[tot])
                k.op("dve", lambda e: e.reciprocal(out=tot[:], in_=tot[:]), reads=[tot], writes=[tot])
                k.op("dve", lambda e: e.tensor_scalar(out=tot[:], in0=tot[:], scalar1=2.0 / (2 * T), scalar2=None, op0=ALU.mult),
                     reads=[tot], writes=[tot])
                k.op("dve", lambda e: e.tensor_tensor(out=tmp[:], in0=hfb[0][:], in1=hfb[1][:], op=ALU.add),
                     reads=hfb, writes=[tmp])
                k.op("dve", lambda e: e.tensor_scalar(out=kpm[0][:, ci, :], in0=tmp[:], scalar1=tot[:, 0:1],
                                                      scalar2=None, op0=ALU.mult), reads=[tmp, tot], writes=[kpm[0]])
                k.op("dve", lambda e: e.tensor_tensor(out=tmp[:], in0=hfb[0][:], in1=hfb[1][:], op=ALU.subtract),
                     reads=hfb, writes=[tmp])
                k.op("dve", lambda e: e.tensor_scalar(out=kpm[1][:, ci, :], in0=tmp[:], scalar1=tot[:, 0:1],
                                                      scalar2=None, op0=ALU.mult), reads=[tmp, tot], writes=[kpm[1]])
        with k.scope() as sc:
            kT = [sc.sb("kT", [128, 16, 512], BF16) for _ in range(2)]
            for i in range(2):
                transpose_to_tokmajor(k, C, kT[i], lambda ci, i=i: kpm[i][:, ci, :], 4, kpm[i])
            ko = [sc.sb("ko", [128, 512], F32) for _ in range(4)]
            cnt = [0]

            def consume(j, bA, bB):
                for part, bk in enumerate((bA, bB)):
                    o = ko[cnt[0] % 4]
                    evac(k, cnt[0], o[:], bk[:], [bk], [o])
                    k.dma(C.KTAB[l, part * 16 + j], o[:], reads=[o], writes=[C.u_KTAB[l]])
                    cnt[0] += 1
            dft_forward(k, C, sc, lambda part: (kT[part], kT[part]), consume)


def stage_hyena(k, C, l):
    W = C.W
    with k.scope() as sc0:
        X0 = sc0.sb("hyX0", [128, 4, T], BF16)
        S = sc0.sb("hyS", [128, 4, T], BF16)
        Y = sc0.sb("hyY", [128, 32, 512], BF16)
        bias = sc0.sb("hybias", [128, 1, 4], F32)
        load_featvecs(k, C, bias, [W["hy_bias"][l]], 4)
        with k.scope() as sc:
            cw = sc.sb("hycw", [128, 4, 12], F32)
            load_featvecs(k, C, cw, [W["hy_short_w"][l, 0], W["hy_short_w"][l, 1], W["hy_short_w"][l, 2],
                                     W["hy_short_b"][l]], 12)
            X1 = sc.sb("hyX1", [128, 4, T], BF16)
            gin = [sc.sb("hyg", [128, T + 2], BF16) for _ in range(2)]
            t1 = sc.sb("hyt1", [128, T], F32)
            t2 = sc.sb("hyt2", [128, T], F32)
            for g in gin:
                k.op("pool", lambda e: e.memset(g[:, 0:1], 0.0), writes=[g])
                k.op("pool", lambda e: e.memset(g[:, T + 1:T + 2], 0.0), writes=[g])
            for fi in range(12):
                g = gin[fi % 2]
                k.dma(g[:, 1:T + 1], C.PT[fi * 128:(fi + 1) * 128, :], reads=[C.u_PT[fi * 128]], writes=[g])
                k.op("dve", lambda e: e.tensor_scalar(out=t1[:], in0=g[:, 1:T + 1], scalar1=cw[:, 1, fi:fi + 1],
                                                      scalar2=cw[:, 3, fi:fi + 1], op0=ALU.mult, op1=ALU.add),
                     reads=[g, cw], writes=[t1])
                k.op("dve", lambda e: e.scalar_tensor_tensor(out=t2[:], in0=g[:, 0:T], scalar=cw[:, 0, fi:fi + 1],
                                                             op0=ALU.mult, in1=t1[:], op1=ALU.add),
                     reads=[g, t1, cw], writes=[t2])
                if fi < 8:
                    dst = X0[:, fi, :] if fi < 4 else X1[:, fi - 4, :]
                    du = X0 if fi < 4 else X1
                    k.op("dve", lambda e: e.scalar_tensor_tensor(out=dst, in0=g[:, 2:T + 2], scalar=cw[:, 2, fi:fi + 1],
                                                                 op0=ALU.mult, in1=t2[:], op1=ALU.add),
                         reads=[g, t2, cw], writes=[du])
                else:
                    k.op("dve", lambda e: e.scalar_tensor_tensor(out=t1[:], in0=g[:, 2:T + 2], scalar=cw[:, 2, fi:fi + 1],
                                                                 op0=ALU.mult, in1=t2[:], op1=ALU.add),
                         reads=[g, t2, cw], writes=[t1])
                    k.op("pool", lambda e: e.tensor_tensor(out=S[:, fi - 8, :], in0=t1[:], in1=X1[:, fi - 8, :], op=ALU.mult),
                         reads=[t1, X1], writes=[S])
        with k.scope() as sc:
            sT = sc.sb("hysT", [128, 16, 512], BF16)
            transpose_to_tokmajor(k, C, sT, lambda ci: S[:, ci, :], 4, S)
            kt_ = [sc.sb("hykt", [128, 2, 512], F32) for _ in range(2)]
            pw = [sc.sb("hypw", [128, 512], F32) for _ in range(4)]

            def consume(j, bA, bB):
                kc = kt_[j % 2]
                k.dma(kc[:, 0, :], C.KTAB[l, j], reads=[C.u_KTAB[l]], writes=[kc])
                k.dma(kc[:, 1, :], C.KTAB[l, 16 + j], reads=[C.u_KTAB[l]], writes=[kc])
                k.op("dve", lambda e: e.tensor_tensor(out=pw[0][:], in0=bA[:], in1=kc[:, 0, :], op=ALU.mult), reads=[bA, kc], writes=[pw[0]])
                k.op("dve", lambda e: e.tensor_tensor(out=pw[1][:], in0=bB[:], in1=kc[:, 1, :], op=ALU.mult), reads=[bB, kc], writes=[pw[1]])
                k.op("dve", lambda e: e.tensor_tensor(out=pw[2][:], in0=bA[:], in1=kc[:, 1, :], op=ALU.mult), reads=[bA, kc], writes=[pw[2]])
                k.op("dve", lambda e: e.tensor_tensor(out=pw[3][:], in0=bB[:], in1=kc[:, 0, :], op=ALU.mult), reads=[bB, kc], writes=[pw[3]])
                k.op("pool", lambda e: e.tensor_tensor(out=Y[:, j, :], in0=pw[0][:], in1=pw[1][:], op=ALU.subtract), reads=[pw[0], pw[1]], writes=[Y])
                k.op("pool", lambda e: e.tensor_tensor(out=Y[:, 16 + j, :], in0=pw[2][:], in1=pw[3][:], op=ALU.add), reads=[pw[2], pw[3]], writes=[Y])
            dft_forward(k, C, sc, lambda part: (sT, sT), consume)
        with k.scope() as sc:
            di = [sc.sb("hydi", [128, 8, 512], BF16) for _ in range(3)]
            u_ = [sc.sb("hyu", [128, 512], F32) for _ in range(2)]
            yo = [sc.sb("hyyo", [128, 512], BF16) for _ in range(2)]
            nld = 0
            ne = 0
            for tt in range(4):
                ts_ = slice(tt * 512, (tt + 1) * 512)
                banks = C.PS[(tt % 2) * 4:(tt % 2) * 4 + 4]
                for fg in range(4):
                    d = di[nld % 3]
                    nld += 1
                    k.dma(d[:], C.c_dfti[fg * 8:(fg + 1) * 8, :, ts_].rearrange("f p t -> p f t"), writes=[d])
                    for fo in range(8):
                        ft = fg * 8 + fo
                        for ci in range(4):
                            k.op("pe", lambda e: e.matmul(banks[ci][:], lhsT=Y[:, ft, ci * 128:(ci + 1) * 128], rhs=d[:, fo, :],
                                                          start=(ft == 0), stop=(ft == 31)),
                                 reads=[Y, d], writes=[banks[ci]], inc=(ft == 31 or (fo == 7 and ci == 3)))
                for ci in range(4):
                    u = u_[ne % 2]
                    o = yo[ne % 2]
                    ne += 1
                    k.op("dve", lambda e: e.scalar_tensor_tensor(out=u[:], in0=S[:, ci, ts_], scalar=bias[:, 0, ci:ci + 1],
                                                                 op0=ALU.mult, in1=banks[ci][:], op1=ALU.add),
                         reads=[S, bias, banks[ci]], writes=[u])
                    k.op("pool", lambda e: e.tensor_tensor(out=o[:], in0=u[:], in1=X0[:, ci, ts_], op=ALU.mult),
                         reads=[u, X0], writes=[o])
                    k.dma(C.YT[ci * 128:(ci + 1) * 128, ts_], o[:], reads=[o], writes=[C.u_YT[ci]])


NCH = 16
NH = 16


def ssd_consts():
    tri = np.triu(np.ones((128, 128), np.float32))
    return {"c_triF": tri, "c_triB": np.ascontiguousarray(tri.T)}


def ssd_context(k, C):
    ein = lambda n, s, dt=F32: k.dram(n, s, dt, kind="ExternalInput")
    C.c_triF = ein("c_triF", [128, 128])
    C.c_triB = ein("c_triB", [128, 128])
    for nm, shp in (("ssd_conv_w", [DEPTH, 5, 1536]), ("ssd_conv_b", [DEPTH, 1536]), ("ssd_a_log", [DEPTH, 2, 16]),
                    ("ssd_dt_bias", [DEPTH, 2, 16]), ("ssd_d", [DEPTH, 16]), ("ssd_norm", [DEPTH, 1024])):
        C.W[nm] = ein(nm, shp)
    kind = "ExternalOutput" if C.dbg else "Internal"
    C.YR = k.dram("YR", [1024, T], F32, kind=kind)
    C.u_YR = units("YR", 8)
    C.triF = k.sb("triF", [128, 128], F32)
    C.triB = k.sb("triB", [128, 128], F32)
    C.onesF = k.sb("onesF", [128, 128], F32)
    k.dma(C.triF[:], C.c_triF[:, :], writes=[C.triF])
    k.dma(C.triB[:], C.c_triB[:, :], writes=[C.triB])
    k.op("dve", lambda e: e.memset(C.onesF[:], 1.0), writes=[C.onesF])


def stage_ssd(k, C, l):
    W = C.W
    with k.scope() as sc0:
        xT = sc0.sb("sxT", [128, NCH, 1024], BF16)
        BT = sc0.sb("sBT", [128, NCH, 256], BF16)
        BCf = sc0.sb("sBCf", [128, 4, T], BF16)
        dtT = sc0.sb("sdtT", [128, NCH, 32], F32)
        aT = sc0.sb("saT", [128, NCH, 32], F32)
        acs = sc0.sb("sacs", [128, NCH, 32], F32)
        tot = sc0.sb("stot", [128, NCH, 32], F32)
        etot = sc0.sb("setot", [128, NCH, 32], F32)
        eacs = sc0.sb("seacs", [128, NCH, 32], F32)
        Wst = sc0.sb("sWst", [128, NCH, 32], F32)
        dvec = sc0.sb("sdvec", [128, 16], F32)
        k.dma(dvec[:], W["ssd_d"][l].partition_broadcast(128), writes=[dvec])
        with k.scope() as sc:
            cw = sc.sb("scw", [128, 6, 12], F32)
            load_featvecs(k, C, cw, [W["ssd_conv_w"][l, i] for i in range(5)] + [W["ssd_conv_b"][l]], 12)
            gin = [sc.sb("sg", [128, T + 4], BF16) for _ in range(2)]
            t1 = sc.sb("st1", [128, T], F32)
            t2 = sc.sb("st2", [128, T], F32)
            xc = [sc.sb("sxc", [128, T], BF16) for _ in range(2)]
            for g in gin:
                k.op("pool", lambda e: e.memset(g[:, 0:2], 0.0), writes=[g])
                k.op("pool", lambda e: e.memset(g[:, T + 2:T + 4], 0.0), writes=[g])
            for fi in range(12):
                g = gin[fi % 2]
                c0 = C_XBC + fi * 128
                k.dma(g[:, 2:T + 2], C.PT[c0:c0 + 128, :], reads=[C.u_PT[c0]], writes=[g])
                k.op("dve", lambda e: e.tensor_scalar(out=t1[:], in0=g[:, 2:T + 2], scalar1=cw[:, 2, fi:fi + 1],
                                                      scalar2=cw[:, 5, fi:fi + 1], op0=ALU.mult, op1=ALU.add),
                     reads=[g, cw], writes=[t1])
                src, dst = t1, t2
                for tap in (0, 1, 3, 4):
                    k.op("dve", lambda e: e.scalar_tensor_tensor(out=dst[:], in0=g[:, tap:tap + T], scalar=cw[:, tap, fi:fi + 1],
                                                                 op0=ALU.mult, in1=src[:], op1=ALU.add),
                         reads=[g, src, cw], writes=[dst])
                    src, dst = dst, src
                if fi < 8:
                    o = xc[fi % 2]
                    k.op("act", lambda e: e.activation(out=o[:], in_=src[:], func=AF.Silu), reads=[src], writes=[o])
                    for c4 in range(4):
                        ps = C.PS[4 + (c4 % 4)]
                        psb = ps[:].bitcast(BF16)
                        for j in range(4):
                            c = c4 * 4 + j
                            k.op("pe", lambda e: e.transpose(out=psb[:, j * 128:(j + 1) * 128], in_=o[:, c * 128:(c + 1) * 128],
                                                             identity=C.identB[:]), reads=[o, C.identB], writes=[ps], inc=(j == 3))
                        evac(k, c4, xT[:, c4 * 4:(c4 + 1) * 4, fi * 128:(fi + 1) * 128],
                             psb[:, 0:512].rearrange("p (c f) -> p c f", c=4), [ps], [xT])
                else:
                    bi = fi - 8
                    k.op("act", lambda e: e.activation(out=BCf[:, bi, :], in_=src[:], func=AF.Silu), reads=[src], writes=[BCf])
                    if bi < 2:
                        for c4 in range(4):
                            ps = C.PS[4 + (c4 % 4)]
                            psb = ps[:].bitcast(BF16)
                            for j in range(4):
                                c = c4 * 4 + j
                                k.op("pe", lambda e: e.transpose(out=psb[:, j * 128:(j + 1) * 128],
                                                                 in_=BCf[:, bi, c * 128:(c + 1) * 128],
                                                                 identity=C.identB[:]), reads=[BCf, C.identB], writes=[ps], inc=(j == 3))
                            evac(k, c4, BT[:, c4 * 4:(c4 + 1) * 4, bi * 128:(bi + 1) * 128],
                                 psb[:, 0:512].rearrange("p (c f) -> p c f", c=4), [ps], [BT])
            dtf = sc.sb("sdtf", [32, T], F32)
            af = sc.sb("saf", [32, T], F32)
            sv = sc.sb("ssv", [32, 3], F32)
            k.dma(sv[:, 0:1], W["ssd_dt_bias"][l].rearrange("d (h o) -> (d h) o", o=1), writes=[sv])
            k.dma(sv[:, 1:2], W["ssd_a_log"][l].rearrange("d (h o) -> (d h) o", o=1), writes=[sv])
            k.op("dve", lambda e: e.memset(sv[:, 2:3], 1.0), writes=[sv])
            k.op("act", lambda e: e.activation(out=sv[:, 1:2], in_=sv[:, 1:2], func=AF.Exp), reads=[sv], writes=[sv])
            k.op("dve", lambda e: e.tensor_scalar(out=sv[:, 1:2], in0=sv[:, 1:2], scalar1=-1.0, scalar2=None, op0=ALU.mult),
                 reads=[sv], writes=[sv])
            k.dma(dtf[:], C.DT[:, :], reads=[C.u_PT[C_DT]], writes=[dtf])
            k.op("act", lambda e: e.activation(out=dtf[:], in_=dtf[:], func=AF.Exp, bias=sv[:, 0:1]), reads=[dtf, sv], writes=[dtf])
            k.op("act", lambda e: e.activation(out=dtf[:], in_=dtf[:], func=AF.Ln, bias=sv[:, 2:3]), reads=[dtf, sv], writes=[dtf])
            k.op("dve", lambda e: e.tensor_scalar(out=af[:], in0=dtf[:], scalar1=sv[:, 1:2], scalar2=None, op0=ALU.mult),
                 reads=[dtf, sv], writes=[af])
            for srcf, dstT in ((dtf, dtT), (af, aT)):
                ps = C.PS[0] if srcf is dtf else C.PS[1]
                for c in range(NCH):
                    k.op("pe", lambda e: e.transpose(out=ps[:, c * 32:(c + 1) * 32], in_=srcf[0:32, c * 128:(c + 1) * 128],
                                                     identity=C.identF[0:32, 0:32]), reads=[srcf, C.identF], writes=[ps],
                         inc=(c == NCH - 1))
                k.op("dve", lambda e: e.tensor_copy(out=dstT[:].rearrange("p c d -> p (c d)"), in_=ps[:]), reads=[ps], writes=[dstT])
            ps = C.PS[2]
            ps2 = C.PS[3]
            for c in range(NCH):
                k.op("pe", lambda e: e.matmul(ps[:, c * 32:c * 32 + 16], lhsT=C.triF[:], rhs=aT[:, c, 0:16], start=True, stop=True),
                     reads=[C.triF, aT], writes=[ps], inc=False)
                k.op("pe", lambda e: e.matmul(ps[:, c * 32 + 16:c * 32 + 32], lhsT=C.triB[:], rhs=aT[:, c, 16:32], start=True, stop=True),
                     reads=[C.triB, aT], writes=[ps], inc=False)
                k.op("pe", lambda e: e.matmul(ps2[:, c * 32:(c + 1) * 32], lhsT=C.onesF[:], rhs=aT[:, c, :], start=True, stop=True),
                     reads=[C.onesF, aT], writes=[ps2], inc=(c == NCH - 1))
            fl = lambda t_: t_[:].rearrange("p c d -> p (c d)")
            k.op("dve", lambda e: e.tensor_copy(out=fl(acs), in_=ps[:]), reads=[ps], writes=[acs])
            k.op("dve", lambda e: e.tensor_copy(out=fl(tot), in_=ps2[:]), reads=[ps2], writes=[tot])
            k.op("act", lambda e: e.activation(out=fl(etot), in_=fl(tot), func=AF.Exp), reads=[tot], writes=[etot])
            k.op("act", lambda e: e.activation(out=fl(eacs), in_=fl(acs), func=AF.Exp), reads=[acs], writes=[eacs])
            k.op("dve", lambda e: e.tensor_tensor(out=fl(Wst), in0=fl(tot), in1=fl(acs), op=ALU.subtract), reads=[tot, acs], writes=[Wst])
            k.op("act", lambda e: e.activation(out=fl(Wst), in_=fl(Wst), func=AF.Exp), reads=[Wst], writes=[Wst])
            k.op("dve", lambda e: e.tensor_tensor(out=fl(Wst), in0=fl(Wst), in1=fl(dtT), op=ALU.mult), reads=[Wst, dtT], writes=[Wst])
        with k.scope() as sc:
            nextS = sc.sb("snext", [128, NCH, 2, 512], BF16)
            st = [sc.sb("sst", [128, 2, 512], F32) for _ in range(2)]
            stb = sc.sb("sstb", [128, 2, 512], BF16)
            xw = [sc.sb("sxw", [128, 1024], BF16) for _ in range(2)]
            for d_ in range(2):
                k.op("pool", lambda e: e.memset(st[d_][:], 0.0), writes=[st[d_]])
            k.op("pool", lambda e: e.memset(nextS[:, NCH - 1, :, :], 0.0), writes=[nextS])

            def state_update(c, dirn, n):
                xw_ = xw[n % 2]
                for h in range(NH):
                    dh = dirn * 16 + h
                    if h % 2 == 0:
                        k.op("dve", lambda e: e.tensor_scalar(out=xw_[:, h * 64:(h + 1) * 64], in0=xT[:, c, h * 64:(h + 1) * 64],
                                                              scalar1=Wst[:, c, dh:dh + 1], scalar2=None, op0=ALU.mult),
                             reads=[xT, Wst], writes=[xw_])
                    else:
                        k.op("act", lambda e: e.activation(out=xw_[:, h * 64:(h + 1) * 64], in_=xT[:, c, h * 64:(h + 1) * 64],
                                                           func=AF.Copy, scale=Wst[:, c, dh:dh + 1]),
                             reads=[xT, Wst], writes=[xw_])
                for g in range(2):
                    ps = C.PS[6 + g]
                    k.op("pe", lambda e: e.matmul(ps[:], lhsT=BT[:, c, g * 128:(g + 1) * 128], rhs=xw_[:, g * 512:(g + 1) * 512],
                                                  start=True, stop=True), reads=[BT, xw_], writes=[ps])
                    for hh in range(8):
                        dh = dirn * 16 + g * 8 + hh
                        hs = slice(hh * 64, (hh + 1) * 64)
                        k.op("dve", lambda e: e.scalar_tensor_tensor(out=st[dirn][:, g, hs], in0=st[dirn][:, g, hs],
                                                                     scalar=etot[:, c, dh:dh + 1], op0=ALU.mult,
                                                                     in1=ps[:, hs], op1=ALU.add),
                             reads=[st[dirn], etot, ps], writes=[st[dirn]])
            n = 0
            for c in range(NCH - 1, 0, -1):
                state_update(c, 1, n)
                n += 1
                k.op("act", lambda e: e.activation(out=nextS[:, c - 1, :, :], in_=st[1][:], func=AF.Copy), reads=[st[1]], writes=[nextS])
            SC = [sc.sb("sSC", [128, 2, 128], F32) for _ in range(2)]
            NR = 6
            Rr = [sc.sb("sRr", [128, 128], F32) for _ in range(NR)]
            Dd = [sc.sb("sDd", [128, 128], F32) for _ in range(NR)]
            Mx = [sc.sb("sMx", [128, 128], F32) for _ in range(NR)]
            Mb = [sc.sb("sMb", [128, 128], BF16) for _ in range(4)]
            ytok = sc.sb("sytok", [128, 1024], F32)
            yTo = [sc.sb("syTo", [128, 512], F32) for _ in range(2)]
            nu = 0
            for c in range(NCH):
                cs = slice(c * 128, (c + 1) * 128)
                for g in range(2):
                    ps = C.PS[g]
                    k.op("pe", lambda e: e.matmul(ps[:, 0:128], lhsT=BCf[:, g, cs], rhs=BCf[:, 2 + g, cs], start=True, stop=True),
                         reads=[BCf], writes=[ps])
                    k.op("dve", lambda e: e.tensor_tensor(out=SC[0][:, g, :], in0=ps[:, 0:128], in1=C.triF[:], op=ALU.mult),
                         reads=[ps, C.triF], writes=[SC[0]])
                    k.op("dve", lambda e: e.tensor_tensor(out=SC[1][:, g, :], in0=ps[:, 0:128], in1=C.triB[:], op=ALU.mult),
                         reads=[ps, C.triB], writes=[SC[1]])
                ydg = [C.PS[6], C.PS[7]]

                def stage_a(h):
                    for dirn in range(2):
                        dh = dirn * 16 + h
                        tri = C.triF if dirn == 0 else C.triB
                        ui = (h * 2 + dirn) % NR
                        R = Rr[ui]
                        pq = C.PS[ui]
                        k.op("act", lambda e: e.activation(out=R[:], in_=tri[:], func=AF.Copy, scale=aT[:, c, dh:dh + 1]),
                             reads=[tri, aT], writes=[R])
                        k.op("pe", lambda e: e.matmul(pq[:, 0:128], lhsT=C.onesF[:], rhs=R[:], start=True, stop=True),
                             reads=[C.onesF, R], writes=[pq])

                def stage_b(h):
                    g = h // 8
                    mts = []
                    for dirn in range(2):
                        dh = dirn * 16 + h
                        ui = (h * 2 + dirn) % NR
                        Dm = Dd[ui]
                        M = Mx[ui]
                        pq = C.PS[ui]
                        k.op("act", lambda e: e.activation(out=Dm[:], in_=pq[:, 0:128], func=AF.Relu, scale=-1.0,
                                                           bias=acs[:, c, dh:dh + 1]), reads=[pq, acs], writes=[Dm])
                        k.op("act", lambda e: e.activation(out=Dm[:], in_=Dm[:], func=AF.Exp, scale=-1.0), reads=[Dm], writes=[Dm])
                        k.op("dve", lambda e: e.scalar_tensor_tensor(out=M[:], in0=Dm[:], scalar=dtT[:, c, dh:dh + 1], op0=ALU.mult,
                                                                     in1=SC[dirn][:, g, :], op1=ALU.mult),
                             reads=[Dm, dtT, SC[dirn]], writes=[M])
                        mts.append(M)
                    mb = Mb[h % 4]
                    k.op("pool", lambda e: e.tensor_tensor(out=mb[:], in0=mts[0][:], in1=mts[1][:], op=ALU.add),
                         reads=mts, writes=[mb])
                    k.op("pe", lambda e: e.matmul(ydg[g][:, (h % 8) * 64:(h % 8 + 1) * 64], lhsT=mb[:], rhs=xT[:, c, h * 64:(h + 1) * 64],
                                                  start=True, stop=True), reads=[mb, xT], writes=[ydg[g]])
                LOOK = 2
                for h in range(min(LOOK, NH)):
                    stage_a(h)
                for h in range(NH):
                    if h + LOOK < NH:
                        stage_a(h + LOOK)
                    stage_b(h)
                k.op("act", lambda e: e.activation(out=stb[:], in_=st[0][:], func=AF.Copy), reads=[st[0]], writes=[stb])
                yoff = [[C.PS[2], C.PS[3]], [C.PS[4], C.PS[5]]]
                for g in range(2):
                    k.op("pe", lambda e: e.matmul(yoff[0][g][:], lhsT=BCf[:, 2 + g, cs], rhs=stb[:, g, :], start=True, stop=True),
                         reads=[BCf, stb], writes=[yoff[0][g]])
                    k.op("pe", lambda e: e.matmul(yoff[1][g][:], lhsT=BCf[:, 2 + g, cs], rhs=nextS[:, c, g, :], start=True, stop=True),
                         reads=[BCf, nextS], writes=[yoff[1][g]])
                for g in range(2):
                    k.op("act", lambda e: e.activation(out=ytok[:, g * 512:(g + 1) * 512], in_=ydg[g][:], func=AF.Copy),
                         reads=[ydg[g]], writes=[ytok])
                for h in range(NH):
                    g = h // 8
                    hs = slice((h % 8) * 64, (h % 8 + 1) * 64)
                    ys = ytok[:, h * 64:(h + 1) * 64]
                    k.op("dve", lambda e: e.scalar_tensor_tensor(out=ys, in0=yoff[0][g][:, hs], scalar=eacs[:, c, h:h + 1],
                                                                 op0=ALU.mult, in1=ys, op1=ALU.add),
                         reads=[yoff[0][g], eacs, ytok], writes=[ytok])
                    k.op("dve", lambda e: e.scalar_tensor_tensor(out=ys, in0=yoff[1][g][:, hs], scalar=eacs[:, c, 16 + h:17 + h],
                                                                 op0=ALU.mult, in1=ys, op1=ALU.add),
                         reads=[yoff[1][g], eacs, ytok], writes=[ytok])
                    k.op("dve", lambda e: e.scalar_tensor_tensor(out=ys, in0=xT[:, c, h * 64:(h + 1) * 64], scalar=dvec[:, h:h + 1],
                                                                 op0=ALU.mult, in1=ys, op1=ALU.add),
                         reads=[xT, dvec, ytok], writes=[ytok])
                for half in range(2):
                    ps = C.PS[half]
                    for j in range(4):
                        ft = half * 4 + j
                        k.op("pe", lambda e: e.transpose(out=ps[:, j * 128:(j + 1) * 128], in_=ytok[:, ft * 128:(ft + 1) * 128],
                                                         identity=C.identF[:]), reads=[ytok, C.identF], writes=[ps], inc=(j == 3))
                    o = yTo[half]
                    evac(k, half, o[:], ps[:], [ps], [o])
                    for j in range(4):
                        ft = half * 4 + j
                        k.dma(C.YR[ft * 128:(ft + 1) * 128, cs], o[:, j * 128:(j + 1) * 128], reads=[o], writes=[C.u_YR[ft]])
                if c < NCH - 1:
                    state_update(c, 0, n)
                    n += 1
    with k.scope() as sc:
        nw = sc.sb("snw", [128, 1, 8], F32)
        load_featvecs(k, C, nw, [W["ssd_norm"][l]], 8)
        ys_ = [sc.sb("sy", [128, 8, 512], F32) for _ in range(2)]
        zs_ = [sc.sb("sz", [128, 8, 512], BF16) for _ in range(2)]
        zg = [sc.sb("szg", [128, 8, 512], F32) for _ in range(2)]
        sq = [sc.sb("ssq", [128, 8, 512], BF16) for _ in range(2)]
        rs = [sc.sb("srs", [128, 512], F32) for _ in range(2)]
        yo = [sc.sb("syo", [128, 512], BF16) for _ in range(2)]
        ne = 0
        for ct in range(4):
            cs = slice(ct * 512, (ct + 1) * 512)
            y = ys_[ct % 2]
            z = zs_[ct % 2]
            zz = zg[ct % 2]
            k.dma(y[:], C.YR[:, cs].rearrange("(n p) t -> p n t", p=128), reads=C.u_YR, writes=[y])
            for j in range(8):
                c0 = C_Z + j * 128
                k.dma(z[:, j, :], C.PT[c0:c0 + 128, cs], reads=[C.u_PT[c0]], writes=[z])
            k.op("act", lambda e: e.activation(out=zz[:], in_=z[:], func=AF.Silu), reads=[z], writes=[zz])
            k.op("pool", lambda e: e.tensor_tensor(out=y[:], in0=y[:], in1=zz[:], op=ALU.mult), reads=[y, zz], writes=[y])
            r = rstd_cols(k, C, (sq, rs), ct, y)
            for j in range(8):
                o = yo[ne % 2]
                ne += 1
                k.op("dve", lambda e: e.scalar_tensor_tensor(out=o[:], in0=y[:, j, :], scalar=nw[:, 0, j:j + 1], op0=ALU.mult,
                                                             in1=r[:], op1=ALU.mult), reads=[y, nw, r], writes=[o])
                k.dma(C.YT[1024 + j * 128:1024 + (j + 1) * 128, cs], o[:], reads=[o], writes=[C.u_YT[8 + j]])


S5_BLK = 64
S5_NB = T // S5_BLK
HALF_PI = 0.5 * math.pi


def s5_context(k, C):
    kind = "ExternalOutput" if C.dbg else "Internal"
    C.S5BS = k.dram("S5BS", [DEPTH, 128, 64, 128], BF16, kind=kind)
    C.S5CF = k.dram("S5CF", [DEPTH, 128, 64, 128], BF16, kind=kind)
    C.S5AB = k.dram("S5AB", [DEPTH, 128, 2, 64], F32, kind=kind)
    C.u_S5 = [Unit("S5tab%d" % l) for l in range(DEPTH)]


def reduce_angle(k, sc, dst, src, shift, shape):
    t = sc.sb("rat", shape, F32)
    y = sc.sb("ray", shape, F32)
    k.op("dve", lambda e: e.tensor_scalar(out=y[:], in0=src[:], scalar1=shift, scalar2=None, op0=ALU.add), reads=[src], writes=[y])
    k.op("dve", lambda e: e.tensor_scalar(out=t[:], in0=y[:], scalar1=1.0 / TWO_PI, scalar2=MAGIC, op0=ALU.mult, op1=ALU.add),
         reads=[y], writes=[t])
    k.op("dve", lambda e: e.tensor_scalar(out=t[:], in0=t[:], scalar1=-MAGIC, scalar2=None, op0=ALU.add), reads=[t], writes=[t])
    k.op("dve", lambda e: e.scalar_tensor_tensor(out=dst[:], in0=t[:], scalar=-TWO_PI, op0=ALU.mult, in1=y[:], op1=ALU.add),
         reads=[t, y], writes=[dst])


def stage_s5_tables(k, C, l):
    W = C.W
    tt = lambda eng, out, a, b, op, rd, wr: k.op(eng, lambda e: e.tensor_tensor(out=out, in0=a, in1=b, op=op), reads=rd, writes=wr)
    with k.scope() as sc:
        A = sc.sb("s5A", [128, 2, 32], F32)
        load_featvecs(k, C, A, [W["s5_a_re"][l].rearrange("d g n -> (d g n)"),
                                W["s5_a_im"][l].rearrange("d g n -> (d g n)")], 32)
        ld = sc.sb("s5ld", [1, 64], F32)
        k.dma(ld[:], W["s5_log_dt"][l].rearrange("d (g o) -> o (d g)", o=1), writes=[ld])
        ps = C.PS[0]
        k.op("pe", lambda e: e.matmul(ps[:, 0:64], lhsT=C.onesF[0:1, :], rhs=ld[0:1, :], start=True, stop=True),
             reads=[C.onesF, ld], writes=[ps])
        dt = sc.sb("s5dt", [128, 32], F32)
        psv = ps[:, 0:64].rearrange("p (d P g) -> p d P g", d=2, g=2)
        for g2 in range(2):
            k.op("dve", lambda e: e.tensor_copy(out=dt[g2 * 64:(g2 + 1) * 64, :].rearrange("p (d P) -> p d P", d=2),
                                                in_=psv[g2 * 64:(g2 + 1) * 64, :, :, g2]), reads=[ps], writes=[dt])
        k.op("act", lambda e: e.activation(out=dt[:], in_=dt[:], func=AF.Exp), reads=[dt], writes=[dt])
        mag = sc.sb("s5mag", [128, 32], F32)
        ang = sc.sb("s5ang", [128, 32], F32)
        tt("dve", mag[:], A[:, 0, :], dt[:], ALU.mult, [A, dt], [mag])
        k.op("act", lambda e: e.activation(out=mag[:], in_=mag[:], func=AF.Exp), reads=[mag], writes=[mag])
        tt("dve", ang[:], A[:, 1, :], dt[:], ALU.mult, [A, dt], [ang])
        sn = sc.sb("s5sn", [128, 32], F32)
        cs_ = sc.sb("s5cs", [128, 32], F32)
        reduce_angle(k, sc, sn, ang, 0.0, [128, 32])
        reduce_angle(k, sc, cs_, ang, HALF_PI, [128, 32])
        k.op("act", lambda e: e.activation(out=sn[:], in_=sn[:], func=AF.Sin), reads=[sn], writes=[sn])
        k.op("act", lambda e: e.activation(out=cs_[:], in_=cs_[:], func=AF.Sin), reads=[cs_], writes=[cs_])
        abr = sc.sb("s5abr", [128, 32], F32)
        abi = sc.sb("s5abi", [128, 32], F32)
        tt("dve", abr[:], mag[:], cs_[:], ALU.mult, [mag, cs_], [abr])
        tt("dve", abi[:], mag[:], sn[:], ALU.mult, [mag, sn], [abi])
        AB = sc.sb("s5AB", [128, 2, 64], F32)
        k.op("dve", lambda e: e.tensor_copy(out=AB[:, 0, 0:32], in_=abr[:]), reads=[abr], writes=[AB])
        k.op("dve", lambda e: e.tensor_copy(out=AB[:, 0, 32:64], in_=abr[:]), reads=[abr], writes=[AB])
        k.op("dve", lambda e: e.tensor_scalar(out=AB[:, 1, 0:32], in0=abi[:], scalar1=-1.0, scalar2=None, op0=ALU.mult),
             reads=[abi], writes=[AB])
        k.op("dve", lambda e: e.tensor_copy(out=AB[:, 1, 32:64], in_=abi[:]), reads=[abi], writes=[AB])
        k.dma(C.S5AB[l], AB[:], reads=[AB], writes=[C.u_S5[l]])
        den = sc.sb("s5den", [128, 32], F32)
        t1 = sc.sb("s5t1", [128, 32], F32)
        t2 = sc.sb("s5t2", [128, 32], F32)
        nr = sc.sb("s5nr", [128, 32], F32)
        fr = sc.sb("s5fr", [128, 32, 1], F32)
        fi = sc.sb("s5fi", [128, 32, 1], F32)
        tt("dve", den[:], A[:, 0, :], A[:, 0, :], ALU.mult, [A], [den])
        tt("dve", t1[:], A[:, 1, :], A[:, 1, :], ALU.mult, [A], [t1])
        tt("dve", den[:], den[:], t1[:], ALU.add, [den, t1], [den])
        k.op("dve", lambda e: e.reciprocal(out=den[:], in_=den[:]), reads=[den], writes=[den])
        k.op("dve", lambda e: e.tensor_scalar(out=nr[:], in0=abr[:], scalar1=-1.0, scalar2=None, op0=ALU.add), reads=[abr], writes=[nr])
        tt("dve", t1[:], nr[:], A[:, 0, :], ALU.mult, [nr, A], [t1])
        tt("dve", t2[:], abi[:], A[:, 1, :], ALU.mult, [abi, A], [t2])
        tt("dve", t1[:], t1[:], t2[:], ALU.add, [t1, t2], [t1])
        tt("dve", fr[:, :, 0], t1[:], den[:], ALU.mult, [t1, den], [fr])
        tt("dve", t1[:], abi[:], A[:, 0, :], ALU.mult, [abi, A], [t1])
        tt("dve", t2[:], nr[:], A[:, 1, :], ALU.mult, [nr, A], [t2])
        tt("dve", t1[:], t1[:], t2[:], ALU.subtract, [t1, t2], [t1])
        tt("dve", fi[:, :, 0], t1[:], den[:], ALU.mult, [t1, den], [fi])
        Bre = sc.sb("s5Bre", [128, 32, 16], F32)
        Bim = sc.sb("s5Bim", [128, 32, 16], F32)
        k.dma(Bre[:], W["s5_b_re"][l].rearrange("d (P g2) n c -> (g2 n) (d P) c", g2=2), writes=[Bre])
        k.dma(Bim[:], W["s5_b_im"][l].rearrange("d (P g2) n c -> (g2 n) (d P) c", g2=2), writes=[Bim])
        frb = fr[:].to_broadcast([128, 32, 16])
        fib = fi[:].to_broadcast([128, 32, 16])
        u1 = sc.sb("s5u1", [128, 32, 16], F32)
        u2 = sc.sb("s5u2", [128, 32, 16], F32)
        E = [sc.sb("s5E", [128, 32, 32], BF16) for _ in range(2)]
        for e_ in E:
            k.op("pool", lambda e: e.memset(e_[:], 0.0), writes=[e_])
        for reim in range(2):
            if reim == 0:
                tt("dve", u1[:], Bre[:], frb, ALU.mult, [Bre, fr], [u1])
                tt("dve", u2[:], Bim[:], fib, ALU.mult, [Bim, fi], [u2])
                tt("dve", u1[:], u1[:], u2[:], ALU.subtract, [u1, u2], [u1])
            else:
                tt("dve", u1[:], Bim[:], frb, ALU.mult, [Bim, fr], [u1])
                tt("dve", u2[:], Bre[:], fib, ALU.mult, [Bre, fi], [u2])
                tt("dve", u1[:], u1[:], u2[:], ALU.add, [u1, u2], [u1])
            for g2 in range(2):
                pr = slice(g2 * 64, (g2 + 1) * 64)
                k.op("dve", lambda e: e.tensor_copy(out=E[reim][pr, :, g2 * 16:(g2 + 1) * 16], in_=u1[pr, :, :]),
                     reads=[u1], writes=[E[reim]])
        BsT = sc.sb("s5BsT", [128, 64, 128], BF16)
        Ez = [sc.sb("s5Ez", [128, 128], BF16) for _ in range(4)]
        for ez in Ez:
            k.op("pool", lambda e: e.memset(ez[:], 0.0), writes=[ez])
        n = 0
        for F in range(4):
            for dirn in range(2):
                for reim in range(2):
                    idx = (F * 2 + dirn) * 2 + reim
                    j0 = dirn * 16 + 4 * F
                    for a in range(4):
                        ez = Ez[a]
                        k.op("dve", lambda e: e.tensor_copy(out=ez[:, a * 32:(a + 1) * 32], in_=E[reim][:, j0 + a, :]),
                             reads=[E[reim]], writes=[ez])
                        ps = C.PS[1 + n % 4]
                        psb = ps[:].bitcast(BF16)
                        k.op("pe", lambda e: e.transpose(out=psb[:, 0:128], in_=ez[:], identity=C.identB[:]),
                             reads=[ez, C.identB], writes=[ps])
                        evac(k, n, BsT[:, idx * 4 + a, :], psb[:, 0:128], [ps], [BsT])
                        n += 1
        k.dma(C.S5BS[l], BsT[:], reads=[BsT], writes=[C.u_S5[l]])
        CfT = sc.sb("s5CfT", [128, 64, 128], BF16)
        k.op("pool", lambda e: e.memset(CfT[:], 0.0), writes=[CfT])
        cin = [sc.sb("s5cin", [128, 2, 64], F32) for _ in range(2)]
        CT = sc.sb("s5CT", [128, 128], F32)
        for reim, wn in enumerate(("s5_c_re", "s5_c_im")):
            for dirn in range(2):
                for h8 in range(2):
                    ci_ = cin[(dirn * 2 + h8) % 2]
                    for pp in range(8):
                        P = h8 * 8 + pp
                        k.dma(ci_[pp * 16:(pp + 1) * 16, :, :],
                              W[wn][l, dirn, 2 * P:2 * P + 2].rearrange("g c n -> c g n"), writes=[ci_])
                    ps = C.PS[5 + (dirn * 2 + h8) % 2]
                    k.op("pe", lambda e: e.transpose(out=ps[:, 0:128], in_=ci_[:].rearrange("p g n -> p (g n)"), identity=C.identF[:]),
                         reads=[ci_, C.identF], writes=[ps])
                    k.op("dve", lambda e: e.tensor_copy(out=CT[:], in_=ps[:, 0:128]), reads=[ps], writes=[CT])
                    for pp in range(8):
                        P = h8 * 8 + pp
                        F, a = P // 4, P % 4
                        idx = ((F * 2 + dirn) * 2 + reim) * 4 + a
                        for g2 in range(2):
                            pr = slice(g2 * 64, (g2 + 1) * 64)
                            k.op("dve", lambda e: e.tensor_scalar(out=CfT[pr, idx, a * 32 + g2 * 16:a * 32 + (g2 + 1) * 16],
                                                                  in0=CT[pr, pp * 16:(pp + 1) * 16],
                                                                  scalar1=(1.0 if reim == 0 else -1.0), scalar2=None, op0=ALU.mult),
                                 reads=[CT], writes=[CfT])
        k.dma(C.S5CF[l], CfT[:], reads=[CfT], writes=[C.u_S5[l]])


def stage_s5(k, C, l):
    W = C.W
    BLK, NB = S5_BLK, S5_NB
    with k.scope() as sc0:
        u = sc0.sb("s5u", [128, 4, T], BF16)
        yfb = [sc0.sb("s5y", [128, 4, T], BF16) for _ in range(2)]
        for F in range(4):
            c0 = C_S5 + F * 128
            k.dma(u[:, F, :], C.PT[c0:c0 + 128, :], reads=[C.u_PT[c0]], writes=[u])
        with k.scope() as sc:
            BsT = sc.sb("s5BsT", [128, 64, 128], BF16)
            CfT = sc.sb("s5CfT", [128, 64, 128], BF16)
            AB = sc.sb("s5AB", [128, 2, 64], F32)
            k.dma(BsT[:], C.S5BS[l], reads=[C.u_S5[l]], writes=[BsT])
            k.dma(CfT[:], C.S5CF[l], reads=[C.u_S5[l]], writes=[CfT])
            k.dma(AB[:], C.S5AB[l], reads=[C.u_S5[l]], writes=[AB])
            S = [sc.sb("s5S", [128, BLK, 64], F32) for _ in range(2)]
            Hf = sc.sb("s5Hf", [128, BLK, 64], F32)
            Hb = [sc.sb("s5Hb", [128, BLK, 64], BF16) for _ in range(2)]
            Hc = sc.sb("s5Hc", [128, 64], F32)
            m1 = sc.sb("s5m1", [128, 64], F32)
            m2 = sc.sb("s5m2", [128, 64], F32)
            k.op("dve", lambda e: e.memset(Hc[:], 0.0), writes=[Hc])
            nps = 0
            for b in range(NB):
                tf0 = b * BLK
                tb0 = T - (b + 1) * BLK
                Sb = S[b % 2]
                for F in range(4):
                    for dirn in range(2):
                        t0 = tf0 if dirn == 0 else tb0
                        ps = C.PS[nps % 4]
                        nps += 1
                        for reim in range(2):
                            idx = (F * 2 + dirn) * 2 + reim
                            for a in range(4):
                                reg = (reim * 4 + a) * BLK
                                k.op("pe", lambda e: e.matmul(ps[:, reg:reg + BLK], lhsT=BsT[:, idx * 4 + a, :],
                                                              rhs=u[:, F, t0:t0 + BLK], start=True, stop=True),
                                     reads=[BsT, u], writes=[ps], inc=(reim == 1 and a == 3))
                        for reim in range(2):
                            col0 = (reim * 2 + dirn) * 16 + 4 * F
                            src = ps[:, reim * 4 * BLK:(reim * 4 + 4) * BLK].rearrange("p (a t) -> p a t", a=4)
                            if dirn == 1:
                                src = src[:, :, ::-1]
                            k.op("act", lambda e: e.activation(out=Sb[:, :, col0:col0 + 4].rearrange("p t a -> p a t"),
                                                               in_=src, func=AF.Copy), reads=[ps], writes=[Sb])
                for i in range(BLK):
                    prev = Hc[:, :] if i == 0 else Hf[:, i - 1, :]
                    pu = Hc if i == 0 else Hf
                    prev_sw = prev.rearrange("p (r c) -> p r c", r=2)[:, ::-1, :]
                    k.op("dve", lambda e: e.tensor_tensor(out=m1[:], in0=prev, in1=AB[:, 0, :], op=ALU.mult), reads=[pu, AB], writes=[m1])
                    k.op("dve", lambda e: e.tensor_tensor(out=m2[:].rearrange("p (r c) -> p r c", r=2), in0=prev_sw,
                                                          in1=AB[:, 1, :].rearrange("p (r c) -> p r c", r=2), op=ALU.mult),
                         reads=[pu, AB], writes=[m2])
                    k.op("dve", lambda e: e.tensor_tensor(out=m1[:], in0=m1[:], in1=m2[:], op=ALU.add), reads=[m1, m2], writes=[m1])
                    k.op("dve", lambda e: e.tensor_tensor(out=Hf[:, i, :], in0=m1[:], in1=Sb[:, i, :], op=ALU.add),
                         reads=[m1, Sb], writes=[Hf])
                k.op("dve", lambda e: e.tensor_copy(out=Hc[:], in_=Hf[:, BLK - 1, :]), reads=[Hf], writes=[Hc])
                hb = Hb[b % 2]
                k.op("act", lambda e: e.activation(out=hb[:], in_=Hf[:], func=AF.Copy), reads=[Hf], writes=[hb])
                for dirn in range(2):
                    t0 = tf0 if dirn == 0 else tb0
                    for F in range(4):
                        ps = C.PS[4 + nps % 4]
                        nps += 1
                        nmm = 0
                        for reim in range(2):
                            for a in range(4):
                                idx = ((F * 2 + dirn) * 2 + reim) * 4 + a
                                col = (reim * 2 + dirn) * 16 + 4 * F + a
                                k.op("pe", lambda e: e.matmul(ps[:, 0:BLK], lhsT=CfT[:, idx, :], rhs=hb[:, :, col],
                                                              start=(nmm == 0), stop=(nmm == 7)),
                                     reads=[CfT, hb], writes=[ps], inc=(nmm == 7))
                                nmm += 1
                        src = ps[:, 0:BLK]
                        if dirn == 1:
                            src = src[:, ::-1]
                        k.op("act", lambda e: e.activation(out=yfb[dirn][:, F, t0:t0 + BLK], in_=src, func=AF.Copy),
                             reads=[ps], writes=[yfb[dirn]])
        with k.scope() as sc:
            dv = sc.sb("s5dv", [128, 2, 4], F32)
            load_featvecs(k, C, dv, [W["s5_d"][l], W["s5_b_glu"][l]], 4)
            gB = sc.sb("s5gB", [128, 4, T], BF16)
            ysum = [sc.sb("s5ys", [128, T], F32) for _ in range(2)]
            for F in range(4):
                ys = ysum[F % 2]
                k.op("pool", lambda e: e.tensor_tensor(out=ys[:], in0=yfb[0][:, F, :], in1=yfb[1][:, F, :], op=ALU.add),
                     reads=yfb, writes=[ys])
                k.op("dve", lambda e: e.scalar_tensor_tensor(out=ys[:], in0=u[:, F, :], scalar=dv[:, 0, F:F + 1], op0=ALU.mult,
                                                             in1=ys[:], op1=ALU.add), reads=[u, dv, ys], writes=[ys])
                k.op("act", lambda e: e.activation(out=gB[:, F, :], in_=ys[:], func=AF.Gelu), reads=[ys], writes=[gB])
            ws = WStream(k, sc, C, [("s5_w_glu", l, m * 128, 128, 4) for m in range(4)], 4, nbuf=2, name="wglu")
            sg = [sc.sb("s5sg", [128, T], F32) for _ in range(2)]
            yo = [sc.sb("s5yo", [128, T], BF16) for _ in range(2)]
            for m in range(4):
                w = ws.get(m)
                banks = C.PS[(m % 2) * 4:(m % 2) * 4 + 4]
                for ct in range(4):
                    for kt in range(4):
                        k.op("pe", lambda e: e.matmul(banks[ct][:], lhsT=w[:, kt, :], rhs=gB[:, kt, ct * 512:(ct + 1) * 512],
                                                      start=(kt == 0), stop=(kt == 3)), reads=[w, gB], writes=[banks[ct]], inc=(kt == 3))
                s_ = sg[m % 2]
                o = yo[m % 2]
                for ct in range(4):
                    k.op("act", lambda e: e.activation(out=s_[:, ct * 512:(ct + 1) * 512], in_=banks[ct][:], func=AF.Sigmoid,
                                                       bias=dv[:, 1, m:m + 1]), reads=[banks[ct], dv], writes=[s_])
                k.op("pool", lambda e: e.tensor_tensor(out=o[:], in0=s_[:], in1=gB[:, m, :], op=ALU.mult), reads=[s_, gB], writes=[o])
                k.dma(C.YT[512 + m * 128:512 + (m + 1) * 128, :], o[:], reads=[o], writes=[C.u_YT[4 + m]])


def all_consts():
    c = {"c_ident": np.eye(128, dtype=np.float32)}
    c.update(hyena_consts())
    c.update(ssd_consts())
    return c


_PROG = {}


def kernel(**inputs):
    nseq = NSEQ
    if "prog" not in _PROG:
        _PROG["prog"] = build_program(nseq=nseq)
        _PROG["consts"] = all_consts()
    k, C = _PROG["prog"]
    consts = _PROG["consts"]
    x = np.ascontiguousarray(np.asarray(inputs["x"], dtype=np.float32))
    in_maps = []
    for c in range(NCORE):
        m = {"x": x[c * nseq:(c + 1) * nseq]}
        for n in C.W:
            m[n] = np.ascontiguousarray(np.asarray(inputs[n], dtype=np.float32))
        m.update(consts)
        in_maps.append(m)
    res = run_bass_kernel_spmd(k.nc, in_maps, core_ids=list(range(NCORE)))
    out = np.concatenate([np.asarray(r["out"], dtype=np.float32) for r in res.results], axis=0)
    return out


def stage_s5_batched(k, C, l, nseq):
    W = C.W
    BLK, GRP = 16, 16
    NB = T // BLK
    NG = NB // GRP
    GT = BLK * GRP
    NSQ = nseq
    CW = NSQ * 64
    with k.scope() as sc:
        BsT = sc.sb("s5BsT", [128, 64, 128], BF16)
        CfT = sc.sb("s5CfT", [128, 64, 128], BF16)
        AB = sc.sb("s5AB", [128, 2, 64], F32)
        k.dma(BsT[:], C.S5BS[l], reads=[C.u_S5[l]], writes=[BsT])
        k.dma(CfT[:], C.S5CF[l], reads=[C.u_S5[l]], writes=[CfT])
        k.dma(AB[:], C.S5AB[l], reads=[C.u_S5[l]], writes=[AB])
        if NSQ >= 4:
            chain_seqs = [list(range(NSQ - 1)), [NSQ - 1]]
        elif NSQ >= 2:
            chain_seqs = [list(range(NSQ - 1)), [NSQ - 1]]
        else:
            chain_seqs = [[0]]
        ceng = ["dve", "pool"]
        NCHN = len(chain_seqs)
        q2c = {}
        for h_, lst in enumerate(chain_seqs):
            for j_, q in enumerate(lst):
                q2c[q] = (h_, j_)
        QCs = [len(lst) for lst in chain_seqs]
        ABw = [sc.sb("s5ABw", [128, 2, QCs[h_] * 64], F32) for h_ in range(NCHN)]
        for h_ in range(NCHN):
            for ab in range(2):
                for q in range(QCs[h_]):
                    k.op("dve", lambda e: e.tensor_copy(out=ABw[h_][:, ab, q * 64:(q + 1) * 64], in_=AB[:, ab, :]),
                         reads=[AB], writes=[ABw[h_]])
        ug = sc.sb("s5ug", [128, NSQ, 2, 4, GT], BF16)
        yst = sc.sb("s5yst", [128, NSQ, 2, 4, GT], BF16)
        S = [[sc.sb("s5S", [128, BLK, QCs[h_] * 64], F32) for h_ in range(NCHN)] for _ in range(2)]
        Hf = [sc.sb("s5Hf", [128, BLK, QCs[h_] * 64], F32) for h_ in range(NCHN)]
        Hb = [sc.sb("s5Hb", [128, BLK, QCs[h_] * 64], BF16) for h_ in range(NCHN)]
        Hc = [sc.sb("s5Hc", [128, QCs[h_] * 64], F32) for h_ in range(NCHN)]
        m1 = [sc.sb("s5m1", [128, QCs[h_] * 64], F32) for h_ in range(NCHN)]
        m2 = [sc.sb("s5m2", [128, QCs[h_] * 64], F32) for h_ in range(NCHN)]
        for h_ in range(NCHN):
            k.op("dve", lambda e: e.memset(Hc[h_][:], 0.0), writes=[Hc[h_]])
        cnt = {"s": 0, "o": 0}

        def trange(G, dirn):
            return (G * GT) if dirn == 0 else (T - (G + 1) * GT)

        def load_group(G):
            for q in range(NSQ):
                for dirn in range(2):
                    t0 = trange(G, dirn)
                    k.dma(ug[:, q, dirn, :, :],
                          C.PT_all[q][C_S5:C_S5 + 512, t0:t0 + GT].rearrange("(f p) t -> p f t", p=128),
                          reads=[C.u_PT_all[q][C_S5 + F * 128] for F in range(4)], writes=[ug])

        def store_group(G):
            for q in range(NSQ):
                for dirn in range(2):
                    t0 = trange(G, dirn)
                    k.dma(C.YS5[q, dirn, :, t0:t0 + GT].rearrange("(f p) t -> p f t", p=128), yst[:, q, dirn, :, :],
                          reads=[yst], writes=[C.u_YS5[q]])

        def local_cols(j, dirn):
            return (j * BLK) if dirn == 0 else (GT - (j + 1) * BLK)

        def summ(b):
            j = b % GRP
            Sb = S[b % 2]
            for q in range(NSQ):
                for F in range(4):
                    for dirn in range(2):
                        c0 = local_cols(j, dirn)
                        ps = C.PS[cnt["s"] % 4]
                        cnt["s"] += 1
                        for reim in range(2):
                            idx = (F * 2 + dirn) * 2 + reim
                            for a in range(4):
                                reg = (reim * 4 + a) * BLK
                                k.op("pe", lambda e: e.matmul(ps[:, reg:reg + BLK], lhsT=BsT[:, idx * 4 + a, :],
                                                              rhs=ug[:, q, dirn, F, c0:c0 + BLK], start=True, stop=True),
                                     reads=[BsT, ug], writes=[ps], inc=(reim == 1 and a == 3))
                        for reim in range(2):
                            col0 = q2c[q][1] * 64 + (reim * 2 + dirn) * 16 + 4 * F
                            Sq = Sb[q2c[q][0]]
                            src = ps[:, reim * 4 * BLK:(reim * 4 + 4) * BLK].rearrange("p (a t) -> p a t", a=4)
                            if dirn == 1:
                                src = src[:, :, ::-1]
                            k.op("act", lambda e: e.activation(out=Sq[:, :, col0:col0 + 4].rearrange("p t a -> p a t"),
                                                               in_=src, func=AF.Copy), reads=[ps], writes=[Sq])

        def scan(b):
            Sb = S[b % 2]
            v4 = lambda ap: ap.rearrange("p (q r c) -> p q r c", r=2, c=32)
            hs = list(range(NCHN))
            for i in range(BLK):
                prevs = [(Hc[h_][:, :], Hc[h_]) if i == 0 else (Hf[h_][:, i - 1, :], Hf[h_]) for h_ in hs]
                for h_ in hs:
                    prev, pu = prevs[h_]
                    k.op(ceng[h_], lambda e: e.tensor_tensor(out=m1[h_][:], in0=prev, in1=ABw[h_][:, 0, :], op=ALU.mult),
                         reads=[pu, ABw[h_]], writes=[m1[h_]])
                for h_ in hs:
                    prev, pu = prevs[h_]
                    k.op(ceng[h_], lambda e: e.tensor_tensor(out=v4(m2[h_][:]), in0=v4(prev)[:, :, ::-1, :], in1=v4(ABw[h_][:, 1, :]),
                                                          op=ALU.mult), reads=[pu, ABw[h_]], writes=[m2[h_]])
                for h_ in hs:
                    k.op(ceng[h_], lambda e: e.tensor_tensor(out=m1[h_][:], in0=m1[h_][:], in1=m2[h_][:], op=ALU.add),
                         reads=[m1[h_], m2[h_]], writes=[m1[h_]])
                for h_ in hs:
                    k.op(ceng[h_], lambda e: e.tensor_tensor(out=Hf[h_][:, i, :], in0=m1[h_][:], in1=Sb[h_][:, i, :], op=ALU.add),
                         reads=[m1[h_], Sb[h_]], writes=[Hf[h_]])
            for h_ in hs:
                k.op(ceng[h_], lambda e: e.tensor_copy(out=Hc[h_][:], in_=Hf[h_][:, BLK - 1, :]), reads=[Hf[h_]], writes=[Hc[h_]])
                k.op("act", lambda e: e.activation(out=Hb[h_][:], in_=Hf[h_][:], func=AF.Copy), reads=[Hf[h_]], writes=[Hb[h_]])

        def outs(b):
            j = b % GRP
            for q in range(NSQ):
                for dirn in range(2):
                    c0 = local_cols(j, dirn)
                    for F in range(4):
                        ps = C.PS[4 + cnt["o"] % 4]
                        cnt["o"] += 1
                        nmm = 0
                        for reim in range(2):
                            for a in range(4):
                                idx = ((F * 2 + dirn) * 2 + reim) * 4 + a
                                col = q2c[q][1] * 64 + (reim * 2 + dirn) * 16 + 4 * F + a
                                hbq = Hb[q2c[q][0]]
                                k.op("pe", lambda e: e.matmul(ps[:, 0:BLK], lhsT=CfT[:, idx, :], rhs=hbq[:, :, col],
                                                              start=(nmm == 0), stop=(nmm == 7)),
                                     reads=[CfT, hbq], writes=[ps], inc=(nmm == 7))
                                nmm += 1
                        src = ps[:, 0:BLK]
                        if dirn == 1:
                            src = src[:, ::-1]
                        k.op("act", lambda e: e.activation(out=yst[:, q, dirn, F, c0:c0 + BLK], in_=src, func=AF.Copy),
                             reads=[ps], writes=[yst])

        for G in range(NG):
            load_group(G)
            for j in range(GRP):
                b = G * GRP + j
                if j == 0:
                    summ(b)
                if j + 1 < GRP:
                    summ(b + 1)
                scan(b)
                outs(b)
            store_group(G)
    for q in range(NSQ):
        C.set_seq(q)
        s5_epilogue(k, C, l, q)


def s5_epilogue(k, C, l, q):
    W = C.W
    with k.scope() as sc:
        dv = sc.sb("s5dv", [128, 2, 4], F32)
        load_featvecs(k, C, dv, [W["s5_d"][l], W["s5_b_glu"][l]], 4)
        u = sc.sb("s5u", [128, 4, T], BF16)
        yfb = [sc.sb("s5y", [128, 4, T], BF16) for _ in range(2)]
        for F in range(4):
            c0 = C_S5 + F * 128
            k.dma(u[:, F, :], C.PT[c0:c0 + 128, :], reads=[C.u_PT[c0]], writes=[u])
            for d_ in range(2):
                k.dma(yfb[d_][:, F, :], C.YS5[q, d_, F * 128:(F + 1) * 128, :], reads=[C.u_YS5[q]], writes=[yfb[d_]])
        gB = sc.sb("s5gB", [128, 4, T], BF16)
        ysum = [sc.sb("s5ys", [128, T], F32) for _ in range(2)]
        for F in range(4):
            ys = ysum[F % 2]
            k.op("pool", lambda e: e.tensor_tensor(out=ys[:], in0=yfb[0][:, F, :], in1=yfb[1][:, F, :], op=ALU.add),
                 reads=yfb, writes=[ys])
            k.op("dve", lambda e: e.scalar_tensor_tensor(out=ys[:], in0=u[:, F, :], scalar=dv[:, 0, F:F + 1], op0=ALU.mult,
                                                         in1=ys[:], op1=ALU.add), reads=[u, dv, ys], writes=[ys])
            k.op("act", lambda e: e.activation(out=gB[:, F, :], in_=ys[:], func=AF.Gelu), reads=[ys], writes=[gB])
        ws = WStream(k, sc, C, [("s5_w_glu", l, m * 128, 128, 4) for m in range(4)], 4, nbuf=2, name="wglu")
        sg = [sc.sb("s5sg", [128, T], F32) for _ in range(2)]
        yo = [sc.sb("s5yo", [128, T], BF16) for _ in range(2)]
        for m in range(4):
            w = ws.get(m)
            banks = C.PS[(m % 2) * 4:(m % 2) * 4 + 4]
            for ct in range(4):
                for kt in range(4):
                    k.op("pe", lambda e: e.matmul(banks[ct][:], lhsT=w[:, kt, :], rhs=gB[:, kt, ct * 512:(ct + 1) * 512],
                                                  start=(kt == 0), stop=(kt == 3)), reads=[w, gB], writes=[banks[ct]], inc=(kt == 3))
            s_ = sg[m % 2]
            o = yo[m % 2]
            for ct in range(4):
                k.op("act", lambda e: e.activation(out=s_[:, ct * 512:(ct + 1) * 512], in_=banks[ct][:], func=AF.Sigmoid,
                                                   bias=dv[:, 1, m:m + 1]), reads=[banks[ct], dv], writes=[s_])
            k.op("pool", lambda e: e.tensor_tensor(out=o[:], in0=s_[:], in1=gB[:, m, :], op=ALU.mult), reads=[s_, gB], writes=[o])
            k.dma(C.YT[512 + m * 128:512 + (m + 1) * 128, :], o[:], reads=[o], writes=[C.u_YT[4 + m]])
```
